# Optimizing a Trainium2 kernel written in Bass

```python
import math
import jax, jax.numpy as jnp
from jax import lax
import numpy as np

D_MODEL = 1024
BATCH = 8
SEQ = 4096
DEPTH = 1

HEAD_DIM = 64
D_MIX = D_MODEL
MOBA_HEADS = (D_MIX // 2) // HEAD_DIM
NSA_HEADS = (D_MIX // 2) // HEAD_DIM
NSA_KV_GROUPS = 2
MOBA_BLOCK = 256
MOBA_TOPK = 3
MOBA_QCHUNK = 32
NSA_CMP_LEN = 32
NSA_CMP_STRIDE = 16
NSA_CMP_HIDDEN = 256
NSA_SLC_BLOCK = 64
NSA_SLC_TOPN = 16
NSA_WINDOW = 512
NSA_QCHUNK = 64
NSA_N_GATES = 3
REL_BUCKETS = 32
REL_MAX_EXACT = REL_BUCKETS // 2
REL_MAX_DIST = 128
N_BIAS_HEADS = MOBA_HEADS + NSA_HEADS
D_FF = 2816
RMS_EPS = 1e-6
PROJ_SIZES = (MOBA_HEADS * HEAD_DIM,) * 3 + (NSA_HEADS * HEAD_DIM,) + (NSA_KV_GROUPS * HEAD_DIM,) * 6 + (NSA_HEADS * NSA_N_GATES,)
D_IN_PROJ = sum(PROJ_SIZES)
PROJ_SPLITS = tuple(int(s) for s in np.cumsum(PROJ_SIZES)[:-1])

kernel_name = 'hybrid_moba_nsa_macaron'


def rmsnorm(x, g):
    xf = x.astype(jnp.float32)
    y = xf * lax.rsqrt(jnp.mean(xf * xf, axis=-1, keepdims=True) + RMS_EPS)
    return (y * g.astype(jnp.float32)).astype(x.dtype)


def swiglu(h, w_gate, w_up, w_down):
    return (jax.nn.silu(h @ w_gate) * (h @ w_up)) @ w_down


def rel_bucket(dist):
    n = jnp.maximum(dist, 0)
    is_small = n < REL_MAX_EXACT
    nf = jnp.maximum(n, 1).astype(jnp.float32)
    large = REL_MAX_EXACT + (jnp.log(nf / REL_MAX_EXACT) / math.log(REL_MAX_DIST / REL_MAX_EXACT) * (REL_BUCKETS - REL_MAX_EXACT)).astype(jnp.int32)
    return jnp.where(is_small, n, jnp.minimum(large, REL_BUCKETS - 1))


def masked_softmax(logits, mask):
    lf = jnp.where(mask, logits.astype(jnp.float32), -jnp.inf)
    m = jnp.max(lf, axis=-1, keepdims=True)
    m = jnp.where(jnp.isfinite(m), m, 0.0)
    p = jnp.exp(lf - m)
    den = jnp.sum(p, axis=-1, keepdims=True)
    return p / jnp.where(den > 0, den, 1.0)


def slc_overlap(n_cmp, n_slc):
    r = NSA_SLC_BLOCK // NSA_CMP_STRIDE
    mc = NSA_CMP_LEN // NSA_CMP_STRIDE
    j = np.arange(n_slc)
    offs = (np.arange(r)[:, None] + np.arange(mc)[None, :]).reshape(-1)
    c = r * j[:, None] + offs[None, :]
    w = np.zeros((n_slc * r + mc, n_slc), np.float32)
    np.add.at(w, (c, np.broadcast_to(j[:, None], c.shape)), 1.0)
    return w[:n_cmp]


def moba_attention(q, k, v, tab):
    B, H, S, dh = q.shape
    nblk = -(-S // MOBA_BLOCK)
    s_pad = nblk * MOBA_BLOCK
    pad = ((0, 0), (0, 0), (0, s_pad - S), (0, 0))
    kp = jnp.pad(k, pad)
    vp = jnp.pad(v, pad)
    k_blocks = kp.reshape(B, H, nblk, MOBA_BLOCK, dh)
    v_blocks = vp.reshape(B, H, nblk, MOBA_BLOCK, dh)
    k_mean = jnp.mean(k_blocks.astype(jnp.float32), axis=3).astype(k.dtype)
    topk = min(MOBA_TOPK, nblk)
    n_sel = topk * MOBA_BLOCK
    C = MOBA_QCHUNK
    n_chunks = S // C
    q_chunks = q.reshape(B, H, n_chunks, C, dh).transpose(2, 0, 1, 3, 4)
    bi = jnp.arange(B)[:, None, None, None]
    hi = jnp.arange(H)[None, :, None, None]
    blk_ids = jnp.arange(nblk)
    offs = jnp.arange(MOBA_BLOCK)
    scale = dh ** -0.5

    def chunk_fn(args):
        qc, c = args
        t = c * C + jnp.arange(C)
        cur = (c * C) // MOBA_BLOCK
        gate = jnp.einsum('bhcd,bhnd->bhcn', qc, k_mean).astype(jnp.float32)
        gate = jnp.where(blk_ids < cur, gate, -jnp.inf)
        _, sel = lax.top_k(gate, topk)
        k_sel = k_blocks[bi, hi, sel].reshape(B, H, C, n_sel, dh)
        v_sel = v_blocks[bi, hi, sel].reshape(B, H, C, n_sel, dh)
        pos_sel = (sel[..., None] * MOBA_BLOCK + offs).reshape(B, H, C, n_sel)
        mask_sel = jnp.repeat(sel < cur, MOBA_BLOCK, axis=-1)
        bias_sel = tab[hi, rel_bucket(t[:, None] - pos_sel)]
        k_own = lax.dynamic_slice_in_dim(kp, cur * MOBA_BLOCK, MOBA_BLOCK, axis=2)
        v_own = lax.dynamic_slice_in_dim(vp, cur * MOBA_BLOCK, MOBA_BLOCK, axis=2)
        d_own = t[:, None] - (cur * MOBA_BLOCK + offs)[None, :]
        mask_own = jnp.broadcast_to(d_own >= 0, (B, H, C, MOBA_BLOCK))
        bias_own = tab[:, rel_bucket(d_own)]
        lg_sel = jnp.einsum('bhcd,bhckd->bhck', qc, k_sel).astype(jnp.float32) * scale + bias_sel
        lg_own = jnp.einsum('bhcd,bhkd->bhck', qc, k_own).astype(jnp.float32) * scale + bias_own
        p = masked_softmax(jnp.concatenate([lg_sel, lg_own], axis=-1), jnp.concatenate([mask_sel, mask_own], axis=-1)).astype(v.dtype)
        return jnp.einsum('bhck,bhckd->bhcd', p[..., :n_sel], v_sel) + jnp.einsum('bhck,bhkd->bhcd', p[..., n_sel:], v_own)

    out = lax.map(chunk_fn, (q_chunks, jnp.arange(n_chunks)))
    return out.transpose(1, 2, 0, 3, 4).reshape(B, H, S, dh)


def nsa_attention(q, kc_raw, vc_raw, ks, vs, kw, vw, gates, tab, pos_k, w1_k, w2_k, pos_v, w1_v, w2_v):
    B, H, S, dh = q.shape
    G = kc_raw.shape[1]
    R = H // G
    scale = dh ** -0.5
    n_cmp = (S - NSA_CMP_LEN) // NSA_CMP_STRIDE + 1
    idx = (np.arange(n_cmp)[:, None] * NSA_CMP_STRIDE + np.arange(NSA_CMP_LEN)[None, :]).astype(np.int32)

    def compress(raw, pos, w1, w2):
        blocks = raw[:, :, idx] + pos
        return jax.nn.silu(blocks.reshape(B, G, n_cmp, NSA_CMP_LEN * dh) @ w1) @ w2

    k_cmp = compress(kc_raw, pos_k, w1_k, w2_k)
    v_cmp = compress(vc_raw, pos_v, w1_v, w2_v)
    cmp_end = jnp.asarray(np.arange(n_cmp, dtype=np.int32) * NSA_CMP_STRIDE + NSA_CMP_LEN - 1)
    n_slc = S // NSA_SLC_BLOCK
    n_top = min(NSA_SLC_TOPN, n_slc)
    n_sel = n_top * NSA_SLC_BLOCK
    overlap = jnp.asarray(slc_overlap(n_cmp, n_slc))
    ks_blocks = ks.reshape(B, G, n_slc, NSA_SLC_BLOCK, dh)
    vs_blocks = vs.reshape(B, G, n_slc, NSA_SLC_BLOCK, dh)
    wpad = ((0, 0), (0, 0), (NSA_WINDOW, 0), (0, 0))
    kw_pad = jnp.pad(kw, wpad)
    vw_pad = jnp.pad(vw, wpad)
    tab_g = tab.reshape(G, R, REL_BUCKETS)
    C = NSA_QCHUNK
    n_chunks = S // C
    q_chunks = q.reshape(B, G, R, n_chunks, C, dh).transpose(3, 0, 1, 2, 4, 5)
    g_chunks = gates.reshape(B, G, R, n_chunks, C, NSA_N_GATES).transpose(3, 0, 1, 2, 4, 5)
    bi = jnp.arange(B)[:, None, None, None]
    gi = jnp.arange(G)[None, :, None, None]
    gi5 = jnp.arange(G)[:, None, None, None]
    ri5 = jnp.arange(R)[:, None, None]
    slc_ids = jnp.arange(n_slc)
    offs = jnp.arange(NSA_SLC_BLOCK)
    win_offs = jnp.arange(NSA_WINDOW + C)

    def chunk_fn(args):
        qc, gc, c = args
        t = c * C + jnp.arange(C)
        d_cmp = t[:, None] - cmp_end[None, :]
        lg = jnp.einsum('bgrcd,bgnd->bgrcn', qc, k_cmp).astype(jnp.float32) * scale + tab_g[:, :, rel_bucket(d_cmp)]
        p_cmp = masked_softmax(lg, d_cmp >= 0)
        o_cmp = jnp.einsum('bgrcn,bgnd->bgrcd', p_cmp.astype(v_cmp.dtype), v_cmp)
        imp = jnp.einsum('bgrcn,nj->bgcj', p_cmp, overlap)
        cur = t // NSA_SLC_BLOCK
        forced = (slc_ids[None, :] == 0) | (slc_ids[None, :] == cur[:, None]) | (slc_ids[None, :] == cur[:, None] - 1)
        allowed = slc_ids[None, :] <= cur[:, None]
        score = jnp.where(forced, jnp.inf, jnp.where(allowed, imp, -jnp.inf))
        _, sel = lax.top_k(score, n_top)
        k_sel = ks_blocks[bi, gi, sel].reshape(B, G, C, n_sel, dh)
        v_sel = vs_blocks[bi, gi, sel].reshape(B, G, C, n_sel, dh)
        pos_sel = (sel[..., None] * NSA_SLC_BLOCK + offs).reshape(B, G, C, n_sel)
        d_sel = t[:, None] - pos_sel
        lg = jnp.einsum('bgrcd,bgckd->bgrck', qc, k_sel).astype(jnp.float32) * scale + tab_g[gi5, ri5, rel_bucket(d_sel)[:, :, None]]
        p_slc = masked_softmax(lg, (d_sel >= 0)[:, :, None]).astype(v_sel.dtype)
        o_slc = jnp.einsum('bgrck,bgckd->bgrcd', p_slc, v_sel)
        k_win = lax.dynamic_slice_in_dim(kw_pad, c * C, NSA_WINDOW + C, axis=2)
        v_win = lax.dynamic_slice_in_dim(vw_pad, c * C, NSA_WINDOW + C, axis=2)
        pos_win = c * C - NSA_WINDOW + win_offs
        d_win = t[:, None] - pos_win[None, :]
        mask_win = (pos_win[None, :] >= 0) & (d_win >= 0) & (d_win < NSA_WINDOW)
        lg = jnp.einsum('bgrcd,bgkd->bgrck', qc, k_win).astype(jnp.float32) * scale + tab_g[:, :, rel_bucket(d_win)]
        p_win = masked_softmax(lg, mask_win).astype(v_win.dtype)
        o_win = jnp.einsum('bgrck,bgkd->bgrcd', p_win, v_win)
        return gc[..., 0:1] * o_cmp + gc[..., 1:2] * o_slc + gc[..., 2:3] * o_win

    out = lax.map(chunk_fn, (q_chunks, g_chunks, jnp.arange(n_chunks)))
    return out.transpose(1, 2, 3, 0, 4, 5).reshape(B, H, S, dh)


def hybrid_mixer(h, w_in, pos_k, w1_k, w2_k, pos_v, w1_v, w2_v, w_out, rel_bias):
    B, S, _ = h.shape
    proj = h @ w_in
    qa, ka, va, qb, kcb, vcb, ksb, vsb, kwb, vwb, gb = jnp.split(proj, PROJ_SPLITS, axis=-1)

    def heads(t, n):
        return t.reshape(B, S, n, HEAD_DIM).transpose(0, 2, 1, 3)

    out_a = moba_attention(heads(qa, MOBA_HEADS), heads(ka, MOBA_HEADS), heads(va, MOBA_HEADS), rel_bias[:, :MOBA_HEADS].T)
    gates = jax.nn.sigmoid(gb.astype(jnp.float32)).astype(h.dtype).reshape(B, S, NSA_HEADS, NSA_N_GATES).transpose(0, 2, 1, 3)
    out_b = nsa_attention(heads(qb, NSA_HEADS), heads(kcb, NSA_KV_GROUPS), heads(vcb, NSA_KV_GROUPS), heads(ksb, NSA_KV_GROUPS), heads(vsb, NSA_KV_GROUPS), heads(kwb, NSA_KV_GROUPS), heads(vwb, NSA_KV_GROUPS), gates, rel_bias[:, MOBA_HEADS:].T, pos_k, w1_k, w2_k, pos_v, w1_v, w2_v)
    o = jnp.concatenate([out_a.transpose(0, 2, 1, 3).reshape(B, S, MOBA_HEADS * HEAD_DIM), out_b.transpose(0, 2, 1, 3).reshape(B, S, NSA_HEADS * HEAD_DIM)], axis=-1)
    return o @ w_out


def setup_inputs(seed: int = 0) -> dict:
    key = jax.random.key(seed)
    ks = jax.random.split(key, 20)

    def nrm(k, shape, s):
        return jax.random.normal(k, shape, jnp.float32) * s

    def gain(k, shape):
        return 1.0 + 0.05 * jax.random.normal(k, shape, jnp.float32)

    L = DEPTH
    flat_cmp = NSA_CMP_LEN * HEAD_DIM
    return {
        'x': nrm(ks[0], (BATCH, SEQ, D_MODEL), 1.0),
        'norm_ffn1': gain(ks[1], (L, D_MODEL)),
        'w_ffn1_gate': nrm(ks[2], (L, D_MODEL, D_FF), D_MODEL ** -0.5),
        'w_ffn1_up': nrm(ks[3], (L, D_MODEL, D_FF), D_MODEL ** -0.5),
        'w_ffn1_down': nrm(ks[4], (L, D_FF, D_MODEL), D_FF ** -0.5),
        'norm_mix': gain(ks[5], (L, D_MODEL)),
        'w_in': nrm(ks[6], (L, D_MODEL, D_IN_PROJ), D_MODEL ** -0.5),
        'cmp_pos_k': nrm(ks[7], (L, NSA_CMP_LEN, HEAD_DIM), 0.5),
        'cmp_w1_k': nrm(ks[8], (L, flat_cmp, NSA_CMP_HIDDEN), flat_cmp ** -0.5),
        'cmp_w2_k': nrm(ks[9], (L, NSA_CMP_HIDDEN, HEAD_DIM), NSA_CMP_HIDDEN ** -0.5),
        'cmp_pos_v': nrm(ks[10], (L, NSA_CMP_LEN, HEAD_DIM), 0.5),
        'cmp_w1_v': nrm(ks[11], (L, flat_cmp, NSA_CMP_HIDDEN), flat_cmp ** -0.5),
        'cmp_w2_v': nrm(ks[12], (L, NSA_CMP_HIDDEN, HEAD_DIM), NSA_CMP_HIDDEN ** -0.5),
        'w_out': nrm(ks[13], (L, D_MIX, D_MODEL), D_MIX ** -0.5),
        'norm_ffn2': gain(ks[14], (L, D_MODEL)),
        'w_ffn2_gate': nrm(ks[15], (L, D_MODEL, D_FF), D_MODEL ** -0.5),
        'w_ffn2_up': nrm(ks[16], (L, D_MODEL, D_FF), D_MODEL ** -0.5),
        'w_ffn2_down': nrm(ks[17], (L, D_FF, D_MODEL), D_FF ** -0.5),
        'rel_bias': nrm(ks[18], (REL_BUCKETS, N_BIAS_HEADS), 0.5),
        'norm_final': gain(ks[19], (D_MODEL,)),
    }


def reference(x, norm_ffn1, w_ffn1_gate, w_ffn1_up, w_ffn1_down, norm_mix, w_in, cmp_pos_k, cmp_w1_k, cmp_w2_k, cmp_pos_v, cmp_w1_v, cmp_w2_v, w_out, norm_ffn2, w_ffn2_gate, w_ffn2_up, w_ffn2_down, rel_bias, norm_final):
    for l in range(DEPTH):
        x = x + 0.5 * swiglu(rmsnorm(x, norm_ffn1[l]), w_ffn1_gate[l], w_ffn1_up[l], w_ffn1_down[l])
        x = x + hybrid_mixer(rmsnorm(x, norm_mix[l]), w_in[l], cmp_pos_k[l], cmp_w1_k[l], cmp_w2_k[l], cmp_pos_v[l], cmp_w1_v[l], cmp_w2_v[l], w_out[l], rel_bias)
        x = x + 0.5 * swiglu(rmsnorm(x, norm_ffn2[l]), w_ffn2_gate[l], w_ffn2_up[l], w_ffn2_down[l])
    return rmsnorm(x, norm_final)
```

```python
import math
from contextlib import ExitStack

import numpy as np
import concourse.bass as bass
import concourse.mybir as mybir
from concourse.bass_utils import run_bass_kernel_spmd

F32 = mybir.dt.float32
BF16 = mybir.dt.bfloat16
ALU = mybir.AluOpType
AF = mybir.ActivationFunctionType
AX = mybir.AxisListType

S = 4096
D = 1024
DFF = 2816
NFC = DFF // 128
DIN = 2840
HD = 64
EPS = 1e-6
BIG = 32768.0
ROFF = 4096
RLEN = 8704


class Res:
    __slots__ = ("name", "lw", "rd")

    def __init__(self, name):
        self.name = name
        self.lw = None
        self.rd = {}


def _flat(x):
    out = []
    for r in x:
        if r is None:
            continue
        if isinstance(r, (list, tuple)):
            out.extend(_flat(r))
        else:
            out.append(r)
    return out


def Res2(*a):
    return list(a)


class Sched:
    ENGS = ("pe", "act", "dve", "pool", "sp")

    def __init__(self, nc, tag, ndma=6):
        self.nc = nc
        self.prog = {e: [] for e in self.ENGS}
        self.cnt = {e: 0 for e in self.ENGS}
        self.sem = {e: nc.alloc_semaphore(name=f"{tag}_{e}") for e in self.ENGS}
        self.dq = {}
        for q, n in (("sp", 4), ("pool", 3)):
            self.dq[q] = [[nc.alloc_semaphore(name=f"{tag}_d{q}{i}"), 0] for i in range(n)]
        self.dqi = {q: 0 for q in self.dq}
        self.waited = {}
        self.all_dma = {}

    def _key(self, tok):
        return (tok[0], tok[1] if tok[0] == "eng" else id(tok[1]))

    def _wait(self, eng, tok):
        if tok is None:
            return
        if tok[0] == "eng":
            if eng == "pe" and tok[1] == "pe":
                return
            k = ("e", tok[1])
            sem = self.sem[tok[1]]
        else:
            k = ("d", id(tok[1]))
            sem = tok[1]
        kk = (eng, k)
        if self.waited.get(kk, 0) >= tok[2]:
            return
        self.waited[kk] = tok[2]
        self.prog[eng].append(("w", sem, tok[2]))

    def _deps(self, eng, reads, writes):
        for r in reads:
            self._wait(eng, r.lw)
        for r in writes:
            self._wait(eng, r.lw)
            for t in r.rd.values():
                self._wait(eng, t)

    def _commit(self, tok, reads, writes):
        kk = self._key(tok)
        for r in reads:
            r.rd[kk] = tok
        for r in writes:
            r.lw = tok
            r.rd = {}

    def op(self, eng, fn, reads=(), writes=()):
        reads, writes = _flat(reads), _flat(writes)
        self._deps(eng, reads, writes)
        self.cnt[eng] += 1
        tok = ("eng", eng, self.cnt[eng])
        self.prog[eng].append(("op", fn))
        self._commit(tok, reads, writes)

    def dma(self, q, out, in_, reads=(), writes=(), **kw):
        reads, writes = _flat(reads), _flat(writes)
        self._deps(q, reads, writes)
        pool = self.dq[q]
        ent = pool[self.dqi[q] % len(pool)]
        self.dqi[q] += 1
        if ent[1] > 0:
            self._wait(q, ("dma", ent[0], ent[1]))
        ent[1] += 16
        tok = ("dma", ent[0], ent[1])
        self.all_dma[id(ent[0])] = tok
        self.prog[q].append(("dma", out, in_, kw, ent[0]))
        self._commit(tok, reads, writes)

    def emit(self):
        nc = self.nc
        for e in self.ENGS:
            for e2 in self.ENGS:
                if e2 != e and self.cnt[e2] > 0:
                    self._wait(e, ("eng", e2, self.cnt[e2]))
            for tok in self.all_dma.values():
                self._wait(e, tok)
        prog, sem = self.prog, self.sem

        def run(ename, eng):
            s = sem[ename]
            for it in prog[ename]:
                if it[0] == "w":
                    eng.wait_ge(it[1], it[2])
                elif it[0] == "op":
                    it[1](eng).then_inc(s, 1)
                else:
                    eng.dma_start(out=it[1], in_=it[2], **it[3]).then_inc(it[4], 16)

        with nc.Block() as blk:
            @blk.tensor
            def _(e):
                run("pe", e)

            @blk.scalar
            def _(e):
                run("act", e)

            @blk.vector
            def _(e):
                run("dve", e)

            @blk.gpsimd
            def _(e):
                run("pool", e)

            @blk.sync
            def _(e):
                run("sp", e)


def dram_view(t, offset, dims):
    return bass.AP(t, offset, [list(d) for d in dims])


def emit_rmsnorm(sc, N, xt, r_x, gain, r_gain, out_t, r_out, ones32, r_const, sq, r_sq, ps, r_ps, rstd, r_rstd, sqa=None, r_sqa=None):
    if sqa is None:
        for c in range(8):
            j = c % 2
            sc.op("act", lambda e, c=c, j=j: e.activation(out=sq[j][:, 0:N], in_=xt[:, c, 0:N], func=AF.Square),
                  reads=[r_x], writes=[r_sq[j]])
            sc.op("pe", lambda e, c=c, j=j: e.matmul(ps[:, 0:N], ones32[:, :], sq[j][:, 0:N], start=(c == 0), stop=(c == 7)),
                  reads=[r_sq[j], r_const], writes=[r_ps])
    else:
        sc.op("act", lambda e: e.activation(out=sqa[:, 0:N], in_=xt[:, 0, 0:N], func=AF.Square), reads=[r_x], writes=[r_sqa])
        for c in range(1, 8):
            j = c % 2
            sc.op("act", lambda e, c=c, j=j: e.activation(out=sq[j][:, 0:N], in_=xt[:, c, 0:N], func=AF.Square),
                  reads=[r_x], writes=[r_sq[j]])
            sc.op("pool", lambda e, j=j: e.tensor_tensor(out=sqa[:, 0:N], in0=sqa[:, 0:N], in1=sq[j][:, 0:N], op=ALU.add),
                  reads=[r_sq[j], r_sqa], writes=[r_sqa])
        sc.op("pe", lambda e: e.matmul(ps[:, 0:N], ones32[:, :], sqa[:, 0:N], start=True, stop=True),
              reads=[r_sqa, r_const], writes=[r_ps])
    sc.op("dve", lambda e: e.tensor_scalar(out=rstd[:, 0:N], in0=ps[:, 0:N], scalar1=1.0 / D, scalar2=EPS,
                                           op0=ALU.mult, op1=ALU.add), reads=[r_ps], writes=[r_rstd])
    sc.op("act", lambda e: e.activation(out=rstd[:, 0:N], in_=rstd[:, 0:N], func=AF.Sqrt), reads=[r_rstd], writes=[r_rstd])
    sc.op("dve", lambda e: e.reciprocal(out=rstd[:, 0:N], in_=rstd[:, 0:N]), reads=[r_rstd], writes=[r_rstd])
    for c in range(8):
        sc.op("dve", lambda e, c=c: e.scalar_tensor_tensor(out=out_t[:, c, 0:N], in0=xt[:, c, 0:N], scalar=gain[:, c:c + 1],
                                                          in1=rstd[:, 0:N], op0=ALU.mult, op1=ALU.mult),
              reads=[r_x, r_rstd, r_gain], writes=[r_out])


def ffn_phase(nc, tag, x_in, x_out, g_pre, wg, wu, wd, g_post, post_out, post_dt, ntok=S, TN=256):
    sc = Sched(nc, tag)
    r_xin, r_xout, r_post = Res("xin"), Res("xout"), Res("post")
    with ExitStack() as es:
        def sb(name, shape, dt):
            return es.enter_context(nc.sbuf_tensor(f"{tag}_{name}", shape, dt))

        def pst(name):
            return es.enter_context(nc.psum_tensor(f"{tag}_{name}", [128, 512], F32))

        wg_sb = sb("wg", [128, 8, DFF], BF16)
        wu_sb = sb("wu", [128, 8, DFF], BF16)
        wd_sb = sb("wd", [128, NFC, D], BF16)
        gpre = sb("gpre", [128, 8], F32)
        gpost = sb("gpost", [128, 8], F32)
        ones32 = sb("ones", [128, 128], F32)
        xt = [sb(f"xt{i}", [128, 8, TN], F32) for i in range(2)]
        ht = [sb(f"ht{i}", [128, 8, TN], BF16) for i in range(2)]
        at = sb("at", [128, NFC, TN], BF16)
        pt = sb("pt", [128, 8, TN], post_dt)
        sq = [sb(f"sq{i}", [128, TN], F32) for i in range(2)]
        sg = [sb(f"sg{i}", [128, TN], F32) for i in range(2)]
        rstd = sb("rstd", [128, TN], F32)
        sqa = sb("sqa", [128, TN], F32)
        r_sqa = Res("sqa")
        ps_g = [pst(f"psg{i}") for i in range(2)]
        ps_u = [pst(f"psu{i}") for i in range(2)]
        ps_y = [pst(f"psy{i}") for i in range(2)]
        ps_n = pst("psn")

        r_const = Res("const")
        r_gain = Res("gain")
        r_xt = [Res("xt0"), Res("xt1")]
        r_ht = [Res("ht0"), Res("ht1")]
        r_at, r_pt, r_rstd = Res("at"), Res("pt"), Res("rstd")
        r_sq = [Res("sq0"), Res("sq1")]
        r_sg = [Res("sg0"), Res("sg1")]
        r_psg = [Res("psg0"), Res("psg1")]
        r_psu = [Res("psu0"), Res("psu1")]
        r_psy = [Res("psy0"), Res("psy1")]
        r_psn = Res("psn")

        sc.op("dve", lambda e: e.memset(ones32[:, :], 1.0), writes=[r_const])
        sc.dma("sp", gpre[:, :], g_pre.rearrange("(c p) -> p c", p=128), writes=[r_gain], allow_slow_non_contiguous=True)
        sc.dma("sp", gpost[:, :], g_post.rearrange("(c p) -> p c", p=128), writes=[r_gain], allow_slow_non_contiguous=True)
        r_wk1 = [Res(f"wk{k}") for k in range(8)]
        r_wk = [[r_wk1[k]] * NFC for k in range(8)]
        r_wd = [Res(f"wd{f}") for f in range(NFC)]
        for k in range(8):
            sc.dma("pool", wg_sb[:, k, :], wg[k * 128:(k + 1) * 128, :], writes=[r_wk1[k]])
            sc.dma("pool", wu_sb[:, k, :], wu[k * 128:(k + 1) * 128, :], writes=[r_wk1[k]])
        for f in range(NFC):
            sc.dma("pool", wd_sb[:, f, :], wd[f * 128:(f + 1) * 128, :], writes=[r_wd[f]])

        xin_v = x_in.rearrange("(c p) t -> p c t", p=128)
        xout_v = x_out.rearrange("(c p) t -> p c t", p=128) if x_out is not None else None
        post_v = post_out.rearrange("(c p) t -> p c t", p=128)
        ntile = ntok // TN

        def load(t):
            b = t % 2
            sc.dma("sp", xt[b][:, :, :], xin_v[:, :, t * TN:(t + 1) * TN], reads=[r_xin], writes=[r_xt[b]])

        def prenorm(t):
            b = t % 2
            emit_rmsnorm(sc, TN, xt[b], r_xt[b], gpre, r_gain, ht[b], r_ht[b], ones32, r_const, sq, r_sq, ps_n, r_psn, rstd, r_rstd, sqa, r_sqa)

        def gateup(t, f0, f1):
            b = t % 2
            H, rH = ht[b], r_ht[b]
            for f in range(f0, f1):
                j = f % 2
                for k in range(8):
                    sc.op("pe", lambda e, f=f, k=k, j=j, H=H: e.matmul(ps_g[j][:, 0:TN], wg_sb[:, k, f * 128:(f + 1) * 128], H[:, k, :],
                                                                      start=(k == 0), stop=(k == 7)),
                          reads=[r_wk[k][f], rH], writes=[r_psg[j]])
                for k in range(8):
                    sc.op("pe", lambda e, f=f, k=k, j=j, H=H: e.matmul(ps_u[j][:, 0:TN], wu_sb[:, k, f * 128:(f + 1) * 128], H[:, k, :],
                                                                      start=(k == 0), stop=(k == 7)),
                          reads=[r_wk[k][f], rH], writes=[r_psu[j]])
                sc.op("act", lambda e, j=j: e.activation(out=sg[j][:, :], in_=ps_g[j][:, 0:TN], func=AF.Silu),
                      reads=[r_psg[j]], writes=[r_sg[j]])
                sc.op("dve", lambda e, f=f, j=j: e.tensor_tensor(out=at[:, f, :], in0=sg[j][:, :], in1=ps_u[j][:, 0:TN], op=ALU.mult),
                      reads=[r_sg[j], r_psu[j]], writes=[r_at])

        def down(t):
            b = t % 2
            X, rX = xt[b], r_xt[b]
            for d in range(8):
                j = d % 2
                for f in range(NFC):
                    sc.op("pe", lambda e, f=f, d=d, j=j: e.matmul(ps_y[j][:, 0:TN], wd_sb[:, f, d * 128:(d + 1) * 128], at[:, f, :],
                                                                 start=(f == 0), stop=(f == NFC - 1)),
                          reads=[r_wd[f], r_at], writes=[r_psy[j]])
                sc.op("dve", lambda e, d=d, j=j, X=X: e.scalar_tensor_tensor(out=X[:, d, :], in0=ps_y[j][:, 0:TN], scalar=0.5,
                                                                          in1=X[:, d, :], op0=ALU.mult, op1=ALU.add),
                      reads=[r_psy[j], rX], writes=[rX])
            if xout_v is not None:
                sc.dma("sp", xout_v[:, :, t * TN:(t + 1) * TN], X[:, :, :], reads=[rX], writes=[r_xout])

        def postnorm(t):
            b = t % 2
            emit_rmsnorm(sc, TN, xt[b], r_xt[b], gpost, r_gain, pt, r_pt, ones32, r_const, sq, r_sq, ps_n, r_psn, rstd, r_rstd, sqa, r_sqa)
            sc.dma("sp", post_v[:, :, t * TN:(t + 1) * TN], pt[:, :, :], reads=[r_pt], writes=[r_post])

        load(0)
        prenorm(0)
        for t in range(ntile + 1):
            if t < ntile:
                gateup(t, 0, 4)
            if t > 0:
                postnorm(t - 1)
            if t + 1 < ntile:
                load(t + 1)
            if t < ntile:
                gateup(t, 4, NFC)
            if t + 1 < ntile:
                prenorm(t + 1)
            if t < ntile:
                down(t)
        sc.emit()


def _bucket(d):
    n = np.maximum(d, 0)
    nf = np.maximum(n, 1).astype(np.float32)
    large = 16 + (np.log(nf / np.float32(16)) / np.float32(math.log(128 / 16)) * np.float32(16)).astype(np.int32)
    return np.where(n < 16, n, np.minimum(large, 31)).astype(np.int64)


def _slc_overlap(n_cmp, n_slc):
    r, mc = 4, 2
    j = np.arange(n_slc)
    offs = (np.arange(r)[:, None] + np.arange(mc)[None, :]).reshape(-1)
    c = r * j[:, None] + offs[None, :]
    w = np.zeros((n_slc * r + mc, n_slc), np.float32)
    np.add.at(w, (c, np.broadcast_to(j[:, None], c.shape)), 1.0)
    return w[:n_cmp]


_CONSTS = None


def static_consts():
    global _CONSTS
    if _CONSTS is not None:
        return _CONSTS
    c = {}
    i = np.arange(RLEN)
    d = i - ROFF
    b = _bucket(d)
    okc = d >= 0
    okw = (d >= 0) & (d < 512)
    ohc = np.zeros((32, RLEN), np.float32)
    ohc[b[okc], i[okc]] = 1.0
    ohw = np.zeros((32, RLEN), np.float32)
    ohw[b[okw], i[okw]] = 1.0
    c["c_ohc"], c["c_ohw"] = ohc, ohw
    c["c_negc"] = np.broadcast_to(np.where(okc, 0.0, -BIG).astype(np.float32), (16, RLEN)).copy()
    c["c_negw"] = np.broadcast_to(np.where(okw, 0.0, -BIG).astype(np.float32), (16, RLEN)).copy()
    c["c_ident"] = np.eye(128, dtype=np.float32)
    c["c_jx"] = np.eye(128, dtype=np.float32)[::-1].copy()
    t = np.arange(S)
    cur = (t // 256)[:, None]
    n = np.arange(16)[None, :]
    def lay(a):
        return np.ascontiguousarray(a.reshape(32, 128, a.shape[-1]).transpose(1, 0, 2)).astype(np.float32)
    c["c_mneg"] = lay(np.where(n >= cur, -1e30, 0.0))
    c["c_mallow"] = lay((n < cur) * 1.0)
    c["c_mown"] = lay((n == cur) * 1.0)
    cur = (t // 64)[:, None]
    j = np.arange(64)[None, :]
    forced = (j == 0) | (j == cur) | (j == cur - 1)
    c["c_slc"] = lay(np.where(forced, 1e4, np.where(j <= cur, 0.0, -1e4)))
    k = np.arange(S)[None, :]
    c["c_kaugm"] = ((k // 256) == np.arange(16)[:, None]).astype(np.float32)
    c["c_kaugs"] = ((k // 64) == np.arange(64)[:, None]).astype(np.float32)
    ov = np.zeros((256, 65), np.float32)
    ov[:255, :64] = _slc_overlap(255, 64)
    ov[:255, 64] = 1.0
    c["c_ovl"] = ov
    _CONSTS = c
    return c


def const_phase(nc, tag, rel_bias, cst, Rc, Rw):
    sc = Sched(nc, tag)
    with ExitStack() as es:
        def sb(name, shape, dt):
            return es.enter_context(nc.sbuf_tensor(f"{tag}_{name}", shape, dt))
        tab = sb("tab", [32, 16], F32)
        oh = sb("oh", [32, RLEN], F32)
        ng = sb("ng", [16, RLEN], F32)
        rs = sb("rs", [16, RLEN], BF16)
        ps = [es.enter_context(nc.psum_tensor(f"{tag}_ps{i}", [128, 512], F32)) for i in range(2)]
        r_tab, r_oh, r_ng, r_rs, r_R = Res("tab"), Res("oh"), Res("ng"), Res("rs"), Res("R")
        r_ps = [Res("ps0"), Res("ps1")]
        sc.dma("sp", tab[:, :], rel_bias[:, :], writes=[r_tab])
        for ohd, ngd, Rd in ((cst["c_ohc"], cst["c_negc"], Rc), (cst["c_ohw"], cst["c_negw"], Rw)):
            sc.dma("sp", oh[:, :], ohd[:, :], writes=[r_oh])
            sc.dma("sp", ng[:, :], ngd[:, :], writes=[r_ng])
            for ch in range(RLEN // 512):
                j = ch % 2
                sl = slice(ch * 512, (ch + 1) * 512)
                sc.op("pe", lambda e, j=j, sl=sl: e.matmul(ps[j][0:16, :], tab[:, :], oh[:, sl], start=True, stop=True),
                      reads=[r_tab, r_oh], writes=[r_ps[j]])
                sc.op("dve", lambda e, j=j, sl=sl: e.scalar_tensor_tensor(out=rs[:, sl], in0=ps[j][0:16, :], scalar=8.0, in1=ng[:, sl],
                                                                          op0=ALU.mult, op1=ALU.add),
                      reads=[r_ps[j], r_ng], writes=[r_rs])
            sc.dma("sp", Rd[:, :], rs[:, :], reads=[r_rs], writes=[r_R])
        sc.emit()


FM_CHUNKS = [0, 1, 2, 3, 4, 5, 6, 7, 12, 13, 14, 15, 16, 17, 18, 20]
VT_GROUPS = [(1024, 512, 0), (2432, 128, 512), (2688, 128, 640)]


def inproj_phase(nc, tag, h2T, w_in, projT, Vtok, gT, ntok=S):
    sc = Sched(nc, tag)
    TN = 512
    with ExitStack() as es:
        def sb(name, shape, dt):
            return es.enter_context(nc.sbuf_tensor(f"{tag}_{name}", shape, dt))

        def pst(name):
            return es.enter_context(nc.psum_tensor(f"{tag}_{name}", [128, 512], F32))
        W = sb("w", [128, 8, DIN], BF16)
        ht = [sb(f"ht{i}", [128, 8, TN], BF16) for i in range(2)]
        stage = sb("stage", [128, 16, TN], BF16)
        gst = sb("gst", [24, TN], F32)
        vst = [sb(f"vst{i}", [128, 768], BF16) for i in range(2)]
        psA = [pst(f"psa{i}") for i in range(2)]
        psG = pst("psg")
        psV = [pst(f"psv{i}") for i in range(3)]
        r_w = [Res(f"w{k}") for k in range(8)]
        r_ht = [Res("ht0"), Res("ht1")]
        r_stage, r_gst = Res("stage"), Res("gst")
        r_vst = [Res("vst0"), Res("vst1")]
        r_psA = [Res("psa0"), Res("psa1")]
        r_psG = Res("psg")
        r_psV = [Res(f"psv{i}") for i in range(3)]
        r_in, r_proj, r_vt, r_g = Res("h2T"), Res("projT"), Res("Vtok"), Res("gT")
        for k in range(8):
            sc.dma("pool", W[:, k, :], w_in[k * 128:(k + 1) * 128, :], writes=[r_w[k]])
        hv = h2T.rearrange("(c p) t -> p c t", p=128)
        pv = projT.rearrange("(c p) t -> p c t", p=128)
        cp = 0
        for it in range(ntok // TN):
            t0 = it * TN
            H, rH = ht[it % 2], r_ht[it % 2]
            sc.dma("sp", H[:, :, :], hv[:, :, t0:t0 + TN], reads=[r_in], writes=[rH])
            for ci, c in enumerate(FM_CHUNKS):
                j = ci % 2
                for k in range(8):
                    sc.op("pe", lambda e, c=c, k=k, j=j, H=H: e.matmul(psA[j][:, :], W[:, k, c * 128:(c + 1) * 128], H[:, k, :],
                                                                      start=(k == 0), stop=(k == 7)),
                          reads=[r_w[k], rH], writes=[r_psA[j]])
                if ci % 2 == 0:
                    sc.op("act", lambda e, ci=ci, j=j: e.copy(out=stage[:, ci, :], in_=psA[j][:, :]), reads=[r_psA[j]], writes=[r_stage])
                else:
                    sc.op("dve", lambda e, ci=ci, j=j: e.tensor_copy(out=stage[:, ci, :], in_=psA[j][:, :]), reads=[r_psA[j]], writes=[r_stage])
            sc.dma("sp", pv[:, :, t0:t0 + TN], stage[:, :, :], reads=[r_stage], writes=[r_proj])
            for k in range(8):
                sc.op("pe", lambda e, k=k, H=H: e.matmul(psG[0:24, :], W[:, k, 2816:2840], H[:, k, :], start=(k == 0), stop=(k == 7)),
                      reads=[r_w[k], rH], writes=[r_psG])
            sc.op("act", lambda e: e.activation(out=gst[:, :], in_=psG[0:24, :], func=AF.Sigmoid), reads=[r_psG], writes=[r_gst])
            sc.dma("sp", gT[:, t0:t0 + TN], gst[:, :], reads=[r_gst], writes=[r_g])
            for s in range(4):
                vs, rvs = vst[s % 2], r_vst[s % 2]
                for gi, (c0, wd_, v0) in enumerate(VT_GROUPS):
                    for k in range(8):
                        sc.op("pe", lambda e, k=k, gi=gi, c0=c0, wd_=wd_, s=s, H=H: e.matmul(
                            psV[gi][:, 0:wd_], H[:, k, s * 128:(s + 1) * 128], W[:, k, c0:c0 + wd_], start=(k == 0), stop=(k == 7)),
                            reads=[r_w[k], rH], writes=[r_psV[gi]])
                    if gi == 0:
                        sc.op("act", lambda e, gi=gi, wd_=wd_, v0=v0, vs=vs: e.copy(out=vs[:, v0:v0 + wd_], in_=psV[gi][:, 0:wd_]),
                              reads=[r_psV[gi]], writes=[rvs])
                    else:
                        sc.op("dve", lambda e, gi=gi, wd_=wd_, v0=v0, vs=vs: e.tensor_copy(out=vs[:, v0:v0 + wd_], in_=psV[gi][:, 0:wd_]),
                              reads=[r_psV[gi]], writes=[rvs])
                tt = t0 + s * 128
                sc.dma("sp", Vtok[tt:tt + 128, :], vs[:, :], reads=[rvs], writes=[r_vt])
        sc.emit()


class AttnCtx:
    def __init__(self, nc, sc, es, tag):
        self.nc, self.sc = nc, sc

        def sb(name, shape, dt):
            return es.enter_context(nc.sbuf_tensor(f"{tag}_{name}", shape, dt))

        def pst(name):
            return es.enter_context(nc.psum_tensor(f"{tag}_{name}", [128, 512], F32))
        self.sb, self.pst = sb, pst
        self.ST = [pst(f"st{i}") for i in range(3)]
        self.rST = [Res(f"st{i}") for i in range(3)]
        self.ACC = [pst(f"acc{i}") for i in range(2)]
        self.rACC = [Res(f"acc{i}") for i in range(2)]
        self.BC = pst("bc")
        self.rBC = Res("bc")
        self.PT = [sb(f"pt{i}", [128, 512], BF16) for i in range(4)]
        self.rPT = [Res(f"pt{i}") for i in range(4)]
        self.rdb = [sb(f"rd{i}", [128, 512], F32) for i in range(2)]
        self.rRDb = [Res(f"rd{i}") for i in range(2)]
        self.rdi = 0
        self.rdh = [sb(f"rdh{i}", [128, 1024], BF16) for i in range(2)]
        self.rRDh = [Res(f"rdh{i}") for i in range(2)]
        self.sel64 = sb("sel64", [128, 128], BF16)
        self.osb = [sb(f"osb{i}", [64, 512], F32) for i in range(2)]
        self.rOSB = [Res(f"osb{i}") for i in range(2)]
        self.sti = self.pti = self.acci = self.osi = 0
        self.pending = []
        self.deferred = []
        self.ident = sb("ident", [128, 128], BF16)
        self.jx = sb("jx", [128, 128], BF16)
        self.ones32 = sb("ones32", [128, 64], F32)
        self.t31 = sb("t31", [128, 16], F32)
        self.rC = Res("const")

    def load_consts(self, cst, rel_bias):
        sc = self.sc
        sc.dma("pool", self.ident[:, :], cst["c_ident"][:, :], writes=[self.rC])
        sc.dma("pool", self.jx[:, :], cst["c_jx"][:, :], writes=[self.rC])
        sc.op("dve", lambda e: e.memset(self.ones32[:, :], 1.0), writes=[self.rC])
        sc.op("dve", lambda e: e.memset(self.sel64[:, :], 0.0), writes=[self.rC])
        sc.op("dve", lambda e: e.memset(self.sel64[64:65, :], 1.0), writes=[self.rC])
        for i in range(2):
            sc.op("dve", lambda e, i=i: e.memset(self.rdh[i][:, :], 0.0), writes=[self.rRDh[i]])
        sc.dma("sp", self.t31[:, :], dram_view(rel_bias.tensor, 31 * 16, [[0, 128], [1, 16]]), writes=[self.rC])

    LOOKAHEAD = 2

    def tile(self, lhsT, rK, rhs, rQ, bias_tile, rB, head, vaug, rV, acc, racc, first, last, extra=None, after=None, cols=(0, 512)):
        sc = self.sc
        st, rst = self.ST[self.sti % 3], self.rST[self.sti % 3]
        self.sti += 1
        pt, rpt = self.PT[self.pti % 4], self.rPT[self.pti % 4]
        self.pti += 1
        near = bias_tile is not None
        lo, hi = cols
        assert not (first and (lo, hi) != (0, 512))
        sc.op("pe", lambda e: e.matmul(st[:, lo:hi], lhsT, rhs[:, lo:hi], start=True, stop=not near), reads=[rK, rQ], writes=[rst])
        if near:
            sc.op("pe", lambda e: e.matmul(st[:, lo:hi], self.jx[:, :], bias_tile[:, lo:hi], start=False, stop=True), reads=[rB, self.rC], writes=[rst])
            sc.op("act", lambda e: e.activation(out=pt[:, lo:hi], in_=st[:, lo:hi], func=AF.Exp, scale=0.125), reads=[rst], writes=[rpt])
        else:
            sc.op("act", lambda e: e.activation(out=pt[:, lo:hi], in_=st[:, lo:hi], func=AF.Exp, scale=0.125, bias=self.t31[:, head:head + 1]),
                  reads=[rst, self.rC], writes=[rpt])

        def pv():
            sc.op("pe", lambda e: e.matmul(acc[0:65, lo:hi], vaug, pt[:, lo:hi], start=first, stop=last), reads=[rV, rpt], writes=[racc])
            if extra is not None:
                extra(pt, rpt)
            if after is not None:
                after()
        self.pending.append(pv)
        while len(self.pending) > self.LOOKAHEAD:
            self.pending.pop(0)()
        for d in self.deferred:
            d[0] -= 1
        while self.deferred and self.deferred[0][0] <= 0:
            self.deferred.pop(0)[1]()

    def flush(self):
        while self.pending:
            self.pending.pop(0)()
        while self.deferred:
            self.deferred.pop(0)[1]()

    def next_acc(self):
        a, r = self.ACC[self.acci % 2], self.rACC[self.acci % 2]
        self.acci += 1
        return a, r

    def finish(self, acc, racc, gate_row, rG, out_ap, rOut, mode, add_ap=None, rAdd=None, split=True, defer=4):
        sc = self.sc
        rd, rRD = self.rdb[self.rdi % 2], self.rRDb[self.rdi % 2]
        rh, rRH = self.rdh[self.rdi % 2], self.rRDh[self.rdi % 2]
        self.rdi += 1
        sc.op("dve", lambda e: e.tensor_scalar(out=rd[64:65, :], in0=acc[64:65, :], scalar1=1e-30, scalar2=None, op0=ALU.max),
              reads=[racc], writes=[rRD])
        sc.op("dve", lambda e: e.reciprocal(out=rd[64:65, :], in_=rd[64:65, :]), reads=[rRD], writes=[rRD])
        if gate_row is not None:
            sc.op("dve", lambda e: e.tensor_tensor(out=rd[64:65, :], in0=rd[64:65, :], in1=gate_row, op=ALU.mult),
                  reads=[rRD, rG], writes=[rRD])
        sc.op("dve", lambda e: e.tensor_copy(out=rh[64:65, 0:512], in_=rd[64:65, :]), reads=[rRD], writes=[rRH])
        sc.op("dve", lambda e: e.tensor_tensor(out=rh[64:65, 512:1024], in0=rd[64:65, :], in1=rh[64:65, 0:512], op=ALU.subtract),
              reads=[rRD, rRH], writes=[rRH])

        def part_b():
            sc.op("pe", lambda e: e.matmul(self.BC[:, :], self.sel64[:, :], rh[:, 0:512], start=True, stop=False),
                  reads=[rRH, self.rC], writes=[self.rBC])
            sc.op("pe", lambda e: e.matmul(self.BC[:, :], self.sel64[:, :], rh[:, 512:1024], start=False, stop=True),
                  reads=[rRH, self.rC], writes=[self.rBC])
            osb, rosb = self.osb[self.osi % 2], self.rOSB[self.osi % 2]
            self.osi += 1
            sc.op("act", lambda e: e.copy(out=osb[:, :], in_=acc[0:64, :]), reads=[racc], writes=[rosb])
            if mode == "set":
                sc.op("dve", lambda e: e.tensor_tensor(out=out_ap, in0=osb[:, :], in1=self.BC[0:64, :], op=ALU.mult),
                      reads=[rosb, self.rBC], writes=[rOut])
            else:
                sc.op("dve", lambda e: e.tensor_tensor(out=osb[:, :], in0=osb[:, :], in1=self.BC[0:64, :], op=ALU.mult),
                      reads=[rosb, self.rBC], writes=[rosb])
                sc.op("dve", lambda e: e.tensor_tensor(out=out_ap, in0=osb[:, :], in1=add_ap, op=ALU.add),
                      reads=[rosb, rAdd], writes=[rOut])
        if split:
            self.deferred.append([defer, part_b])
        else:
            part_b()


def rvec(R, h, off, pstride):
    return dram_view(R.tensor, h * RLEN + off, [[pstride, 128], [1, 512]])


def moba_phase(nc, tag, projT, Vtok, Rc, rel_bias, cst, OT_all, heads=range(8), nqt=8):
    sc = Sched(nc, tag)
    with ExitStack() as es:
        A = AttnCtx(nc, sc, es, tag)
        sb, pst = A.sb, A.pst
        A.load_consts(cst, rel_bias)
        QA = [sb(f"qa{i}", [128, S], BF16) for i in range(2)]
        KA = [sb(f"ka{i}", [128, S], BF16) for i in range(2)]
        VA = [sb(f"va{i}", [128, 32, 65], BF16) for i in range(2)]
        BT = [sb(f"bt{i}", [128, 5, 512], BF16) for i in range(2)]
        OTs = [sb(f"ots{i}", [64, S], BF16) for i in range(2)]
        cneg = sb("cneg", [128, 32, 16], F32)
        callow = sb("callow", [128, 32, 16], F32)
        cown = sb("cown", [128, 32, 16], F32)
        km32 = sb("km32", [64, 16], F32)
        kmT = sb("kmT", [64, 16], BF16)
        gmA = sb("gmA", [128, 32, 16], F32)
        g2A = sb("g2A", [128, 32, 16], F32)
        eqA = sb("eqA", [128, 32, 16], F32)
        mx = sb("mx", [128, 32], F32)
        mpadA = sb("mpadA", [128, 32, 80], BF16)
        psG = pst("psg")
        psT = pst("pst")
        rQq = [Res("qq0"), Res("qq1")]
        rQm = [Res("qm0"), Res("qm1")]
        rK = [Res("k0"), Res("k1")]
        rKaug = [Res("kaug0"), Res("kaug1")]
        rV = [Res("v0"), Res("v1")]
        rVone = [Res("vone0"), Res("vone1")]
        rBT = [[Res(f"bt{b}_{i}") for i in range(5)] for b in range(2)]
        rOTs = [Res("ots0"), Res("ots1")]
        rGc, rkm32, rkm, rgm, rtop8, rsel, rmpad, rmpad0, rg2 = (Res(n) for n in ("gc", "km32", "km", "gm", "top8", "sel", "mpad", "mpad0", "g2"))
        rpsG, rpsT = Res("psg"), Res("pst")
        r_proj, r_vt, r_R, r_ot = Res("projT"), Res("Vtok"), Res("Rc"), Res("OT")
        sc.dma("sp", cneg[:, :, :], cst["c_mneg"][:, :, :], writes=[rGc])
        sc.dma("sp", callow[:, :, :], cst["c_mallow"][:, :, :], writes=[rGc])
        sc.dma("sp", cown[:, :, :], cst["c_mown"][:, :, :], writes=[rGc])
        sc.op("dve", lambda e: e.memset(mpadA[:, :, :], 0.0), writes=[rmpad0])
        for i in range(2):
            sc.op("pool", lambda e, i=i: e.memset(VA[i][:, :, 64:65], 1.0), writes=[rVone[i]])
            sc.dma("pool", KA[i][64:80, :], cst["c_kaugm"][:, :], writes=[rKaug[i]])
        vtv = Vtok.rearrange("(t p) c -> p t c", p=128)
        heads = list(heads)

        def load_head(h):
            hb = h % 2
            qrow = (h // 2) * 128 + (h % 2) * 64
            krow = (4 + h // 2) * 128 + (h % 2) * 64
            sc.dma("sp", QA[hb][0:64, :], projT[qrow:qrow + 64, :], reads=[r_proj], writes=[rQq[hb]])
            sc.dma("sp", KA[hb][0:64, :], projT[krow:krow + 64, :], reads=[r_proj], writes=[rK[hb]])
            sc.dma("sp", VA[hb][:, :, 0:64], vtv[:, :, h * 64:(h + 1) * 64], reads=[r_vt], writes=[rV[hb]], allow_slow_non_contiguous=True)
            for i in range(5):
                rel = i - 1
                sc.dma("sp", BT[hb][:, i, :], rvec(Rc, h, ROFF - 128 * rel - 127, 1), reads=[r_R], writes=[rBT[hb][i]])

        def gating(h):
            hb = h % 2
            sc.op("dve", lambda e, hb=hb: e.tensor_reduce(out=km32[:, :], in_=KA[hb][0:64, :].rearrange("p (n b) -> p n b", b=256),
                                                          axis=AX.X, op=ALU.add), reads=[rK[hb]], writes=[rkm32])
            sc.op("dve", lambda e: e.tensor_scalar(out=kmT[:, :], in0=km32[:, :], scalar1=1.0 / 256, scalar2=None, op0=ALU.mult),
                  reads=[rkm32], writes=[rkm])
            for i in range(4 * nqt):
                sc.op("pe", lambda e, hb=hb, i=i: e.matmul(psG[:, i * 16:(i + 1) * 16], QA[hb][0:64, i * 128:(i + 1) * 128], kmT[:, :],
                                                          start=True, stop=True), reads=[rQq[hb], rkm], writes=[rpsG])
            nq = 4 * nqt
            psGv = psG[:, 0:nq * 16].rearrange("p (s n) -> p s n", n=16)
            mb = mx[:, 0:nq].to_broadcast([128, nq, 16])
            sc.op("dve", lambda e: e.tensor_tensor(out=gmA[:, 0:nq, :], in0=psGv, in1=cneg[:, 0:nq, :], op=ALU.add), reads=[rpsG, rGc], writes=[rgm])
            src, rsrc = gmA, rgm
            for rnd in range(2):
                sc.op("dve", lambda e, src=src: e.tensor_reduce(out=mx[:, 0:nq], in_=src[:, 0:nq, :], axis=AX.X, op=ALU.max), reads=[rsrc], writes=[rtop8])
                sc.op("dve", lambda e, src=src: e.tensor_tensor(out=eqA[:, 0:nq, :], in0=src[:, 0:nq, :], in1=mb, op=ALU.is_ge),
                      reads=[rsrc, rtop8], writes=[rsel])
                sc.op("dve", lambda e, src=src: e.scalar_tensor_tensor(out=g2A[:, 0:nq, :], in0=eqA[:, 0:nq, :], scalar=-1e32, in1=src[:, 0:nq, :],
                                                                      op0=ALU.mult, op1=ALU.add), reads=[rsel, rsrc], writes=[rg2])
                src, rsrc = g2A, rg2
            sc.op("dve", lambda e: e.tensor_reduce(out=mx[:, 0:nq], in_=g2A[:, 0:nq, :], axis=AX.X, op=ALU.max), reads=[rg2], writes=[rtop8])
            sc.op("dve", lambda e: e.tensor_tensor(out=eqA[:, 0:nq, :], in0=gmA[:, 0:nq, :], in1=mb, op=ALU.is_ge), reads=[rgm, rtop8], writes=[rsel])
            sc.op("dve", lambda e: e.tensor_tensor(out=eqA[:, 0:nq, :], in0=eqA[:, 0:nq, :], in1=callow[:, 0:nq, :], op=ALU.mult),
                  reads=[rsel, rGc], writes=[rsel])
            sc.op("dve", lambda e: e.tensor_tensor(out=eqA[:, 0:nq, :], in0=eqA[:, 0:nq, :], in1=cown[:, 0:nq, :], op=ALU.add),
                  reads=[rsel, rGc], writes=[rsel])
            sc.op("dve", lambda e: e.tensor_scalar(out=mpadA[:, 0:nq, 64:80], in0=eqA[:, 0:nq, :], scalar1=BIG, scalar2=-BIG, op0=ALU.mult, op1=ALU.add),
                  reads=[rsel, rmpad0], writes=[rmpad])
            for i4 in range(nqt):
                for s_ in range(4):
                    sc.op("pe", lambda e, s_=s_, i4=i4: e.matmul(psT[0:80, s_ * 128:(s_ + 1) * 128], mpadA[:, i4 * 4 + s_, :], A.ident[:, :], start=True, stop=True),
                          reads=[rmpad, A.rC], writes=[rpsT])
                sc.op("act", lambda e, hb=hb, i4=i4: e.copy(out=QA[hb][64:80, i4 * 512:(i4 + 1) * 512], in_=psT[64:80, :]),
                      reads=[rpsT], writes=[rQm[hb]])

        load_head(heads[0])
        for hi, h in enumerate(heads):
            hb = h % 2
            if hi + 1 < len(heads):
                load_head(heads[hi + 1])
            if hi == 0:
                gating(h)
            for qt in range(nqt):
                acc, racc = A.next_acc()
                kts = list(range(0, 4 * qt + 4))
                for idx, kt in enumerate(kts):
                    near = kt >= 4 * qt - 1
                    lastt = idx == len(kts) - 1
                    fin = None
                    if lastt:
                        def fin(acc=acc, racc=racc, hb=hb, qt=qt):
                            A.finish(acc, racc, None, None, OTs[hb][:, qt * 512:(qt + 1) * 512], rOTs[hb], "set", defer=min(10, 4 * qt + 8))
                    A.tile(KA[hb][0:80, kt * 128:(kt + 1) * 128], Res2(rK[hb], rKaug[hb]), QA[hb][0:80, qt * 512:(qt + 1) * 512], Res2(rQq[hb], rQm[hb]),
                           BT[hb][:, kt - (4 * qt - 1), :] if near else None, rBT[hb][kt - (4 * qt - 1)] if near else None, h, VA[hb][:, kt, :], Res2(rV[hb], rVone[hb]),
                           acc, racc, idx == 0, lastt, after=fin, cols=(max(0, 128 * (kt - 4 * qt)), 512))
                if qt == min(5, nqt - 1) and hi + 1 < len(heads):
                    gating(heads[hi + 1])
            A.flush()
            sc.dma("sp", OT_all[h * 64:(h + 1) * 64, :], OTs[hb][:, :], reads=[rOTs[hb]], writes=[r_ot])
        sc.emit()


def compress_phase(nc, tag, projT, pos_k, w1_k, w2_k, pos_v, w1_v, w2_v, kcmpT, vcmp):
    sc = Sched(nc, tag)
    with ExitStack() as es:
        def sb(name, shape, dt):
            return es.enter_context(nc.sbuf_tensor(f"{tag}_{name}", shape, dt))

        def pst(name):
            return es.enter_context(nc.psum_tensor(f"{tag}_{name}", [128, 512], F32))
        w1 = sb("w1", [64, 32, 256], BF16)
        w2 = sb("w2", [128, 2, 64], BF16)
        posT = sb("posT", [64, 32], BF16)
        cb = sb("cb", [128, 2], F32)
        raw = sb("raw", [64, S], BF16)
        hid = sb("hid", [128, 2, 256], BF16)
        kc = sb("kc", [64, 256], BF16)
        vc = sb("vc", [128, 2, 64], BF16)
        psH = [pst(f"psh{i}") for i in range(2)]
        psC = pst("psc")
        ps2 = pst("ps2")
        r_w1, r_w2, r_pos, r_cb, r_raw, r_hid, r_kc, r_vc = (Res(n) for n in ("w1", "w2", "pos", "cb", "raw", "hid", "kc", "vc"))
        r_psH = [Res("psh0"), Res("psh1")]
        r_psC, r_ps2 = Res("psc"), Res("ps2")
        r_proj, r_out = Res("projT"), Res("out")
        sc.op("dve", lambda e: e.memset(kc[:, :], 0.0), writes=[r_kc])
        sc.op("dve", lambda e: e.memset(vc[:, :, :], 0.0), writes=[r_vc])
        sc.op("dve", lambda e: e.memset(hid[:, :, :], 0.0), writes=[r_hid])
        for kv, (pos, w1d, w2d, chunk) in enumerate(((pos_k, w1_k, w2_k, 12), (pos_v, w1_v, w2_v, 13))):
            sc.dma("pool", w1[:, :, :], w1d.rearrange("(l d) j -> d l j", d=64), writes=[r_w1])
            sc.dma("pool", w2[:, :, :], w2d.rearrange("(c p) d -> p c d", p=128), writes=[r_w2])
            sc.dma("pool", posT[:, :], pos.rearrange("l d -> d l"), writes=[r_pos], allow_slow_non_contiguous=True)
            for jc in range(2):
                for l in range(32):
                    sc.op("pe", lambda e, l=l, jc=jc: e.matmul(psC[:, 0:1], w1[:, l, jc * 128:(jc + 1) * 128], posT[:, l:l + 1],
                                                             start=(l == 0), stop=(l == 31)), reads=[r_w1, r_pos], writes=[r_psC])
                sc.op("act", lambda e, jc=jc: e.copy(out=cb[:, jc:jc + 1], in_=psC[:, 0:1]), reads=[r_psC], writes=[r_cb])
            for g in range(2):
                row = chunk * 128 + g * 64
                sc.dma("sp", raw[:, :], projT[row:row + 64, :], reads=[r_proj], writes=[r_raw])
                rv = raw[:, :].rearrange("p (n s) -> p n s", s=16)
                for jc in range(2):
                    for l in range(32):
                        rhs = rv[:, 0:255, l] if l < 16 else rv[:, 1:256, l - 16]
                        sc.op("pe", lambda e, l=l, jc=jc, rhs=rhs: e.matmul(psH[jc][:, 0:255], w1[:, l, jc * 128:(jc + 1) * 128], rhs,
                                                                          start=(l == 0), stop=(l == 31)), reads=[r_w1, r_raw], writes=[r_psH[jc]])
                    sc.op("act", lambda e, jc=jc: e.activation(out=hid[:, jc, 0:255], in_=psH[jc][:, 0:255], func=AF.Silu, bias=cb[:, jc:jc + 1]),
                          reads=[r_psH[jc], r_cb], writes=[r_hid])
                if kv == 0:
                    for jc in range(2):
                        sc.op("pe", lambda e, jc=jc: e.matmul(ps2[0:64, 0:255], w2[:, jc, :], hid[:, jc, 0:255], start=(jc == 0), stop=(jc == 1)),
                              reads=[r_w2, r_hid], writes=[r_ps2])
                    sc.op("dve", lambda e: e.tensor_copy(out=kc[:, 0:255], in_=ps2[0:64, 0:255]), reads=[r_ps2], writes=[r_kc])
                    sc.dma("sp", kcmpT[g * 64:(g + 1) * 64, :], kc[:, :], reads=[r_kc], writes=[r_out])
                else:
                    for c in range(2):
                        for jc in range(2):
                            sc.op("pe", lambda e, jc=jc, c=c: e.matmul(ps2[:, c * 64:(c + 1) * 64], hid[:, jc, c * 128:(c + 1) * 128], w2[:, jc, :],
                                                                     start=(jc == 0), stop=(jc == 1)), reads=[r_w2, r_hid], writes=[r_ps2])
                    sc.op("dve", lambda e: e.tensor_copy(out=vc[:, :, :], in_=ps2[:, 0:128].rearrange("p (c d) -> p c d", d=64)),
                          reads=[r_ps2], writes=[r_vc])
                    sc.dma("sp", vcmp[g * 256:(g + 1) * 256, :].rearrange("(c p) d -> p c d", p=128), vc[:, :, :], reads=[r_vc], writes=[r_out])
        sc.emit()


def nsa_phase(nc, tag, projT, Vtok, gT, kcmpT, vcmp, Rc, Rw, rel_bias, cst, OT_all, groups=range(2), nqt=8, parts=('cmp', 'sel', 'sw'), selv=2):
    sc = Sched(nc, tag)
    with ExitStack() as es:
        A = AttnCtx(nc, sc, es, tag)
        sb, pst = A.sb, A.pst
        A.load_consts(cst, rel_bias)
        QN = [sb(f"qn{i}", [128, S], BF16) for i in range(4)]
        KS = sb("ks", [128, S], BF16)
        KW = sb("kw", [128, S], BF16)
        VS = sb("vs", [128, 32, 65], BF16)
        VW = sb("vw", [128, 32, 65], BF16)
        KC = sb("kc", [128, 256], BF16)
        VC = sb("vc", [128, 2, 65], BF16)
        OTc = [sb(f"otc{i}", [64, S], BF16) for i in range(4)]
        OTs = [sb(f"ots{i}", [64, S], BF16) for i in range(2)]
        OTh = sb("oth", [64, 512], F32)
        BT = [sb(f"bt{i}", [128, 13, 512], BF16) for i in range(2)]
        CB = [sb(f"cbt{i}", [128, 512], BF16) for i in range(2)]
        G = [sb(f"g{i}", [128, 3, 512], F32) for i in range(2)]
        imp = sb("imp", [128, 32, 64], F32)
        cslc = sb("cslc", [128, 32, 64], F32)
        ovl = sb("ovl", [128, 2, 65], BF16)
        rdi = sb("rdi", [128, 4], F32)
        scr = sb("scr", [128, 4, 64], F32)
        wk = sb("wk", [128, 4, 64], F32)
        t8a = sb("t8a", [128, 4, 8], F32)
        t8b = sb("t8b", [128, 4, 8], F32)
        mpad = sb("mpad", [128, 4, 128], BF16)
        psI = pst("psi")
        psT = pst("pst")
        rQq = [Res(f"qq{i}") for i in range(4)]
        rQm = [Res(f"qm{i}") for i in range(4)]
        rKS, rKSaug, rKW, rVS, rVW, rVone, rKC, rVC = (Res(n) for n in ("ks", "ksaug", "kw", "vs", "vw", "vone", "kc", "vc"))
        rOTc = [Res(f"otc{i}") for i in range(4)]
        rOTs = [Res("ots0"), Res("ots1")]
        rOTh = Res("oth")
        rBT = [[Res(f"bt{b}_{i}") for i in range(13)] for b in range(2)]
        rCB = [Res("cb0"), Res("cb1")]
        rG = [Res("g0"), Res("g1")]
        rimp, rC2, rrdi, rscr, rwk, rt8a, rt8b, rmpad, rmpad0 = (Res(n) for n in ("imp", "c2", "rdi", "scr", "wk", "t8a", "t8b", "mpad", "mpad0"))
        rpsI, rpsT = Res("psi"), Res("pst")
        r_proj, r_vt, r_g, r_kc, r_vc, r_R, r_ot = (Res(n) for n in ("projT", "Vtok", "gT", "kcmp", "vcmp", "R", "OT"))
        sc.dma("sp", cslc[:, :, :], cst["c_slc"][:, :, :], writes=[rC2])
        sc.dma("pool", ovl[:, :, :], cst["c_ovl"].rearrange("(c p) j -> p c j", p=128), writes=[rC2])
        sc.dma("pool", KS[64:128, :], cst["c_kaugs"][:, :], writes=[rKSaug])
        sc.op("dve", lambda e: e.memset(mpad[:, :, :], 0.0), writes=[rmpad0])
        sc.op("dve", lambda e: e.memset(imp[:, :, :], 0.0), writes=[rimp])
        rKz = Res("kzero")
        sc.op("pool", lambda e: e.memset(KW[64:128, :], 0.0), writes=[rKz])
        sc.op("pool", lambda e: e.memset(KC[64:128, :], 0.0), writes=[rKz])
        for r in range(4):
            sc.op("pool", lambda e, r=r: e.memset(QN[r][64:128, :], 0.0), writes=[rQm[r]])
        sc.op("pool", lambda e: e.memset(VS[:, :, 64:65], 1.0), writes=[rVone])
        sc.op("pool", lambda e: e.memset(VW[:, :, 64:65], 1.0), writes=[rVone])
        sc.op("pool", lambda e: e.memset(VC[:, :, 64:65], 1.0), writes=[rVone])
        vtv = Vtok.rearrange("(t p) c -> p t c", p=128)
        gi = [0]

        def load_gates(hb, qt):
            j = gi[0] % 2
            gi[0] += 1
            sc.dma("sp", G[j][64:65, :, :], dram_view(gT.tensor, (hb * 3) * S + qt * 512, [[0, 1], [S, 3], [1, 512]]),
                   reads=[r_g], writes=[rG[j]])
            return G[j], rG[j]

        def load_bt(hb):
            hg, b2 = 8 + hb, hb % 2
            for i in range(5):
                sc.dma("sp", BT[b2][:, i, :], rvec(Rc, hg, ROFF - 128 * (i - 1) - 127, 1), reads=[r_R], writes=[rBT[b2][i]])
            for i in range(8):
                sc.dma("sp", BT[b2][:, 5 + i, :], rvec(Rw, hg, ROFF - 128 * (i - 4) - 127, 1), reads=[r_R], writes=[rBT[b2][5 + i]])

        for g in groups:
            sc.dma("sp", KS[0:64, :], projT[14 * 128 + g * 64:14 * 128 + g * 64 + 64, :], reads=[r_proj], writes=[rKS])
            sc.dma("sp", KW[0:64, :], projT[15 * 128 + g * 64:15 * 128 + g * 64 + 64, :], reads=[r_proj], writes=[rKW])
            sc.dma("sp", VS[:, :, 0:64], vtv[:, :, 512 + g * 64:512 + (g + 1) * 64], reads=[r_vt], writes=[rVS], allow_slow_non_contiguous=True)
            sc.dma("sp", VW[:, :, 0:64], vtv[:, :, 640 + g * 64:640 + (g + 1) * 64], reads=[r_vt], writes=[rVW], allow_slow_non_contiguous=True)
            sc.dma("sp", KC[0:64, :], kcmpT[g * 64:(g + 1) * 64, :], reads=[r_kc], writes=[rKC])
            sc.dma("sp", VC[:, :, 0:64], vcmp[g * 256:(g + 1) * 256, :].rearrange("(c p) d -> p c d", p=128), reads=[r_vc], writes=[rVC],
                   allow_slow_non_contiguous=True)
            for r in range(4):
                hb = 4 * g + r
                qrow = (8 + hb // 2) * 128 + (hb % 2) * 64
                sc.dma("sp", QN[r][0:64, :], projT[qrow:qrow + 64, :], reads=[r_proj], writes=[rQq[r]])
            if 'sw' in parts:
                load_bt(4 * g)
            cbi = 0
            for r in (range(4) if 'cmp' in parts else []):
                hb = 4 * g + r
                hg = 8 + hb
                for qt in range(nqt):
                    Gt, rGt = load_gates(hb, qt)
                    acc, racc = A.next_acc()
                    chunks = [0] if qt <= 3 else [0, 1]
                    for ci, c in enumerate(chunks):
                        needb = (c == 1) or (qt <= 4)
                        bt, rbt = None, None
                        if needb:
                            bt, rbt = CB[cbi % 2], rCB[cbi % 2]
                            cbi += 1
                            sc.dma("sp", bt[:, :], rvec(Rc, hg, ROFF + 512 * qt - 2048 * c - 2063, 16), reads=[r_R], writes=[rbt])
                        first, last = ci == 0, ci == len(chunks) - 1

                        def extra(pt, rpt, c=c, first=first, last=last):
                            for s in range(4):
                                sc.op("pe", lambda e, s=s: e.matmul(psI[:, s * 65:(s + 1) * 65], pt[:, s * 128:(s + 1) * 128], ovl[:, c, :],
                                                                    start=(first and s == 0), stop=last, skip_group_check=True),
                                      reads=[rpt, rC2], writes=[rpsI])
                        A.tile(KC[0:128, c * 128:(c + 1) * 128], Res2(rKC, rKz), QN[r][0:128, qt * 512:(qt + 1) * 512], Res2(rQq[r], rQm[r]),
                               bt[:, :] if needb else None, rbt, hg, VC[:, c, :], Res2(rVC, rVone), acc, racc, first, last, extra=extra)
                    A.flush()
                    A.finish(acc, racc, Gt[64:65, 0, :], rGt, OTc[r][:, qt * 512:(qt + 1) * 512], rOTc[r], "set", split=False)
                    pv = psI[:, 0:260].rearrange("p (s j) -> p s j", j=65)
                    sc.op("dve", lambda e, pv=pv: e.tensor_scalar(out=rdi[:, :], in0=pv[:, :, 64], scalar1=1e-30, scalar2=None, op0=ALU.max),
                          reads=[rpsI], writes=[rrdi])
                    sc.op("dve", lambda e: e.reciprocal(out=rdi[:, :], in_=rdi[:, :]), reads=[rrdi], writes=[rrdi])
                    for s in range(4):
                        i = qt * 4 + s
                        if r == 0:
                            sc.op("dve", lambda e, s=s, i=i: e.tensor_scalar(out=imp[:, i, :], in0=psI[:, s * 65:s * 65 + 64], scalar1=rdi[:, s:s + 1],
                                                                          scalar2=None, op0=ALU.mult), reads=[rpsI, rrdi], writes=[rimp])
                        else:
                            sc.op("dve", lambda e, s=s, i=i: e.scalar_tensor_tensor(out=imp[:, i, :], in0=psI[:, s * 65:s * 65 + 64], scalar=rdi[:, s:s + 1],
                                                                                 in1=imp[:, i, :], op0=ALU.mult, op1=ALU.add),
                                  reads=[rpsI, rrdi, rimp], writes=[rimp])
            for i4 in (range(nqt) if 'sel' in parts else []):
                sc.op("dve", lambda e, i4=i4: e.tensor_tensor(out=scr[:, :, :], in0=imp[:, i4 * 4:(i4 + 1) * 4, :], in1=cslc[:, i4 * 4:(i4 + 1) * 4, :],
                                                              op=ALU.add), reads=[rimp, rC2], writes=[rscr])
                for s in range(4):
                    sc.op("dve", lambda e, s=s: e.max(out=t8a[:, s, :], in_=scr[:, s, :]), reads=[rscr], writes=[rt8a])
                    sc.op("dve", lambda e, s=s: e.tensor_scalar(out=wk[:, s, :], in0=scr[:, s, :], scalar1=t8a[:, s, 7:8], scalar2=-6e4,
                                                                op0=ALU.is_ge, op1=ALU.mult), reads=[rscr, rt8a], writes=[rwk])
                    sc.op("dve", lambda e, s=s: e.tensor_tensor(out=wk[:, s, :], in0=wk[:, s, :], in1=scr[:, s, :], op=ALU.add),
                          reads=[rscr, rwk], writes=[rwk])
                    sc.op("dve", lambda e, s=s: e.max(out=t8b[:, s, :], in_=wk[:, s, :]), reads=[rwk], writes=[rt8b])
                    sc.op("dve", lambda e, s=s: e.tensor_scalar(out=wk[:, s, :], in0=scr[:, s, :], scalar1=t8b[:, s, 7:8], scalar2=BIG,
                                                                op0=ALU.is_ge, op1=ALU.mult), reads=[rscr, rt8b, rwk], writes=[rwk])
                sc.op("dve", lambda e: e.tensor_scalar(out=mpad[:, :, 64:128], in0=wk[:, :, :], scalar1=-BIG, scalar2=None, op0=ALU.add),
                      reads=[rwk, rmpad0], writes=[rmpad])
                for s in (range(4) if selv >= 1 else []):
                    sc.op("pe", lambda e, s=s: e.matmul(psT[:, s * 128:(s + 1) * 128], mpad[:, s, :], A.ident[:, :], start=True, stop=True),
                          reads=[rmpad, A.rC], writes=[rpsT])
                for r in (range(4) if selv >= 2 else []):
                    sc.op("act", lambda e, r=r, i4=i4: e.copy(out=QN[r][64:128, i4 * 512:(i4 + 1) * 512], in_=psT[64:128, :]),
                          reads=[rpsT], writes=[rQm[r]])
            for r in (range(4) if 'sw' in parts else []):
                hb = 4 * g + r
                hg = 8 + hb
                b2 = hb % 2
                if r + 1 < 4:
                    load_bt(4 * g + r + 1)
                for qt in range(nqt):
                    Gt, rGt = load_gates(hb, qt)
                    acc, racc = A.next_acc()
                    kts = list(range(0, 4 * qt + 4))
                    for idx, kt in enumerate(kts):
                        near = kt >= 4 * qt - 1
                        lastt = idx == len(kts) - 1
                        fin = None
                        if lastt:
                            def fin(acc=acc, racc=racc, Gt=Gt, rGt=rGt, r=r, qt=qt):
                                A.finish(acc, racc, Gt[64:65, 1, :], rGt, OTh[:, :], rOTh, "add", add_ap=OTc[r][:, qt * 512:(qt + 1) * 512], rAdd=rOTc[r],
                                         defer=(4 if qt == 0 else 8))
                        A.tile(KS[0:128, kt * 128:(kt + 1) * 128], Res2(rKS, rKSaug), QN[r][0:128, qt * 512:(qt + 1) * 512], Res2(rQq[r], rQm[r]),
                               BT[b2][:, kt - (4 * qt - 1), :] if near else None, rBT[b2][kt - (4 * qt - 1)] if near else None, hg, VS[:, kt, :], Res2(rVS, rVone),
                               acc, racc, idx == 0, lastt, after=fin, cols=(max(0, 128 * (kt - 4 * qt)), 512))
                    acc, racc = A.next_acc()
                    kts = list(range(max(0, 4 * qt - 4), 4 * qt + 4))
                    if qt > 0:
                        kts = [4 * qt - 1] + [k_ for k_ in kts if k_ != 4 * qt - 1]
                    for idx, kt in enumerate(kts):
                        lastt = idx == len(kts) - 1
                        rel = kt - 4 * qt
                        wcols = (128 * rel, 512) if rel >= 0 else (0, min(512, 128 * (rel + 5)))
                        fin = None
                        if lastt:
                            def fin(acc=acc, racc=racc, Gt=Gt, rGt=rGt, b2=b2, qt=qt):
                                A.finish(acc, racc, Gt[64:65, 2, :], rGt, OTs[b2][:, qt * 512:(qt + 1) * 512], rOTs[b2], "add", add_ap=OTh[:, :], rAdd=rOTh,
                                         defer=min(10, 4 * qt + 8))
                        A.tile(KW[0:128, kt * 128:(kt + 1) * 128], Res2(rKW, rKz), QN[r][0:128, qt * 512:(qt + 1) * 512], Res2(rQq[r], rQm[r]),
                               BT[b2][:, 5 + kt - (4 * qt - 4), :], rBT[b2][5 + kt - (4 * qt - 4)], hg, VW[:, kt, :], Res2(rVW, rVone),
                               acc, racc, idx == 0, lastt, after=fin, cols=wcols)
                A.flush()
                sc.dma("sp", OT_all[hg * 64:(hg + 1) * 64, :], OTs[b2][:, :], reads=[rOTs[b2]], writes=[r_ot])
        sc.emit()


def outproj_phase(nc, tag, OT_all, x1T, w_out, x2T, ntok=S):
    sc = Sched(nc, tag)
    TN = 512
    with ExitStack() as es:
        def sb(name, shape, dt):
            return es.enter_context(nc.sbuf_tensor(f"{tag}_{name}", shape, dt))
        wo = sb("wo", [128, 8, D], BF16)
        ot = [sb(f"ot{i}", [128, 8, TN], BF16) for i in range(2)]
        xt = [sb(f"xt{i}", [128, 8, TN], F32) for i in range(2)]
        ps = [es.enter_context(nc.psum_tensor(f"{tag}_ps{i}", [128, 512], F32)) for i in range(2)]
        r_w = Res("w")
        r_ot = [Res("ot0"), Res("ot1")]
        r_xt = [Res("xt0"), Res("xt1")]
        r_ps = [Res("ps0"), Res("ps1")]
        r_in, r_x1, r_x2 = Res("OT"), Res("x1"), Res("x2")
        for k in range(8):
            sc.dma("pool", wo[:, k, :], w_out[k * 128:(k + 1) * 128, :], writes=[r_w])
        ov = OT_all.rearrange("(c p) t -> p c t", p=128)
        x1v = x1T.rearrange("(c p) t -> p c t", p=128)
        x2v = x2T.rearrange("(c p) t -> p c t", p=128)
        for it in range(ntok // TN):
            t0 = it * TN
            b = it % 2
            sc.dma("sp", ot[b][:, :, :], ov[:, :, t0:t0 + TN], reads=[r_in], writes=[r_ot[b]])
            sc.dma("sp", xt[b][:, :, :], x1v[:, :, t0:t0 + TN], reads=[r_x1], writes=[r_xt[b]])
            for d in range(8):
                j = d % 2
                for o in range(8):
                    sc.op("pe", lambda e, o=o, d=d, j=j, b=b: e.matmul(ps[j][:, :], wo[:, o, d * 128:(d + 1) * 128], ot[b][:, o, :],
                                                                      start=(o == 0), stop=(o == 7)), reads=[r_w, r_ot[b]], writes=[r_ps[j]])
                sc.op("dve", lambda e, d=d, j=j, b=b: e.tensor_tensor(out=xt[b][:, d, :], in0=ps[j][:, :], in1=xt[b][:, d, :], op=ALU.add),
                      reads=[r_ps[j], r_xt[b]], writes=[r_xt[b]])
            sc.dma("sp", x2v[:, :, t0:t0 + TN], xt[b][:, :, :], reads=[r_xt[b]], writes=[r_x2])
        sc.emit()


W_NAMES = ["norm_ffn1", "w_ffn1_gate", "w_ffn1_up", "w_ffn1_down", "norm_mix", "w_in", "cmp_pos_k", "cmp_w1_k", "cmp_w2_k",
           "cmp_pos_v", "cmp_w1_v", "cmp_w2_v", "w_out", "norm_ffn2", "w_ffn2_gate", "w_ffn2_up", "w_ffn2_down"]
W_SHAPES = {"norm_ffn1": [D], "w_ffn1_gate": [D, DFF], "w_ffn1_up": [D, DFF], "w_ffn1_down": [DFF, D], "norm_mix": [D], "w_in": [D, DIN],
            "cmp_pos_k": [32, 64], "cmp_w1_k": [2048, 256], "cmp_w2_k": [256, 64], "cmp_pos_v": [32, 64], "cmp_w1_v": [2048, 256],
            "cmp_w2_v": [256, 64], "w_out": [D, D], "norm_ffn2": [D], "w_ffn2_gate": [D, DFF], "w_ffn2_up": [D, DFF], "w_ffn2_down": [DFF, D]}


def build_program(stages=None, debug_out=(), nsa_kw={}):
    nc = bass.Bass("TRN2", target_bir_lowering=False)
    cstn = static_consts()

    def din(name, shape, dt=F32):
        return nc.dram_tensor(name, list(shape), dt, kind="ExternalInput").ap()

    xT = din("xT", [D, S])
    w = {n: din(n, W_SHAPES[n]) for n in W_NAMES}
    rel_bias = din("rel_bias", [32, 16])
    norm_final = din("norm_final", [D])
    cst = {n: din(n, a.shape) for n, a in cstn.items()}
    outT = nc.dram_tensor("outT", [D, S], F32, kind="ExternalOutput").ap()

    def scratch(name, shape, dt):
        kind = "ExternalOutput" if name in debug_out else "Internal"
        return nc.dram_tensor(name, list(shape), dt, kind=kind).ap()
    x1T = scratch("x1T", [D, S], F32)
    h2T = scratch("h2T", [D, S], BF16)
    projT = scratch("projT", [16 * 128, S], BF16)
    Vtok = scratch("Vtok", [S, 768], BF16)
    gT = scratch("gT", [24, S], F32)
    Rc = scratch("Rc", [16, RLEN], BF16)
    Rw = scratch("Rw", [16, RLEN], BF16)
    kcmpT = scratch("kcmpT", [128, 256], BF16)
    vcmp = scratch("vcmp", [512, 64], BF16)
    OT_all = scratch("OT_all", [D, S], BF16)
    x2T = scratch("x2T", [D, S], F32)
    st = stages or ["const", "ffn1", "inproj", "moba", "compress", "nsa", "outproj", "ffn2"]
    if "const" in st:
        const_phase(nc, "c0", rel_bias, cst, Rc, Rw)
    if "ffn1" in st:
        ffn_phase(nc, "f1", xT, x1T, w["norm_ffn1"], w["w_ffn1_gate"], w["w_ffn1_up"], w["w_ffn1_down"], w["norm_mix"], h2T, BF16)
    if "inproj" in st:
        inproj_phase(nc, "ip", h2T, w["w_in"], projT, Vtok, gT)
    if "moba" in st:
        moba_phase(nc, "mo", projT, Vtok, Rc, rel_bias, cst, OT_all)
    if "compress" in st:
        compress_phase(nc, "cp", projT, w["cmp_pos_k"], w["cmp_w1_k"], w["cmp_w2_k"], w["cmp_pos_v"], w["cmp_w1_v"], w["cmp_w2_v"], kcmpT, vcmp)
    if "nsa" in st:
        nsa_phase(nc, "ns", projT, Vtok, gT, kcmpT, vcmp, Rc, Rw, rel_bias, cst, OT_all, **nsa_kw)
    if "outproj" in st:
        outproj_phase(nc, "op", OT_all, x1T, w["w_out"], x2T)
    if "ffn2" in st:
        ffn_phase(nc, "f2", x2T, None, w["norm_ffn2"], w["w_ffn2_gate"], w["w_ffn2_up"], w["w_ffn2_down"], norm_final, outT, F32)
    return nc


def make_in_map(inputs, b):
    m = {"xT": np.ascontiguousarray(inputs["x"][b].T)}
    for n in W_NAMES:
        m[n] = np.ascontiguousarray(np.asarray(inputs[n], dtype=np.float32)[0])
    m["rel_bias"] = np.ascontiguousarray(np.asarray(inputs["rel_bias"], dtype=np.float32))
    m["norm_final"] = np.ascontiguousarray(np.asarray(inputs["norm_final"], dtype=np.float32))
    m.update(static_consts())
    return m


def kernel(**inputs):
    inputs = {k: np.asarray(v) for k, v in inputs.items()}
    nb = inputs["x"].shape[0]
    nc = build_program()
    in_maps = [make_in_map(inputs, b) for b in range(nb)]
    res = run_bass_kernel_spmd(nc, in_maps, core_ids=list(range(nb)))
    out = np.stack([np.ascontiguousarray(np.asarray(r["outT"]).T) for r in res.results], axis=0)
    return out.astype(np.float32)
```

```python
import math
from contextlib import ExitStack

import numpy as np
import concourse.bass as bass
import concourse.mybir as mybir
from concourse.bass_utils import run_bass_kernel_spmd

F32 = mybir.dt.float32
BF16 = mybir.dt.bfloat16
ALU = mybir.AluOpType
AF = mybir.ActivationFunctionType
AX = mybir.AxisListType

S = 4096
D = 1024
DFF = 2816
NFC = DFF // 128
DIN = 2840
HD = 64
EPS = 1e-6
BIG = 32768.0
ROFF = 4096
RLEN = 8704


class Res:
    __slots__ = ("name", "lw", "rd")

    def __init__(self, name):
        self.name = name
        self.lw = None
        self.rd = {}


def _flat(x):
    out = []
    for r in x:
        if r is None:
            continue
        if isinstance(r, (list, tuple)):
            out.extend(_flat(r))
        else:
            out.append(r)
    return out


def Res2(*a):
    return list(a)


class Sched:
    ENGS = ("pe", "act", "dve", "pool", "sp")

    def __init__(self, nc, tag, ndma=6):
        self.nc = nc
        self.prog = {e: [] for e in self.ENGS}
        self.cnt = {e: 0 for e in self.ENGS}
        self.sem = {e: nc.alloc_semaphore(name=f"{tag}_{e}") for e in self.ENGS}
        self.dq = {}
        for q, n in (("sp", 4), ("pool", 3)):
            self.dq[q] = [[nc.alloc_semaphore(name=f"{tag}_d{q}{i}"), 0] for i in range(n)]
        self.dqi = {q: 0 for q in self.dq}
        self.waited = {}
        self.all_dma = {}

    def _key(self, tok):
        return (tok[0], tok[1] if tok[0] == "eng" else id(tok[1]))

    def _wait(self, eng, tok):
        if tok is None:
            return
        if tok[0] == "eng":
            if eng == "pe" and tok[1] == "pe":
                return
            k = ("e", tok[1])
            sem = self.sem[tok[1]]
        else:
            k = ("d", id(tok[1]))
            sem = tok[1]
        kk = (eng, k)
        if self.waited.get(kk, 0) >= tok[2]:
            return
        self.waited[kk] = tok[2]
        self.prog[eng].append(("w", sem, tok[2]))

    def _deps(self, eng, reads, writes):
        for r in reads:
            self._wait(eng, r.lw)
        for r in writes:
            self._wait(eng, r.lw)
            for t in r.rd.values():
                self._wait(eng, t)

    def _commit(self, tok, reads, writes):
        kk = self._key(tok)
        for r in reads:
            r.rd[kk] = tok
        for r in writes:
            r.lw = tok
            r.rd = {}

    def op(self, eng, fn, reads=(), writes=()):
        reads, writes = _flat(reads), _flat(writes)
        self._deps(eng, reads, writes)
        self.cnt[eng] += 1
        tok = ("eng", eng, self.cnt[eng])
        self.prog[eng].append(("op", fn))
        self._commit(tok, reads, writes)

    def dma(self, q, out, in_, reads=(), writes=(), **kw):
        reads, writes = _flat(reads), _flat(writes)
        self._deps(q, reads, writes)
        pool = self.dq[q]
        ent = pool[self.dqi[q] % len(pool)]
        self.dqi[q] += 1
        if ent[1] > 0:
            self._wait(q, ("dma", ent[0], ent[1]))
        ent[1] += 16
        tok = ("dma", ent[0], ent[1])
        self.all_dma[id(ent[0])] = tok
        self.prog[q].append(("dma", out, in_, kw, ent[0]))
        self._commit(tok, reads, writes)

    def emit(self):
        nc = self.nc
        for e in self.ENGS:
            for e2 in self.ENGS:
                if e2 != e and self.cnt[e2] > 0:
                    self._wait(e, ("eng", e2, self.cnt[e2]))
            for tok in self.all_dma.values():
                self._wait(e, tok)
        prog, sem = self.prog, self.sem

        def run(ename, eng):
            s = sem[ename]
            for it in prog[ename]:
                if it[0] == "w":
                    eng.wait_ge(it[1], it[2])
                elif it[0] == "op":
                    it[1](eng).then_inc(s, 1)
                else:
                    eng.dma_start(out=it[1], in_=it[2], **it[3]).then_inc(it[4], 16)

        with nc.Block() as blk:
            @blk.tensor
            def _(e):
                run("pe", e)

            @blk.scalar
            def _(e):
                run("act", e)

            @blk.vector
            def _(e):
                run("dve", e)

            @blk.gpsimd
            def _(e):
                run("pool", e)

            @blk.sync
            def _(e):
                run("sp", e)


def dram_view(t, offset, dims):
    return bass.AP(t, offset, [list(d) for d in dims])


def emit_rmsnorm(sc, N, xt, r_x, gain, r_gain, out_t, r_out, ones32, r_const, sq, r_sq, ps, r_ps, rstd, r_rstd):
    for c in range(8):
        j = c % 2
        sc.op("act", lambda e, c=c, j=j: e.activation(out=sq[j][:, 0:N], in_=xt[:, c, 0:N], func=AF.Square),
              reads=[r_x], writes=[r_sq[j]])
        sc.op("pe", lambda e, c=c, j=j: e.matmul(ps[:, 0:N], ones32[:, :], sq[j][:, 0:N], start=(c == 0), stop=(c == 7)),
              reads=[r_sq[j], r_const], writes=[r_ps])
    sc.op("dve", lambda e: e.tensor_scalar(out=rstd[:, 0:N], in0=ps[:, 0:N], scalar1=1.0 / D, scalar2=EPS,
                                           op0=ALU.mult, op1=ALU.add), reads=[r_ps], writes=[r_rstd])
    sc.op("act", lambda e: e.activation(out=rstd[:, 0:N], in_=rstd[:, 0:N], func=AF.Sqrt), reads=[r_rstd], writes=[r_rstd])
    sc.op("dve", lambda e: e.reciprocal(out=rstd[:, 0:N], in_=rstd[:, 0:N]), reads=[r_rstd], writes=[r_rstd])
    for c in range(8):
        sc.op("dve", lambda e, c=c: e.scalar_tensor_tensor(out=out_t[:, c, 0:N], in0=xt[:, c, 0:N], scalar=gain[:, c:c + 1],
                                                          in1=rstd[:, 0:N], op0=ALU.mult, op1=ALU.mult),
              reads=[r_x, r_rstd, r_gain], writes=[r_out])


def ffn_phase(nc, tag, x_in, x_out, g_pre, wg, wu, wd, g_post, post_out, post_dt, ntok=S, TN=256):
    sc = Sched(nc, tag)
    r_xin, r_xout, r_post = Res("xin"), Res("xout"), Res("post")
    with ExitStack() as es:
        def sb(name, shape, dt):
            return es.enter_context(nc.sbuf_tensor(f"{tag}_{name}", shape, dt))

        def pst(name):
            return es.enter_context(nc.psum_tensor(f"{tag}_{name}", [128, 512], F32))

        wg_sb = sb("wg", [128, 8, DFF], BF16)
        wu_sb = sb("wu", [128, 8, DFF], BF16)
        wd_sb = sb("wd", [128, NFC, D], BF16)
        gpre = sb("gpre", [128, 8], F32)
        gpost = sb("gpost", [128, 8], F32)
        ones32 = sb("ones", [128, 128], F32)
        xt = [sb(f"xt{i}", [128, 8, TN], F32) for i in range(2)]
        ht = [sb(f"ht{i}", [128, 8, TN], BF16) for i in range(2)]
        at = sb("at", [128, NFC, TN], BF16)
        pt = sb("pt", [128, 8, TN], post_dt)
        sq = [sb(f"sq{i}", [128, TN], F32) for i in range(2)]
        sg = [sb(f"sg{i}", [128, TN], F32) for i in range(2)]
        rstd = sb("rstd", [128, TN], F32)
        ps_g = [pst(f"psg{i}") for i in range(2)]
        ps_u = [pst(f"psu{i}") for i in range(2)]
        ps_y = [pst(f"psy{i}") for i in range(2)]
        ps_n = pst("psn")

        r_const = Res("const")
        r_gain = Res("gain")
        r_xt = [Res("xt0"), Res("xt1")]
        r_ht = [Res("ht0"), Res("ht1")]
        r_at, r_pt, r_rstd = Res("at"), Res("pt"), Res("rstd")
        r_sq = [Res("sq0"), Res("sq1")]
        r_sg = [Res("sg0"), Res("sg1")]
        r_psg = [Res("psg0"), Res("psg1")]
        r_psu = [Res("psu0"), Res("psu1")]
        r_psy = [Res("psy0"), Res("psy1")]
        r_psn = Res("psn")

        sc.op("dve", lambda e: e.memset(ones32[:, :], 1.0), writes=[r_const])
        sc.dma("sp", gpre[:, :], g_pre.rearrange("(c p) -> p c", p=128), writes=[r_gain], allow_slow_non_contiguous=True)
        sc.dma("sp", gpost[:, :], g_post.rearrange("(c p) -> p c", p=128), writes=[r_gain], allow_slow_non_contiguous=True)
        r_wk1 = [Res(f"wk{k}") for k in range(8)]
        r_wk = [[r_wk1[k]] * NFC for k in range(8)]
        r_wd = [Res(f"wd{f}") for f in range(NFC)]
        for k in range(8):
            sc.dma("pool", wg_sb[:, k, :], wg[k * 128:(k + 1) * 128, :], writes=[r_wk1[k]])
            sc.dma("pool", wu_sb[:, k, :], wu[k * 128:(k + 1) * 128, :], writes=[r_wk1[k]])
        for f in range(NFC):
            sc.dma("pool", wd_sb[:, f, :], wd[f * 128:(f + 1) * 128, :], writes=[r_wd[f]])

        xin_v = x_in.rearrange("(c p) t -> p c t", p=128)
        xout_v = x_out.rearrange("(c p) t -> p c t", p=128) if x_out is not None else None
        post_v = post_out.rearrange("(c p) t -> p c t", p=128)
        ntile = ntok // TN

        def load(t):
            b = t % 2
            sc.dma("sp", xt[b][:, :, :], xin_v[:, :, t * TN:(t + 1) * TN], reads=[r_xin], writes=[r_xt[b]])

        def prenorm(t):
            b = t % 2
            emit_rmsnorm(sc, TN, xt[b], r_xt[b], gpre, r_gain, ht[b], r_ht[b], ones32, r_const, sq, r_sq, ps_n, r_psn, rstd, r_rstd)

        def gateup(t, f0, f1):
            b = t % 2
            H, rH = ht[b], r_ht[b]
            for f in range(f0, f1):
                j = f % 2
                for k in range(8):
                    sc.op("pe", lambda e, f=f, k=k, j=j, H=H: e.matmul(ps_g[j][:, 0:TN], wg_sb[:, k, f * 128:(f + 1) * 128], H[:, k, :],
                                                                      start=(k == 0), stop=(k == 7)),
                          reads=[r_wk[k][f], rH], writes=[r_psg[j]])
                for k in range(8):
                    sc.op("pe", lambda e, f=f, k=k, j=j, H=H: e.matmul(ps_u[j][:, 0:TN], wu_sb[:, k, f * 128:(f + 1) * 128], H[:, k, :],
                                                                      start=(k == 0), stop=(k == 7)),
                          reads=[r_wk[k][f], rH], writes=[r_psu[j]])
                sc.op("act", lambda e, j=j: e.activation(out=sg[j][:, :], in_=ps_g[j][:, 0:TN], func=AF.Silu),
                      reads=[r_psg[j]], writes=[r_sg[j]])
                sc.op("dve", lambda e, f=f, j=j: e.tensor_tensor(out=at[:, f, :], in0=sg[j][:, :], in1=ps_u[j][:, 0:TN], op=ALU.mult),
                      reads=[r_sg[j], r_psu[j]], writes=[r_at])

        def down(t):
            b = t % 2
            X, rX = xt[b], r_xt[b]
            for d in range(8):
                j = d % 2
                for f in range(NFC):
                    sc.op("pe", lambda e, f=f, d=d, j=j: e.matmul(ps_y[j][:, 0:TN], wd_sb[:, f, d * 128:(d + 1) * 128], at[:, f, :],
                                                                 start=(f == 0), stop=(f == NFC - 1)),
                          reads=[r_wd[f], r_at], writes=[r_psy[j]])
                sc.op("dve", lambda e, d=d, j=j, X=X: e.scalar_tensor_tensor(out=X[:, d, :], in0=ps_y[j][:, 0:TN], scalar=0.5,
                                                                          in1=X[:, d, :], op0=ALU.mult, op1=ALU.add),
                      reads=[r_psy[j], rX], writes=[rX])
            if xout_v is not None:
                sc.dma("sp", xout_v[:, :, t * TN:(t + 1) * TN], X[:, :, :], reads=[rX], writes=[r_xout])

        def postnorm(t):
            b = t % 2
            emit_rmsnorm(sc, TN, xt[b], r_xt[b], gpost, r_gain, pt, r_pt, ones32, r_const, sq, r_sq, ps_n, r_psn, rstd, r_rstd)
            sc.dma("sp", post_v[:, :, t * TN:(t + 1) * TN], pt[:, :, :], reads=[r_pt], writes=[r_post])

        load(0)
        prenorm(0)
        for t in range(ntile + 1):
            if t < ntile:
                gateup(t, 0, 4)
            if t > 0:
                postnorm(t - 1)
            if t + 1 < ntile:
                load(t + 1)
            if t < ntile:
                gateup(t, 4, NFC)
            if t + 1 < ntile:
                prenorm(t + 1)
            if t < ntile:
                down(t)
        sc.emit()


def _bucket(d):
    n = np.maximum(d, 0)
    nf = np.maximum(n, 1).astype(np.float32)
    large = 16 + (np.log(nf / np.float32(16)) / np.float32(math.log(128 / 16)) * np.float32(16)).astype(np.int32)
    return np.where(n < 16, n, np.minimum(large, 31)).astype(np.int64)


def _slc_overlap(n_cmp, n_slc):
    r, mc = 4, 2
    j = np.arange(n_slc)
    offs = (np.arange(r)[:, None] + np.arange(mc)[None, :]).reshape(-1)
    c = r * j[:, None] + offs[None, :]
    w = np.zeros((n_slc * r + mc, n_slc), np.float32)
    np.add.at(w, (c, np.broadcast_to(j[:, None], c.shape)), 1.0)
    return w[:n_cmp]


_CONSTS = None


def static_consts():
    global _CONSTS
    if _CONSTS is not None:
        return _CONSTS
    c = {}
    i = np.arange(RLEN)
    d = i - ROFF
    b = _bucket(d)
    okc = d >= 0
    okw = (d >= 0) & (d < 512)
    ohc = np.zeros((32, RLEN), np.float32)
    ohc[b[okc], i[okc]] = 1.0
    ohw = np.zeros((32, RLEN), np.float32)
    ohw[b[okw], i[okw]] = 1.0
    c["c_ohc"], c["c_ohw"] = ohc, ohw
    c["c_negc"] = np.broadcast_to(np.where(okc, 0.0, -BIG).astype(np.float32), (16, RLEN)).copy()
    c["c_negw"] = np.broadcast_to(np.where(okw, 0.0, -BIG).astype(np.float32), (16, RLEN)).copy()
    c["c_ident"] = np.eye(128, dtype=np.float32)
    c["c_jx"] = np.eye(128, dtype=np.float32)[::-1].copy()
    t = np.arange(S)
    cur = (t // 256)[:, None]
    n = np.arange(16)[None, :]
    def lay(a):
        return np.ascontiguousarray(a.reshape(32, 128, a.shape[-1]).transpose(1, 0, 2)).astype(np.float32)
    c["c_mneg"] = lay(np.where(n >= cur, -1e30, 0.0))
    c["c_mallow"] = lay((n < cur) * 1.0)
    c["c_mown"] = lay((n == cur) * 1.0)
    cur = (t // 64)[:, None]
    j = np.arange(64)[None, :]
    forced = (j == 0) | (j == cur) | (j == cur - 1)
    c["c_slc"] = lay(np.where(forced, 1e4, np.where(j <= cur, 0.0, -1e4)))
    k = np.arange(S)[None, :]
    c["c_kaugm"] = ((k // 256) == np.arange(16)[:, None]).astype(np.float32)
    c["c_kaugs"] = ((k // 64) == np.arange(64)[:, None]).astype(np.float32)
    ov = np.zeros((256, 65), np.float32)
    ov[:255, :64] = _slc_overlap(255, 64)
    ov[:255, 64] = 1.0
    c["c_ovl"] = ov
    _CONSTS = c
    return c


def const_phase(nc, tag, rel_bias, cst, Rc, Rw):
    sc = Sched(nc, tag)
    with ExitStack() as es:
        def sb(name, shape, dt):
            return es.enter_context(nc.sbuf_tensor(f"{tag}_{name}", shape, dt))
        tab = sb("tab", [32, 16], F32)
        oh = sb("oh", [32, RLEN], F32)
        ng = sb("ng", [16, RLEN], F32)
        rs = sb("rs", [16, RLEN], BF16)
        ps = [es.enter_context(nc.psum_tensor(f"{tag}_ps{i}", [128, 512], F32)) for i in range(2)]
        r_tab, r_oh, r_ng, r_rs, r_R = Res("tab"), Res("oh"), Res("ng"), Res("rs"), Res("R")
        r_ps = [Res("ps0"), Res("ps1")]
        sc.dma("sp", tab[:, :], rel_bias[:, :], writes=[r_tab])
        for ohd, ngd, Rd in ((cst["c_ohc"], cst["c_negc"], Rc), (cst["c_ohw"], cst["c_negw"], Rw)):
            sc.dma("sp", oh[:, :], ohd[:, :], writes=[r_oh])
            sc.dma("sp", ng[:, :], ngd[:, :], writes=[r_ng])
            for ch in range(RLEN // 512):
                j = ch % 2
                sl = slice(ch * 512, (ch + 1) * 512)
                sc.op("pe", lambda e, j=j, sl=sl: e.matmul(ps[j][0:16, :], tab[:, :], oh[:, sl], start=True, stop=True),
                      reads=[r_tab, r_oh], writes=[r_ps[j]])
                sc.op("dve", lambda e, j=j, sl=sl: e.scalar_tensor_tensor(out=rs[:, sl], in0=ps[j][0:16, :], scalar=8.0, in1=ng[:, sl],
                                                                          op0=ALU.mult, op1=ALU.add),
                      reads=[r_ps[j], r_ng], writes=[r_rs])
            sc.dma("sp", Rd[:, :], rs[:, :], reads=[r_rs], writes=[r_R])
        sc.emit()


FM_CHUNKS = [0, 1, 2, 3, 4, 5, 6, 7, 12, 13, 14, 15, 16, 17, 18, 20]
VT_GROUPS = [(1024, 512, 0), (2432, 128, 512), (2688, 128, 640)]


def inproj_phase(nc, tag, h2T, w_in, projT, Vtok, gT, ntok=S):
    sc = Sched(nc, tag)
    TN = 512
    with ExitStack() as es:
        def sb(name, shape, dt):
            return es.enter_context(nc.sbuf_tensor(f"{tag}_{name}", shape, dt))

        def pst(name):
            return es.enter_context(nc.psum_tensor(f"{tag}_{name}", [128, 512], F32))
        W = sb("w", [128, 8, DIN], BF16)
        ht = [sb(f"ht{i}", [128, 8, TN], BF16) for i in range(2)]
        stage = sb("stage", [128, 16, TN], BF16)
        gst = sb("gst", [24, TN], F32)
        vst = [sb(f"vst{i}", [128, 768], BF16) for i in range(2)]
        psA = [pst(f"psa{i}") for i in range(2)]
        psG = pst("psg")
        psV = [pst(f"psv{i}") for i in range(3)]
        r_w = [Res(f"w{k}") for k in range(8)]
        r_ht = [Res("ht0"), Res("ht1")]
        r_stage, r_gst = Res("stage"), Res("gst")
        r_vst = [Res("vst0"), Res("vst1")]
        r_psA = [Res("psa0"), Res("psa1")]
        r_psG = Res("psg")
        r_psV = [Res(f"psv{i}") for i in range(3)]
        r_in, r_proj, r_vt, r_g = Res("h2T"), Res("projT"), Res("Vtok"), Res("gT")
        for k in range(8):
            sc.dma("pool", W[:, k, :], w_in[k * 128:(k + 1) * 128, :], writes=[r_w[k]])
        hv = h2T.rearrange("(c p) t -> p c t", p=128)
        pv = projT.rearrange("(c p) t -> p c t", p=128)
        cp = 0
        for it in range(ntok // TN):
            t0 = it * TN
            H, rH = ht[it % 2], r_ht[it % 2]
            sc.dma("sp", H[:, :, :], hv[:, :, t0:t0 + TN], reads=[r_in], writes=[rH])
            for ci, c in enumerate(FM_CHUNKS):
                j = ci % 2
                for k in range(8):
                    sc.op("pe", lambda e, c=c, k=k, j=j, H=H: e.matmul(psA[j][:, :], W[:, k, c * 128:(c + 1) * 128], H[:, k, :],
                                                                      start=(k == 0), stop=(k == 7)),
                          reads=[r_w[k], rH], writes=[r_psA[j]])
                if ci % 2 == 0:
                    sc.op("act", lambda e, ci=ci, j=j: e.copy(out=stage[:, ci, :], in_=psA[j][:, :]), reads=[r_psA[j]], writes=[r_stage])
                else:
                    sc.op("dve", lambda e, ci=ci, j=j: e.tensor_copy(out=stage[:, ci, :], in_=psA[j][:, :]), reads=[r_psA[j]], writes=[r_stage])
            sc.dma("sp", pv[:, :, t0:t0 + TN], stage[:, :, :], reads=[r_stage], writes=[r_proj])
            for k in range(8):
                sc.op("pe", lambda e, k=k, H=H: e.matmul(psG[0:24, :], W[:, k, 2816:2840], H[:, k, :], start=(k == 0), stop=(k == 7)),
                      reads=[r_w[k], rH], writes=[r_psG])
            sc.op("act", lambda e: e.activation(out=gst[:, :], in_=psG[0:24, :], func=AF.Sigmoid), reads=[r_psG], writes=[r_gst])
            sc.dma("sp", gT[:, t0:t0 + TN], gst[:, :], reads=[r_gst], writes=[r_g])
            for s in range(4):
                vs, rvs = vst[s % 2], r_vst[s % 2]
                for gi, (c0, wd_, v0) in enumerate(VT_GROUPS):
                    for k in range(8):
                        sc.op("pe", lambda e, k=k, gi=gi, c0=c0, wd_=wd_, s=s, H=H: e.matmul(
                            psV[gi][:, 0:wd_], H[:, k, s * 128:(s + 1) * 128], W[:, k, c0:c0 + wd_], start=(k == 0), stop=(k == 7)),
                            reads=[r_w[k], rH], writes=[r_psV[gi]])
                    if gi == 0:
                        sc.op("act", lambda e, gi=gi, wd_=wd_, v0=v0, vs=vs: e.copy(out=vs[:, v0:v0 + wd_], in_=psV[gi][:, 0:wd_]),
                              reads=[r_psV[gi]], writes=[rvs])
                    else:
                        sc.op("dve", lambda e, gi=gi, wd_=wd_, v0=v0, vs=vs: e.tensor_copy(out=vs[:, v0:v0 + wd_], in_=psV[gi][:, 0:wd_]),
                              reads=[r_psV[gi]], writes=[rvs])
                tt = t0 + s * 128
                sc.dma("sp", Vtok[tt:tt + 128, :], vs[:, :], reads=[rvs], writes=[r_vt])
        sc.emit()


class AttnCtx:
    def __init__(self, nc, sc, es, tag):
        self.nc, self.sc = nc, sc

        def sb(name, shape, dt):
            return es.enter_context(nc.sbuf_tensor(f"{tag}_{name}", shape, dt))

        def pst(name):
            return es.enter_context(nc.psum_tensor(f"{tag}_{name}", [128, 512], F32))
        self.sb, self.pst = sb, pst
        self.ST = [pst(f"st{i}") for i in range(3)]
        self.rST = [Res(f"st{i}") for i in range(3)]
        self.ACC = [pst(f"acc{i}") for i in range(2)]
        self.rACC = [Res(f"acc{i}") for i in range(2)]
        self.BC = pst("bc")
        self.rBC = Res("bc")
        self.PT = [sb(f"pt{i}", [128, 512], BF16) for i in range(4)]
        self.rPT = [Res(f"pt{i}") for i in range(4)]
        self.rdb = [sb(f"rd{i}", [128, 512], F32) for i in range(2)]
        self.rRDb = [Res(f"rd{i}") for i in range(2)]
        self.rdi = 0
        self.rdh = [sb(f"rdh{i}", [128, 1024], BF16) for i in range(2)]
        self.rRDh = [Res(f"rdh{i}") for i in range(2)]
        self.sel64 = sb("sel64", [128, 128], BF16)
        self.osb = [sb(f"osb{i}", [64, 512], F32) for i in range(2)]
        self.rOSB = [Res(f"osb{i}") for i in range(2)]
        self.sti = self.pti = self.acci = self.osi = 0
        self.pending = []
        self.deferred = []
        self.ident = sb("ident", [128, 128], BF16)
        self.jx = sb("jx", [128, 128], BF16)
        self.ones32 = sb("ones32", [128, 64], F32)
        self.t31 = sb("t31", [128, 16], F32)
        self.rC = Res("const")

    def load_consts(self, cst, rel_bias):
        sc = self.sc
        sc.dma("pool", self.ident[:, :], cst["c_ident"][:, :], writes=[self.rC])
        sc.dma("pool", self.jx[:, :], cst["c_jx"][:, :], writes=[self.rC])
        sc.op("dve", lambda e: e.memset(self.ones32[:, :], 1.0), writes=[self.rC])
        sc.op("dve", lambda e: e.memset(self.sel64[:, :], 0.0), writes=[self.rC])
        sc.op("dve", lambda e: e.memset(self.sel64[64:65, :], 1.0), writes=[self.rC])
        for i in range(2):
            sc.op("dve", lambda e, i=i: e.memset(self.rdh[i][:, :], 0.0), writes=[self.rRDh[i]])
        sc.dma("sp", self.t31[:, :], dram_view(rel_bias.tensor, 31 * 16, [[0, 128], [1, 16]]), writes=[self.rC])

    LOOKAHEAD = 2

    def tile(self, lhsT, rK, rhs, rQ, bias_tile, rB, head, vaug, rV, acc, racc, first, last, extra=None, after=None, cols=(0, 512)):
        sc = self.sc
        st, rst = self.ST[self.sti % 3], self.rST[self.sti % 3]
        self.sti += 1
        pt, rpt = self.PT[self.pti % 4], self.rPT[self.pti % 4]
        self.pti += 1
        near = bias_tile is not None
        lo, hi = cols
        assert not (first and (lo, hi) != (0, 512))
        sc.op("pe", lambda e: e.matmul(st[:, lo:hi], lhsT, rhs[:, lo:hi], start=True, stop=not near), reads=[rK, rQ], writes=[rst])
        if near:
            sc.op("pe", lambda e: e.matmul(st[:, lo:hi], self.jx[:, :], bias_tile[:, lo:hi], start=False, stop=True), reads=[rB, self.rC], writes=[rst])
            sc.op("act", lambda e: e.activation(out=pt[:, lo:hi], in_=st[:, lo:hi], func=AF.Exp, scale=0.125), reads=[rst], writes=[rpt])
        else:
            sc.op("act", lambda e: e.activation(out=pt[:, lo:hi], in_=st[:, lo:hi], func=AF.Exp, scale=0.125, bias=self.t31[:, head:head + 1]),
                  reads=[rst, self.rC], writes=[rpt])

        def pv():
            sc.op("pe", lambda e: e.matmul(acc[0:65, lo:hi], vaug, pt[:, lo:hi], start=first, stop=last), reads=[rV, rpt], writes=[racc])
            if extra is not None:
                extra(pt, rpt)
            if after is not None:
                after()
        self.pending.append(pv)
        while len(self.pending) > self.LOOKAHEAD:
            self.pending.pop(0)()
        for d in self.deferred:
            d[0] -= 1
        while self.deferred and self.deferred[0][0] <= 0:
            self.deferred.pop(0)[1]()

    def flush(self):
        while self.pending:
            self.pending.pop(0)()
        while self.deferred:
            self.deferred.pop(0)[1]()

    def next_acc(self):
        a, r = self.ACC[self.acci % 2], self.rACC[self.acci % 2]
        self.acci += 1
        return a, r

    def finish(self, acc, racc, gate_row, rG, out_ap, rOut, mode, add_ap=None, rAdd=None, split=True, defer=4):
        sc = self.sc
        rd, rRD = self.rdb[self.rdi % 2], self.rRDb[self.rdi % 2]
        rh, rRH = self.rdh[self.rdi % 2], self.rRDh[self.rdi % 2]
        self.rdi += 1
        sc.op("dve", lambda e: e.tensor_scalar(out=rd[64:65, :], in0=acc[64:65, :], scalar1=1e-30, scalar2=None, op0=ALU.max),
              reads=[racc], writes=[rRD])
        sc.op("dve", lambda e: e.reciprocal(out=rd[64:65, :], in_=rd[64:65, :]), reads=[rRD], writes=[rRD])
        if gate_row is not None:
            sc.op("dve", lambda e: e.tensor_tensor(out=rd[64:65, :], in0=rd[64:65, :], in1=gate_row, op=ALU.mult),
                  reads=[rRD, rG], writes=[rRD])
        sc.op("dve", lambda e: e.tensor_copy(out=rh[64:65, 0:512], in_=rd[64:65, :]), reads=[rRD], writes=[rRH])
        sc.op("dve", lambda e: e.tensor_tensor(out=rh[64:65, 512:1024], in0=rd[64:65, :], in1=rh[64:65, 0:512], op=ALU.subtract),
              reads=[rRD, rRH], writes=[rRH])

        def part_b():
            sc.op("pe", lambda e: e.matmul(self.BC[:, :], self.sel64[:, :], rh[:, 0:512], start=True, stop=False),
                  reads=[rRH, self.rC], writes=[self.rBC])
            sc.op("pe", lambda e: e.matmul(self.BC[:, :], self.sel64[:, :], rh[:, 512:1024], start=False, stop=True),
                  reads=[rRH, self.rC], writes=[self.rBC])
            osb, rosb = self.osb[self.osi % 2], self.rOSB[self.osi % 2]
            self.osi += 1
            sc.op("act", lambda e: e.copy(out=osb[:, :], in_=acc[0:64, :]), reads=[racc], writes=[rosb])
            if mode == "set":
                sc.op("dve", lambda e: e.tensor_tensor(out=out_ap, in0=osb[:, :], in1=self.BC[0:64, :], op=ALU.mult),
                      reads=[rosb, self.rBC], writes=[rOut])
            else:
                sc.op("dve", lambda e: e.tensor_tensor(out=osb[:, :], in0=osb[:, :], in1=self.BC[0:64, :], op=ALU.mult),
                      reads=[rosb, self.rBC], writes=[rosb])
                sc.op("dve", lambda e: e.tensor_tensor(out=out_ap, in0=osb[:, :], in1=add_ap, op=ALU.add),
                      reads=[rosb, rAdd], writes=[rOut])
        if split:
            self.deferred.append([defer, part_b])
        else:
            part_b()


def rvec(R, h, off, pstride):
    return dram_view(R.tensor, h * RLEN + off, [[pstride, 128], [1, 512]])


def moba_phase(nc, tag, projT, Vtok, Rc, rel_bias, cst, OT_all, heads=range(8), nqt=8):
    sc = Sched(nc, tag)
    with ExitStack() as es:
        A = AttnCtx(nc, sc, es, tag)
        sb, pst = A.sb, A.pst
        A.load_consts(cst, rel_bias)
        QA = [sb(f"qa{i}", [128, S], BF16) for i in range(2)]
        KA = [sb(f"ka{i}", [128, S], BF16) for i in range(2)]
        VA = [sb(f"va{i}", [128, 32, 65], BF16) for i in range(2)]
        BT = [sb(f"bt{i}", [128, 5, 512], BF16) for i in range(2)]
        OTs = [sb(f"ots{i}", [64, S], BF16) for i in range(2)]
        cneg = sb("cneg", [128, 32, 16], F32)
        callow = sb("callow", [128, 32, 16], F32)
        cown = sb("cown", [128, 32, 16], F32)
        km32 = sb("km32", [64, 16], F32)
        kmT = sb("kmT", [64, 16], BF16)
        gmA = sb("gmA", [128, 32, 16], F32)
        g2A = sb("g2A", [128, 32, 16], F32)
        eqA = sb("eqA", [128, 32, 16], F32)
        mx = sb("mx", [128, 32], F32)
        mpadA = sb("mpadA", [128, 32, 80], BF16)
        psG = pst("psg")
        psT = pst("pst")
        rQq = [Res("qq0"), Res("qq1")]
        rQm = [Res("qm0"), Res("qm1")]
        rK = [Res("k0"), Res("k1")]
        rKaug = [Res("kaug0"), Res("kaug1")]
        rV = [Res("v0"), Res("v1")]
        rVone = [Res("vone0"), Res("vone1")]
        rBT = [[Res(f"bt{b}_{i}") for i in range(5)] for b in range(2)]
        rOTs = [Res("ots0"), Res("ots1")]
        rGc, rkm32, rkm, rgm, rtop8, rsel, rmpad, rmpad0, rg2 = (Res(n) for n in ("gc", "km32", "km", "gm", "top8", "sel", "mpad", "mpad0", "g2"))
        rpsG, rpsT = Res("psg"), Res("pst")
        r_proj, r_vt, r_R, r_ot = Res("projT"), Res("Vtok"), Res("Rc"), Res("OT")
        sc.dma("sp", cneg[:, :, :], cst["c_mneg"][:, :, :], writes=[rGc])
        sc.dma("sp", callow[:, :, :], cst["c_mallow"][:, :, :], writes=[rGc])
        sc.dma("sp", cown[:, :, :], cst["c_mown"][:, :, :], writes=[rGc])
        sc.op("dve", lambda e: e.memset(mpadA[:, :, :], 0.0), writes=[rmpad0])
        for i in range(2):
            sc.op("pool", lambda e, i=i: e.memset(VA[i][:, :, 64:65], 1.0), writes=[rVone[i]])
            sc.dma("pool", KA[i][64:80, :], cst["c_kaugm"][:, :], writes=[rKaug[i]])
        vtv = Vtok.rearrange("(t p) c -> p t c", p=128)
        heads = list(heads)

        def load_head(h):
            hb = h % 2
            qrow = (h // 2) * 128 + (h % 2) * 64
            krow = (4 + h // 2) * 128 + (h % 2) * 64
            sc.dma("sp", QA[hb][0:64, :], projT[qrow:qrow + 64, :], reads=[r_proj], writes=[rQq[hb]])
            sc.dma("sp", KA[hb][0:64, :], projT[krow:krow + 64, :], reads=[r_proj], writes=[rK[hb]])
            sc.dma("sp", VA[hb][:, :, 0:64], vtv[:, :, h * 64:(h + 1) * 64], reads=[r_vt], writes=[rV[hb]], allow_slow_non_contiguous=True)
            for i in range(5):
                rel = i - 1
                sc.dma("sp", BT[hb][:, i, :], rvec(Rc, h, ROFF - 128 * rel - 127, 1), reads=[r_R], writes=[rBT[hb][i]])

        def gating(h):
            hb = h % 2
            sc.op("dve", lambda e, hb=hb: e.tensor_reduce(out=km32[:, :], in_=KA[hb][0:64, :].rearrange("p (n b) -> p n b", b=256),
                                                          axis=AX.X, op=ALU.add), reads=[rK[hb]], writes=[rkm32])
            sc.op("dve", lambda e: e.tensor_scalar(out=kmT[:, :], in0=km32[:, :], scalar1=1.0 / 256, scalar2=None, op0=ALU.mult),
                  reads=[rkm32], writes=[rkm])
            for i in range(4 * nqt):
                sc.op("pe", lambda e, hb=hb, i=i: e.matmul(psG[:, i * 16:(i + 1) * 16], QA[hb][0:64, i * 128:(i + 1) * 128], kmT[:, :],
                                                          start=True, stop=True), reads=[rQq[hb], rkm], writes=[rpsG])
            nq = 4 * nqt
            psGv = psG[:, 0:nq * 16].rearrange("p (s n) -> p s n", n=16)
            mb = mx[:, 0:nq].to_broadcast([128, nq, 16])
            sc.op("dve", lambda e: e.tensor_tensor(out=gmA[:, 0:nq, :], in0=psGv, in1=cneg[:, 0:nq, :], op=ALU.add), reads=[rpsG, rGc], writes=[rgm])
            src, rsrc = gmA, rgm
            for rnd in range(2):
                sc.op("dve", lambda e, src=src: e.tensor_reduce(out=mx[:, 0:nq], in_=src[:, 0:nq, :], axis=AX.X, op=ALU.max), reads=[rsrc], writes=[rtop8])
                sc.op("dve", lambda e, src=src: e.tensor_tensor(out=eqA[:, 0:nq, :], in0=src[:, 0:nq, :], in1=mb, op=ALU.is_ge),
                      reads=[rsrc, rtop8], writes=[rsel])
                sc.op("dve", lambda e, src=src: e.scalar_tensor_tensor(out=g2A[:, 0:nq, :], in0=eqA[:, 0:nq, :], scalar=-1e32, in1=src[:, 0:nq, :],
                                                                      op0=ALU.mult, op1=ALU.add), reads=[rsel, rsrc], writes=[rg2])
                src, rsrc = g2A, rg2
            sc.op("dve", lambda e: e.tensor_reduce(out=mx[:, 0:nq], in_=g2A[:, 0:nq, :], axis=AX.X, op=ALU.max), reads=[rg2], writes=[rtop8])
            sc.op("dve", lambda e: e.tensor_tensor(out=eqA[:, 0:nq, :], in0=gmA[:, 0:nq, :], in1=mb, op=ALU.is_ge), reads=[rgm, rtop8], writes=[rsel])
            sc.op("dve", lambda e: e.tensor_tensor(out=eqA[:, 0:nq, :], in0=eqA[:, 0:nq, :], in1=callow[:, 0:nq, :], op=ALU.mult),
                  reads=[rsel, rGc], writes=[rsel])
            sc.op("dve", lambda e: e.tensor_tensor(out=eqA[:, 0:nq, :], in0=eqA[:, 0:nq, :], in1=cown[:, 0:nq, :], op=ALU.add),
                  reads=[rsel, rGc], writes=[rsel])
            sc.op("dve", lambda e: e.tensor_scalar(out=mpadA[:, 0:nq, 64:80], in0=eqA[:, 0:nq, :], scalar1=BIG, scalar2=-BIG, op0=ALU.mult, op1=ALU.add),
                  reads=[rsel, rmpad0], writes=[rmpad])
            for i4 in range(nqt):
                for s_ in range(4):
                    sc.op("pe", lambda e, s_=s_, i4=i4: e.matmul(psT[0:80, s_ * 128:(s_ + 1) * 128], mpadA[:, i4 * 4 + s_, :], A.ident[:, :], start=True, stop=True),
                          reads=[rmpad, A.rC], writes=[rpsT])
                sc.op("act", lambda e, hb=hb, i4=i4: e.copy(out=QA[hb][64:80, i4 * 512:(i4 + 1) * 512], in_=psT[64:80, :]),
                      reads=[rpsT], writes=[rQm[hb]])

        load_head(heads[0])
        for hi, h in enumerate(heads):
            hb = h % 2
            if hi + 1 < len(heads):
                load_head(heads[hi + 1])
            if hi == 0:
                gating(h)
            for qt in range(nqt):
                acc, racc = A.next_acc()
                kts = list(range(0, 4 * qt + 4))
                for idx, kt in enumerate(kts):
                    near = kt >= 4 * qt - 1
                    lastt = idx == len(kts) - 1
                    fin = None
                    if lastt:
                        def fin(acc=acc, racc=racc, hb=hb, qt=qt):
                            A.finish(acc, racc, None, None, OTs[hb][:, qt * 512:(qt + 1) * 512], rOTs[hb], "set", defer=min(10, 4 * qt + 8))
                    A.tile(KA[hb][0:80, kt * 128:(kt + 1) * 128], Res2(rK[hb], rKaug[hb]), QA[hb][0:80, qt * 512:(qt + 1) * 512], Res2(rQq[hb], rQm[hb]),
                           BT[hb][:, kt - (4 * qt - 1), :] if near else None, rBT[hb][kt - (4 * qt - 1)] if near else None, h, VA[hb][:, kt, :], Res2(rV[hb], rVone[hb]),
                           acc, racc, idx == 0, lastt, after=fin, cols=(max(0, 128 * (kt - 4 * qt)), 512))
                if qt == min(5, nqt - 1) and hi + 1 < len(heads):
                    gating(heads[hi + 1])
            A.flush()
            sc.dma("sp", OT_all[h * 64:(h + 1) * 64, :], OTs[hb][:, :], reads=[rOTs[hb]], writes=[r_ot])
        sc.emit()


def compress_phase(nc, tag, projT, pos_k, w1_k, w2_k, pos_v, w1_v, w2_v, kcmpT, vcmp):
    sc = Sched(nc, tag)
    with ExitStack() as es:
        def sb(name, shape, dt):
            return es.enter_context(nc.sbuf_tensor(f"{tag}_{name}", shape, dt))

        def pst(name):
            return es.enter_context(nc.psum_tensor(f"{tag}_{name}", [128, 512], F32))
        w1 = sb("w1", [64, 32, 256], BF16)
        w2 = sb("w2", [128, 2, 64], BF16)
        posT = sb("posT", [64, 32], BF16)
        cb = sb("cb", [128, 2], F32)
        raw = sb("raw", [64, S], BF16)
        hid = sb("hid", [128, 2, 256], BF16)
        kc = sb("kc", [64, 256], BF16)
        vc = sb("vc", [128, 2, 64], BF16)
        psH = [pst(f"psh{i}") for i in range(2)]
        psC = pst("psc")
        ps2 = pst("ps2")
        r_w1, r_w2, r_pos, r_cb, r_raw, r_hid, r_kc, r_vc = (Res(n) for n in ("w1", "w2", "pos", "cb", "raw", "hid", "kc", "vc"))
        r_psH = [Res("psh0"), Res("psh1")]
        r_psC, r_ps2 = Res("psc"), Res("ps2")
        r_proj, r_out = Res("projT"), Res("out")
        sc.op("dve", lambda e: e.memset(kc[:, :], 0.0), writes=[r_kc])
        sc.op("dve", lambda e: e.memset(vc[:, :, :], 0.0), writes=[r_vc])
        sc.op("dve", lambda e: e.memset(hid[:, :, :], 0.0), writes=[r_hid])
        for kv, (pos, w1d, w2d, chunk) in enumerate(((pos_k, w1_k, w2_k, 12), (pos_v, w1_v, w2_v, 13))):
            sc.dma("pool", w1[:, :, :], w1d.rearrange("(l d) j -> d l j", d=64), writes=[r_w1])
            sc.dma("pool", w2[:, :, :], w2d.rearrange("(c p) d -> p c d", p=128), writes=[r_w2])
            sc.dma("pool", posT[:, :], pos.rearrange("l d -> d l"), writes=[r_pos], allow_slow_non_contiguous=True)
            for jc in range(2):
                for l in range(32):
                    sc.op("pe", lambda e, l=l, jc=jc: e.matmul(psC[:, 0:1], w1[:, l, jc * 128:(jc + 1) * 128], posT[:, l:l + 1],
                                                             start=(l == 0), stop=(l == 31)), reads=[r_w1, r_pos], writes=[r_psC])
                sc.op("act", lambda e, jc=jc: e.copy(out=cb[:, jc:jc + 1], in_=psC[:, 0:1]), reads=[r_psC], writes=[r_cb])
            for g in range(2):
                row = chunk * 128 + g * 64
                sc.dma("sp", raw[:, :], projT[row:row + 64, :], reads=[r_proj], writes=[r_raw])
                rv = raw[:, :].rearrange("p (n s) -> p n s", s=16)
                for jc in range(2):
                    for l in range(32):
                        rhs = rv[:, 0:255, l] if l < 16 else rv[:, 1:256, l - 16]
                        sc.op("pe", lambda e, l=l, jc=jc, rhs=rhs: e.matmul(psH[jc][:, 0:255], w1[:, l, jc * 128:(jc + 1) * 128], rhs,
                                                                          start=(l == 0), stop=(l == 31)), reads=[r_w1, r_raw], writes=[r_psH[jc]])
                    sc.op("act", lambda e, jc=jc: e.activation(out=hid[:, jc, 0:255], in_=psH[jc][:, 0:255], func=AF.Silu, bias=cb[:, jc:jc + 1]),
                          reads=[r_psH[jc], r_cb], writes=[r_hid])
                if kv == 0:
                    for jc in range(2):
                        sc.op("pe", lambda e, jc=jc: e.matmul(ps2[0:64, 0:255], w2[:, jc, :], hid[:, jc, 0:255], start=(jc == 0), stop=(jc == 1)),
                              reads=[r_w2, r_hid], writes=[r_ps2])
                    sc.op("dve", lambda e: e.tensor_copy(out=kc[:, 0:255], in_=ps2[0:64, 0:255]), reads=[r_ps2], writes=[r_kc])
                    sc.dma("sp", kcmpT[g * 64:(g + 1) * 64, :], kc[:, :], reads=[r_kc], writes=[r_out])
                else:
                    for c in range(2):
                        for jc in range(2):
                            sc.op("pe", lambda e, jc=jc, c=c: e.matmul(ps2[:, c * 64:(c + 1) * 64], hid[:, jc, c * 128:(c + 1) * 128], w2[:, jc, :],
                                                                     start=(jc == 0), stop=(jc == 1)), reads=[r_w2, r_hid], writes=[r_ps2])
                    sc.op("dve", lambda e: e.tensor_copy(out=vc[:, :, :], in_=ps2[:, 0:128].rearrange("p (c d) -> p c d", d=64)),
                          reads=[r_ps2], writes=[r_vc])
                    sc.dma("sp", vcmp[g * 256:(g + 1) * 256, :].rearrange("(c p) d -> p c d", p=128), vc[:, :, :], reads=[r_vc], writes=[r_out])
        sc.emit()


def nsa_phase(nc, tag, projT, Vtok, gT, kcmpT, vcmp, Rc, Rw, rel_bias, cst, OT_all, groups=range(2), nqt=8, parts=('cmp', 'sel', 'sw'), selv=2):
    sc = Sched(nc, tag)
    with ExitStack() as es:
        A = AttnCtx(nc, sc, es, tag)
        sb, pst = A.sb, A.pst
        A.load_consts(cst, rel_bias)
        QN = [sb(f"qn{i}", [128, S], BF16) for i in range(4)]
        KS = sb("ks", [128, S], BF16)
        KW = sb("kw", [128, S], BF16)
        VS = sb("vs", [128, 32, 65], BF16)
        VW = sb("vw", [128, 32, 65], BF16)
        KC = sb("kc", [128, 256], BF16)
        VC = sb("vc", [128, 2, 65], BF16)
        OTc = [sb(f"otc{i}", [64, S], BF16) for i in range(4)]
        OTs = [sb(f"ots{i}", [64, S], BF16) for i in range(2)]
        OTh = sb("oth", [64, 512], F32)
        BT = [sb(f"bt{i}", [128, 13, 512], BF16) for i in range(2)]
        CB = [sb(f"cbt{i}", [128, 512], BF16) for i in range(2)]
        G = [sb(f"g{i}", [128, 3, 512], F32) for i in range(2)]
        imp = sb("imp", [128, 32, 64], F32)
        cslc = sb("cslc", [128, 32, 64], F32)
        ovl = sb("ovl", [128, 2, 65], BF16)
        rdi = sb("rdi", [128, 4], F32)
        scr = sb("scr", [128, 4, 64], F32)
        wk = sb("wk", [128, 4, 64], F32)
        t8a = sb("t8a", [128, 4, 8], F32)
        t8b = sb("t8b", [128, 4, 8], F32)
        mpad = sb("mpad", [128, 4, 128], BF16)
        psI = pst("psi")
        psT = pst("pst")
        rQq = [Res(f"qq{i}") for i in range(4)]
        rQm = [Res(f"qm{i}") for i in range(4)]
        rKS, rKSaug, rKW, rVS, rVW, rVone, rKC, rVC = (Res(n) for n in ("ks", "ksaug", "kw", "vs", "vw", "vone", "kc", "vc"))
        rOTc = [Res(f"otc{i}") for i in range(4)]
        rOTs = [Res("ots0"), Res("ots1")]
        rOTh = Res("oth")
        rBT = [[Res(f"bt{b}_{i}") for i in range(13)] for b in range(2)]
        rCB = [Res("cb0"), Res("cb1")]
        rG = [Res("g0"), Res("g1")]
        rimp, rC2, rrdi, rscr, rwk, rt8a, rt8b, rmpad, rmpad0 = (Res(n) for n in ("imp", "c2", "rdi", "scr", "wk", "t8a", "t8b", "mpad", "mpad0"))
        rpsI, rpsT = Res("psi"), Res("pst")
        r_proj, r_vt, r_g, r_kc, r_vc, r_R, r_ot = (Res(n) for n in ("projT", "Vtok", "gT", "kcmp", "vcmp", "R", "OT"))
        sc.dma("sp", cslc[:, :, :], cst["c_slc"][:, :, :], writes=[rC2])
        sc.dma("pool", ovl[:, :, :], cst["c_ovl"].rearrange("(c p) j -> p c j", p=128), writes=[rC2])
        sc.dma("pool", KS[64:128, :], cst["c_kaugs"][:, :], writes=[rKSaug])
        sc.op("dve", lambda e: e.memset(mpad[:, :, :], 0.0), writes=[rmpad0])
        sc.op("dve", lambda e: e.memset(imp[:, :, :], 0.0), writes=[rimp])
        rKz = Res("kzero")
        sc.op("pool", lambda e: e.memset(KW[64:128, :], 0.0), writes=[rKz])
        sc.op("pool", lambda e: e.memset(KC[64:128, :], 0.0), writes=[rKz])
        for r in range(4):
            sc.op("pool", lambda e, r=r: e.memset(QN[r][64:128, :], 0.0), writes=[rQm[r]])
        sc.op("pool", lambda e: e.memset(VS[:, :, 64:65], 1.0), writes=[rVone])
        sc.op("pool", lambda e: e.memset(VW[:, :, 64:65], 1.0), writes=[rVone])
        sc.op("pool", lambda e: e.memset(VC[:, :, 64:65], 1.0), writes=[rVone])
        vtv = Vtok.rearrange("(t p) c -> p t c", p=128)
        gi = [0]

        def load_gates(hb, qt):
            j = gi[0] % 2
            gi[0] += 1
            sc.dma("sp", G[j][64:65, :, :], dram_view(gT.tensor, (hb * 3) * S + qt * 512, [[0, 1], [S, 3], [1, 512]]),
                   reads=[r_g], writes=[rG[j]])
            return G[j], rG[j]

        def load_bt(hb):
            hg, b2 = 8 + hb, hb % 2
            for i in range(5):
                sc.dma("sp", BT[b2][:, i, :], rvec(Rc, hg, ROFF - 128 * (i - 1) - 127, 1), reads=[r_R], writes=[rBT[b2][i]])
            for i in range(8):
                sc.dma("sp", BT[b2][:, 5 + i, :], rvec(Rw, hg, ROFF - 128 * (i - 4) - 127, 1), reads=[r_R], writes=[rBT[b2][5 + i]])

        for g in groups:
            sc.dma("sp", KS[0:64, :], projT[14 * 128 + g * 64:14 * 128 + g * 64 + 64, :], reads=[r_proj], writes=[rKS])
            sc.dma("sp", KW[0:64, :], projT[15 * 128 + g * 64:15 * 128 + g * 64 + 64, :], reads=[r_proj], writes=[rKW])
            sc.dma("sp", VS[:, :, 0:64], vtv[:, :, 512 + g * 64:512 + (g + 1) * 64], reads=[r_vt], writes=[rVS], allow_slow_non_contiguous=True)
            sc.dma("sp", VW[:, :, 0:64], vtv[:, :, 640 + g * 64:640 + (g + 1) * 64], reads=[r_vt], writes=[rVW], allow_slow_non_contiguous=True)
            sc.dma("sp", KC[0:64, :], kcmpT[g * 64:(g + 1) * 64, :], reads=[r_kc], writes=[rKC])
            sc.dma("sp", VC[:, :, 0:64], vcmp[g * 256:(g + 1) * 256, :].rearrange("(c p) d -> p c d", p=128), reads=[r_vc], writes=[rVC],
                   allow_slow_non_contiguous=True)
            for r in range(4):
                hb = 4 * g + r
                qrow = (8 + hb // 2) * 128 + (hb % 2) * 64
                sc.dma("sp", QN[r][0:64, :], projT[qrow:qrow + 64, :], reads=[r_proj], writes=[rQq[r]])
            if 'sw' in parts:
                load_bt(4 * g)
            cbi = 0
            for r in (range(4) if 'cmp' in parts else []):
                hb = 4 * g + r
                hg = 8 + hb
                for qt in range(nqt):
                    Gt, rGt = load_gates(hb, qt)
                    acc, racc = A.next_acc()
                    chunks = [0] if qt <= 3 else [0, 1]
                    for ci, c in enumerate(chunks):
                        needb = (c == 1) or (qt <= 4)
                        bt, rbt = None, None
                        if needb:
                            bt, rbt = CB[cbi % 2], rCB[cbi % 2]
                            cbi += 1
                            sc.dma("sp", bt[:, :], rvec(Rc, hg, ROFF + 512 * qt - 2048 * c - 2063, 16), reads=[r_R], writes=[rbt])
                        first, last = ci == 0, ci == len(chunks) - 1

                        def extra(pt, rpt, c=c, first=first, last=last):
                            for s in range(4):
                                sc.op("pe", lambda e, s=s: e.matmul(psI[:, s * 65:(s + 1) * 65], pt[:, s * 128:(s + 1) * 128], ovl[:, c, :],
                                                                    start=(first and s == 0), stop=last, skip_group_check=True),
                                      reads=[rpt, rC2], writes=[rpsI])
                        A.tile(KC[0:128, c * 128:(c + 1) * 128], Res2(rKC, rKz), QN[r][0:128, qt * 512:(qt + 1) * 512], Res2(rQq[r], rQm[r]),
                               bt[:, :] if needb else None, rbt, hg, VC[:, c, :], Res2(rVC, rVone), acc, racc, first, last, extra=extra)
                    A.flush()
                    A.finish(acc, racc, Gt[64:65, 0, :], rGt, OTc[r][:, qt * 512:(qt + 1) * 512], rOTc[r], "set", split=True, defer=10 ** 6)
                    pv = psI[:, 0:260].rearrange("p (s j) -> p s j", j=65)
                    sc.op("dve", lambda e, pv=pv: e.tensor_scalar(out=rdi[:, :], in0=pv[:, :, 64], scalar1=1e-30, scalar2=None, op0=ALU.max),
                          reads=[rpsI], writes=[rrdi])
                    sc.op("dve", lambda e: e.reciprocal(out=rdi[:, :], in_=rdi[:, :]), reads=[rrdi], writes=[rrdi])
                    for s in range(4):
                        i = qt * 4 + s
                        if r == 0:
                            sc.op("dve", lambda e, s=s, i=i: e.tensor_scalar(out=imp[:, i, :], in0=psI[:, s * 65:s * 65 + 64], scalar1=rdi[:, s:s + 1],
                                                                          scalar2=None, op0=ALU.mult), reads=[rpsI, rrdi], writes=[rimp])
                        else:
                            sc.op("dve", lambda e, s=s, i=i: e.scalar_tensor_tensor(out=imp[:, i, :], in0=psI[:, s * 65:s * 65 + 64], scalar=rdi[:, s:s + 1],
                                                                                 in1=imp[:, i, :], op0=ALU.mult, op1=ALU.add),
                                  reads=[rpsI, rrdi, rimp], writes=[rimp])
            A.flush()
            for i4 in (range(nqt) if 'sel' in parts else []):
                sc.op("dve", lambda e, i4=i4: e.tensor_tensor(out=scr[:, :, :], in0=imp[:, i4 * 4:(i4 + 1) * 4, :], in1=cslc[:, i4 * 4:(i4 + 1) * 4, :],
                                                              op=ALU.add), reads=[rimp, rC2], writes=[rscr])
                for s in range(4):
                    sc.op("dve", lambda e, s=s: e.max(out=t8a[:, s, :], in_=scr[:, s, :]), reads=[rscr], writes=[rt8a])
                    sc.op("dve", lambda e, s=s: e.tensor_scalar(out=wk[:, s, :], in0=scr[:, s, :], scalar1=t8a[:, s, 7:8], scalar2=-6e4,
                                                                op0=ALU.is_ge, op1=ALU.mult), reads=[rscr, rt8a], writes=[rwk])
                    sc.op("dve", lambda e, s=s: e.tensor_tensor(out=wk[:, s, :], in0=wk[:, s, :], in1=scr[:, s, :], op=ALU.add),
                          reads=[rscr, rwk], writes=[rwk])
                    sc.op("dve", lambda e, s=s: e.max(out=t8b[:, s, :], in_=wk[:, s, :]), reads=[rwk], writes=[rt8b])
                    sc.op("dve", lambda e, s=s: e.tensor_scalar(out=wk[:, s, :], in0=scr[:, s, :], scalar1=t8b[:, s, 7:8], scalar2=BIG,
                                                                op0=ALU.is_ge, op1=ALU.mult), reads=[rscr, rt8b, rwk], writes=[rwk])
                sc.op("dve", lambda e: e.tensor_scalar(out=mpad[:, :, 64:128], in0=wk[:, :, :], scalar1=-BIG, scalar2=None, op0=ALU.add),
                      reads=[rwk, rmpad0], writes=[rmpad])
                for s in (range(4) if selv >= 1 else []):
                    sc.op("pe", lambda e, s=s: e.matmul(psT[:, s * 128:(s + 1) * 128], mpad[:, s, :], A.ident[:, :], start=True, stop=True),
                          reads=[rmpad, A.rC], writes=[rpsT])
                for r in (range(4) if selv >= 2 else []):
                    sc.op("act", lambda e, r=r, i4=i4: e.copy(out=QN[r][64:128, i4 * 512:(i4 + 1) * 512], in_=psT[64:128, :]),
                          reads=[rpsT], writes=[rQm[r]])
            for r in (range(4) if 'sw' in parts else []):
                hb = 4 * g + r
                hg = 8 + hb
                b2 = hb % 2
                if r + 1 < 4:
                    load_bt(4 * g + r + 1)
                for qt in range(nqt):
                    Gt, rGt = load_gates(hb, qt)
                    acc, racc = A.next_acc()
                    kts = list(range(0, 4 * qt + 4))
                    for idx, kt in enumerate(kts):
                        near = kt >= 4 * qt - 1
                        lastt = idx == len(kts) - 1
                        fin = None
                        if lastt:
                            def fin(acc=acc, racc=racc, Gt=Gt, rGt=rGt, r=r, qt=qt):
                                A.finish(acc, racc, Gt[64:65, 1, :], rGt, OTh[:, :], rOTh, "add", add_ap=OTc[r][:, qt * 512:(qt + 1) * 512], rAdd=rOTc[r],
                                         defer=(4 if qt == 0 else 8))
                        A.tile(KS[0:128, kt * 128:(kt + 1) * 128], Res2(rKS, rKSaug), QN[r][0:128, qt * 512:(qt + 1) * 512], Res2(rQq[r], rQm[r]),
                               BT[b2][:, kt - (4 * qt - 1), :] if near else None, rBT[b2][kt - (4 * qt - 1)] if near else None, hg, VS[:, kt, :], Res2(rVS, rVone),
                               acc, racc, idx == 0, lastt, after=fin, cols=(max(0, 128 * (kt - 4 * qt)), 512))
                    acc, racc = A.next_acc()
                    kts = list(range(max(0, 4 * qt - 4), 4 * qt + 4))
                    if qt > 0:
                        kts = [4 * qt - 1] + [k_ for k_ in kts if k_ != 4 * qt - 1]
                    for idx, kt in enumerate(kts):
                        lastt = idx == len(kts) - 1
                        rel = kt - 4 * qt
                        wcols = (128 * rel, 512) if rel >= 0 else (0, min(512, 128 * (rel + 5)))
                        fin = None
                        if lastt:
                            def fin(acc=acc, racc=racc, Gt=Gt, rGt=rGt, b2=b2, qt=qt):
                                A.finish(acc, racc, Gt[64:65, 2, :], rGt, OTs[b2][:, qt * 512:(qt + 1) * 512], rOTs[b2], "add", add_ap=OTh[:, :], rAdd=rOTh,
                                         defer=min(10, 4 * qt + 8))
                        A.tile(KW[0:128, kt * 128:(kt + 1) * 128], Res2(rKW, rKz), QN[r][0:128, qt * 512:(qt + 1) * 512], Res2(rQq[r], rQm[r]),
                               BT[b2][:, 5 + kt - (4 * qt - 4), :], rBT[b2][5 + kt - (4 * qt - 4)], hg, VW[:, kt, :], Res2(rVW, rVone),
                               acc, racc, idx == 0, lastt, after=fin, cols=wcols)
                A.flush()
                sc.dma("sp", OT_all[hg * 64:(hg + 1) * 64, :], OTs[b2][:, :], reads=[rOTs[b2]], writes=[r_ot])
        sc.emit()


def outproj_phase(nc, tag, OT_all, x1T, w_out, x2T, ntok=S):
    sc = Sched(nc, tag)
    TN = 512
    with ExitStack() as es:
        def sb(name, shape, dt):
            return es.enter_context(nc.sbuf_tensor(f"{tag}_{name}", shape, dt))
        wo = sb("wo", [128, 8, D], BF16)
        ot = [sb(f"ot{i}", [128, 8, TN], BF16) for i in range(2)]
        xt = [sb(f"xt{i}", [128, 8, TN], F32) for i in range(2)]
        ps = [es.enter_context(nc.psum_tensor(f"{tag}_ps{i}", [128, 512], F32)) for i in range(2)]
        r_w = Res("w")
        r_ot = [Res("ot0"), Res("ot1")]
        r_xt = [Res("xt0"), Res("xt1")]
        r_ps = [Res("ps0"), Res("ps1")]
        r_in, r_x1, r_x2 = Res("OT"), Res("x1"), Res("x2")
        for k in range(8):
            sc.dma("pool", wo[:, k, :], w_out[k * 128:(k + 1) * 128, :], writes=[r_w])
        ov = OT_all.rearrange("(c p) t -> p c t", p=128)
        x1v = x1T.rearrange("(c p) t -> p c t", p=128)
        x2v = x2T.rearrange("(c p) t -> p c t", p=128)
        for it in range(ntok // TN):
            t0 = it * TN
            b = it % 2
            sc.dma("sp", ot[b][:, :, :], ov[:, :, t0:t0 + TN], reads=[r_in], writes=[r_ot[b]])
            sc.dma("sp", xt[b][:, :, :], x1v[:, :, t0:t0 + TN], reads=[r_x1], writes=[r_xt[b]])
            for d in range(8):
                j = d % 2
                for o in range(8):
                    sc.op("pe", lambda e, o=o, d=d, j=j, b=b: e.matmul(ps[j][:, :], wo[:, o, d * 128:(d + 1) * 128], ot[b][:, o, :],
                                                                      start=(o == 0), stop=(o == 7)), reads=[r_w, r_ot[b]], writes=[r_ps[j]])
                sc.op("dve", lambda e, d=d, j=j, b=b: e.tensor_tensor(out=xt[b][:, d, :], in0=ps[j][:, :], in1=xt[b][:, d, :], op=ALU.add),
                      reads=[r_ps[j], r_xt[b]], writes=[r_xt[b]])
            sc.dma("sp", x2v[:, :, t0:t0 + TN], xt[b][:, :, :], reads=[r_xt[b]], writes=[r_x2])
        sc.emit()


W_NAMES = ["norm_ffn1", "w_ffn1_gate", "w_ffn1_up", "w_ffn1_down", "norm_mix", "w_in", "cmp_pos_k", "cmp_w1_k", "cmp_w2_k",
           "cmp_pos_v", "cmp_w1_v", "cmp_w2_v", "w_out", "norm_ffn2", "w_ffn2_gate", "w_ffn2_up", "w_ffn2_down"]
W_SHAPES = {"norm_ffn1": [D], "w_ffn1_gate": [D, DFF], "w_ffn1_up": [D, DFF], "w_ffn1_down": [DFF, D], "norm_mix": [D], "w_in": [D, DIN],
            "cmp_pos_k": [32, 64], "cmp_w1_k": [2048, 256], "cmp_w2_k": [256, 64], "cmp_pos_v": [32, 64], "cmp_w1_v": [2048, 256],
            "cmp_w2_v": [256, 64], "w_out": [D, D], "norm_ffn2": [D], "w_ffn2_gate": [D, DFF], "w_ffn2_up": [D, DFF], "w_ffn2_down": [DFF, D]}


def build_program(stages=None, debug_out=(), nsa_kw={}):
    nc = bass.Bass("TRN2", target_bir_lowering=False)
    cstn = static_consts()

    def din(name, shape, dt=F32):
        return nc.dram_tensor(name, list(shape), dt, kind="ExternalInput").ap()

    xT = din("xT", [D, S])
    w = {n: din(n, W_SHAPES[n]) for n in W_NAMES}
    rel_bias = din("rel_bias", [32, 16])
    norm_final = din("norm_final", [D])
    cst = {n: din(n, a.shape) for n, a in cstn.items()}
    outT = nc.dram_tensor("outT", [D, S], F32, kind="ExternalOutput").ap()

    def scratch(name, shape, dt):
        kind = "ExternalOutput" if name in debug_out else "Internal"
        return nc.dram_tensor(name, list(shape), dt, kind=kind).ap()
    x1T = scratch("x1T", [D, S], F32)
    h2T = scratch("h2T", [D, S], BF16)
    projT = scratch("projT", [16 * 128, S], BF16)
    Vtok = scratch("Vtok", [S, 768], BF16)
    gT = scratch("gT", [24, S], F32)
    Rc = scratch("Rc", [16, RLEN], BF16)
    Rw = scratch("Rw", [16, RLEN], BF16)
    kcmpT = scratch("kcmpT", [128, 256], BF16)
    vcmp = scratch("vcmp", [512, 64], BF16)
    OT_all = scratch("OT_all", [D, S], BF16)
    x2T = scratch("x2T", [D, S], F32)
    st = stages or ["const", "ffn1", "inproj", "moba", "compress", "nsa", "outproj", "ffn2"]
    if "const" in st:
        const_phase(nc, "c0", rel_bias, cst, Rc, Rw)
    if "ffn1" in st:
        ffn_phase(nc, "f1", xT, x1T, w["norm_ffn1"], w["w_ffn1_gate"], w["w_ffn1_up"], w["w_ffn1_down"], w["norm_mix"], h2T, BF16)
    if "inproj" in st:
        inproj_phase(nc, "ip", h2T, w["w_in"], projT, Vtok, gT)
    if "moba" in st:
        moba_phase(nc, "mo", projT, Vtok, Rc, rel_bias, cst, OT_all)
    if "compress" in st:
        compress_phase(nc, "cp", projT, w["cmp_pos_k"], w["cmp_w1_k"], w["cmp_w2_k"], w["cmp_pos_v"], w["cmp_w1_v"], w["cmp_w2_v"], kcmpT, vcmp)
    if "nsa" in st:
        nsa_phase(nc, "ns", projT, Vtok, gT, kcmpT, vcmp, Rc, Rw, rel_bias, cst, OT_all, **nsa_kw)
    if "outproj" in st:
        outproj_phase(nc, "op", OT_all, x1T, w["w_out"], x2T)
    if "ffn2" in st:
        ffn_phase(nc, "f2", x2T, None, w["norm_ffn2"], w["w_ffn2_gate"], w["w_ffn2_up"], w["w_ffn2_down"], norm_final, outT, F32)
    return nc


def make_in_map(inputs, b):
    m = {"xT": np.ascontiguousarray(inputs["x"][b].T)}
    for n in W_NAMES:
        m[n] = np.ascontiguousarray(np.asarray(inputs[n], dtype=np.float32)[0])
    m["rel_bias"] = np.ascontiguousarray(np.asarray(inputs["rel_bias"], dtype=np.float32))
    m["norm_final"] = np.ascontiguousarray(np.asarray(inputs["norm_final"], dtype=np.float32))
    m.update(static_consts())
    return m


def kernel(**inputs):
    inputs = {k: np.asarray(v) for k, v in inputs.items()}
    nb = inputs["x"].shape[0]
    nc = build_program()
    in_maps = [make_in_map(inputs, b) for b in range(nb)]
    res = run_bass_kernel_spmd(nc, in_maps, core_ids=list(range(nb)))
    out = np.stack([np.ascontiguousarray(np.asarray(r["outT"]).T) for r in res.results], axis=0)
    return out.astype(np.float32)
```

```python
import math
from contextlib import ExitStack

import numpy as np
import concourse.bass as bass
import concourse.mybir as mybir
from concourse.bass_utils import run_bass_kernel_spmd

F32 = mybir.dt.float32
BF16 = mybir.dt.bfloat16
ALU = mybir.AluOpType
AF = mybir.ActivationFunctionType
AX = mybir.AxisListType

S = 4096
D = 1024
DFF = 2816
NFC = DFF // 128
DIN = 2840
HD = 64
EPS = 1e-6
BIG = 32768.0
ROFF = 4096
RLEN = 8704


class Res:
    __slots__ = ("name", "lw", "rd")

    def __init__(self, name):
        self.name = name
        self.lw = None
        self.rd = {}


def _flat(x):
    out = []
    for r in x:
        if r is None:
            continue
        if isinstance(r, (list, tuple)):
            out.extend(_flat(r))
        else:
            out.append(r)
    return out


def Res2(*a):
    return list(a)


class Sched:
    ENGS = ("pe", "act", "dve", "pool", "sp")

    def __init__(self, nc, tag, ndma=6):
        self.nc = nc
        self.prog = {e: [] for e in self.ENGS}
        self.cnt = {e: 0 for e in self.ENGS}
        self.sem = {e: nc.alloc_semaphore(name=f"{tag}_{e}") for e in self.ENGS}
        self.dq = {}
        for q, n in (("sp", 4), ("pool", 3)):
            self.dq[q] = [[nc.alloc_semaphore(name=f"{tag}_d{q}{i}"), 0] for i in range(n)]
        self.dqi = {q: 0 for q in self.dq}
        self.waited = {}
        self.all_dma = {}

    def _key(self, tok):
        return (tok[0], tok[1] if tok[0] == "eng" else id(tok[1]))

    def _wait(self, eng, tok):
        if tok is None:
            return
        if tok[0] == "eng":
            if eng == "pe" and tok[1] == "pe":
                return
            k = ("e", tok[1])
            sem = self.sem[tok[1]]
        else:
            k = ("d", id(tok[1]))
            sem = tok[1]
        kk = (eng, k)
        if self.waited.get(kk, 0) >= tok[2]:
            return
        self.waited[kk] = tok[2]
        self.prog[eng].append(("w", sem, tok[2]))

    def _deps(self, eng, reads, writes):
        for r in reads:
            self._wait(eng, r.lw)
        for r in writes:
            self._wait(eng, r.lw)
            for t in r.rd.values():
                self._wait(eng, t)

    def _commit(self, tok, reads, writes):
        kk = self._key(tok)
        for r in reads:
            r.rd[kk] = tok
        for r in writes:
            r.lw = tok
            r.rd = {}

    def op(self, eng, fn, reads=(), writes=()):
        reads, writes = _flat(reads), _flat(writes)
        self._deps(eng, reads, writes)
        self.cnt[eng] += 1
        tok = ("eng", eng, self.cnt[eng])
        self.prog[eng].append(("op", fn))
        self._commit(tok, reads, writes)

    def dma(self, q, out, in_, reads=(), writes=(), **kw):
        reads, writes = _flat(reads), _flat(writes)
        self._deps(q, reads, writes)
        pool = self.dq[q]
        ent = pool[self.dqi[q] % len(pool)]
        self.dqi[q] += 1
        if ent[1] > 0:
            self._wait(q, ("dma", ent[0], ent[1]))
        ent[1] += 16
        tok = ("dma", ent[0], ent[1])
        self.all_dma[id(ent[0])] = tok
        self.prog[q].append(("dma", out, in_, kw, ent[0]))
        self._commit(tok, reads, writes)

    def emit(self):
        nc = self.nc
        for e in self.ENGS:
            for e2 in self.ENGS:
                if e2 != e and self.cnt[e2] > 0:
                    self._wait(e, ("eng", e2, self.cnt[e2]))
            for tok in self.all_dma.values():
                self._wait(e, tok)
        prog, sem = self.prog, self.sem

        def run(ename, eng):
            s = sem[ename]
            for it in prog[ename]:
                if it[0] == "w":
                    eng.wait_ge(it[1], it[2])
                elif it[0] == "op":
                    it[1](eng).then_inc(s, 1)
                else:
                    eng.dma_start(out=it[1], in_=it[2], **it[3]).then_inc(it[4], 16)

        with nc.Block() as blk:
            @blk.tensor
            def _(e):
                run("pe", e)

            @blk.scalar
            def _(e):
                run("act", e)

            @blk.vector
            def _(e):
                run("dve", e)

            @blk.gpsimd
            def _(e):
                run("pool", e)

            @blk.sync
            def _(e):
                run("sp", e)


def dram_view(t, offset, dims):
    return bass.AP(t, offset, [list(d) for d in dims])


def emit_rmsnorm(sc, N, xt, r_x, gain, r_gain, out_t, r_out, ones32, r_const, sq, r_sq, ps, r_ps, rstd, r_rstd):
    for c in range(8):
        j = c % 2
        sc.op("act", lambda e, c=c, j=j: e.activation(out=sq[j][:, 0:N], in_=xt[:, c, 0:N], func=AF.Square),
              reads=[r_x], writes=[r_sq[j]])
        sc.op("pe", lambda e, c=c, j=j: e.matmul(ps[:, 0:N], ones32[:, :], sq[j][:, 0:N], start=(c == 0), stop=(c == 7)),
              reads=[r_sq[j], r_const], writes=[r_ps])
    sc.op("dve", lambda e: e.tensor_scalar(out=rstd[:, 0:N], in0=ps[:, 0:N], scalar1=1.0 / D, scalar2=EPS,
                                           op0=ALU.mult, op1=ALU.add), reads=[r_ps], writes=[r_rstd])
    sc.op("act", lambda e: e.activation(out=rstd[:, 0:N], in_=rstd[:, 0:N], func=AF.Sqrt), reads=[r_rstd], writes=[r_rstd])
    sc.op("dve", lambda e: e.reciprocal(out=rstd[:, 0:N], in_=rstd[:, 0:N]), reads=[r_rstd], writes=[r_rstd])
    for c in range(8):
        sc.op("dve", lambda e, c=c: e.scalar_tensor_tensor(out=out_t[:, c, 0:N], in0=xt[:, c, 0:N], scalar=gain[:, c:c + 1],
                                                          in1=rstd[:, 0:N], op0=ALU.mult, op1=ALU.mult),
              reads=[r_x, r_rstd, r_gain], writes=[r_out])


def ffn_phase(nc, tag, x_in, x_out, g_pre, wg, wu, wd, g_post, post_out, post_dt, ntok=S, TN=256):
    sc = Sched(nc, tag)
    r_xin, r_xout, r_post = Res("xin"), Res("xout"), Res("post")
    with ExitStack() as es:
        def sb(name, shape, dt):
            return es.enter_context(nc.sbuf_tensor(f"{tag}_{name}", shape, dt))

        def pst(name):
            return es.enter_context(nc.psum_tensor(f"{tag}_{name}", [128, 512], F32))

        wg_sb = sb("wg", [128, 8, DFF], BF16)
        wu_sb = sb("wu", [128, 8, DFF], BF16)
        wd_sb = sb("wd", [128, NFC, D], BF16)
        gpre = sb("gpre", [128, 8], F32)
        gpost = sb("gpost", [128, 8], F32)
        ones32 = sb("ones", [128, 128], F32)
        xt = [sb(f"xt{i}", [128, 8, TN], F32) for i in range(2)]
        ht = [sb(f"ht{i}", [128, 8, TN], BF16) for i in range(2)]
        at = sb("at", [128, NFC, TN], BF16)
        pt = sb("pt", [128, 8, TN], post_dt)
        sq = [sb(f"sq{i}", [128, TN], F32) for i in range(2)]
        sg = [sb(f"sg{i}", [128, TN], F32) for i in range(2)]
        rstd = sb("rstd", [128, TN], F32)
        ps_g = [pst(f"psg{i}") for i in range(2)]
        ps_u = [pst(f"psu{i}") for i in range(2)]
        ps_y = [pst(f"psy{i}") for i in range(2)]
        ps_n = pst("psn")

        r_const = Res("const")
        r_gain = Res("gain")
        r_xt = [Res("xt0"), Res("xt1")]
        r_ht = [Res("ht0"), Res("ht1")]
        r_at, r_pt, r_rstd = Res("at"), Res("pt"), Res("rstd")
        r_sq = [Res("sq0"), Res("sq1")]
        r_sg = [Res("sg0"), Res("sg1")]
        r_psg = [Res("psg0"), Res("psg1")]
        r_psu = [Res("psu0"), Res("psu1")]
        r_psy = [Res("psy0"), Res("psy1")]
        r_psn = Res("psn")

        sc.op("dve", lambda e: e.memset(ones32[:, :], 1.0), writes=[r_const])
        sc.dma("sp", gpre[:, :], g_pre.rearrange("(c p) -> p c", p=128), writes=[r_gain], allow_slow_non_contiguous=True)
        sc.dma("sp", gpost[:, :], g_post.rearrange("(c p) -> p c", p=128), writes=[r_gain], allow_slow_non_contiguous=True)
        r_wk1 = [Res(f"wk{k}") for k in range(8)]
        r_wk = [[r_wk1[k]] * NFC for k in range(8)]
        r_wd = [Res(f"wd{f}") for f in range(NFC)]
        for k in range(8):
            sc.dma("pool", wg_sb[:, k, :], wg[k * 128:(k + 1) * 128, :], writes=[r_wk1[k]])
            sc.dma("pool", wu_sb[:, k, :], wu[k * 128:(k + 1) * 128, :], writes=[r_wk1[k]])
        for f in range(NFC):
            sc.dma("pool", wd_sb[:, f, :], wd[f * 128:(f + 1) * 128, :], writes=[r_wd[f]])

        xin_v = x_in.rearrange("(c p) t -> p c t", p=128)
        xout_v = x_out.rearrange("(c p) t -> p c t", p=128) if x_out is not None else None
        post_v = post_out.rearrange("(c p) t -> p c t", p=128)
        ntile = ntok // TN

        def load(t):
            b = t % 2
            sc.dma("sp", xt[b][:, :, :], xin_v[:, :, t * TN:(t + 1) * TN], reads=[r_xin], writes=[r_xt[b]])

        def prenorm(t):
            b = t % 2
            emit_rmsnorm(sc, TN, xt[b], r_xt[b], gpre, r_gain, ht[b], r_ht[b], ones32, r_const, sq, r_sq, ps_n, r_psn, rstd, r_rstd)

        def gateup(t, f0, f1):
            b = t % 2
            H, rH = ht[b], r_ht[b]
            for f in range(f0, f1):
                j = f % 2
                for k in range(8):
                    sc.op("pe", lambda e, f=f, k=k, j=j, H=H: e.matmul(ps_g[j][:, 0:TN], wg_sb[:, k, f * 128:(f + 1) * 128], H[:, k, :],
                                                                      start=(k == 0), stop=(k == 7)),
                          reads=[r_wk[k][f], rH], writes=[r_psg[j]])
                for k in range(8):
                    sc.op("pe", lambda e, f=f, k=k, j=j, H=H: e.matmul(ps_u[j][:, 0:TN], wu_sb[:, k, f * 128:(f + 1) * 128], H[:, k, :],
                                                                      start=(k == 0), stop=(k == 7)),
                          reads=[r_wk[k][f], rH], writes=[r_psu[j]])
                sc.op("act", lambda e, j=j: e.activation(out=sg[j][:, :], in_=ps_g[j][:, 0:TN], func=AF.Silu),
                      reads=[r_psg[j]], writes=[r_sg[j]])
                sc.op("dve", lambda e, f=f, j=j: e.tensor_tensor(out=at[:, f, :], in0=sg[j][:, :], in1=ps_u[j][:, 0:TN], op=ALU.mult),
                      reads=[r_sg[j], r_psu[j]], writes=[r_at])

        def down(t):
            b = t % 2
            X, rX = xt[b], r_xt[b]
            for d in range(8):
                j = d % 2
                for f in range(NFC):
                    sc.op("pe", lambda e, f=f, d=d, j=j: e.matmul(ps_y[j][:, 0:TN], wd_sb[:, f, d * 128:(d + 1) * 128], at[:, f, :],
                                                                 start=(f == 0), stop=(f == NFC - 1)),
                          reads=[r_wd[f], r_at], writes=[r_psy[j]])
                sc.op("dve", lambda e, d=d, j=j, X=X: e.scalar_tensor_tensor(out=X[:, d, :], in0=ps_y[j][:, 0:TN], scalar=0.5,
                                                                          in1=X[:, d, :], op0=ALU.mult, op1=ALU.add),
                      reads=[r_psy[j], rX], writes=[rX])
            if xout_v is not None:
                sc.dma("sp", xout_v[:, :, t * TN:(t + 1) * TN], X[:, :, :], reads=[rX], writes=[r_xout])

        def postnorm(t):
            b = t % 2
            emit_rmsnorm(sc, TN, xt[b], r_xt[b], gpost, r_gain, pt, r_pt, ones32, r_const, sq, r_sq, ps_n, r_psn, rstd, r_rstd)
            sc.dma("sp", post_v[:, :, t * TN:(t + 1) * TN], pt[:, :, :], reads=[r_pt], writes=[r_post])

        load(0)
        prenorm(0)
        for t in range(ntile + 1):
            if t < ntile:
                gateup(t, 0, 4)
            if t > 0:
                postnorm(t - 1)
            if t + 1 < ntile:
                load(t + 1)
            if t < ntile:
                gateup(t, 4, NFC)
            if t + 1 < ntile:
                prenorm(t + 1)
            if t < ntile:
                down(t)
        sc.emit()


def _bucket(d):
    n = np.maximum(d, 0)
    nf = np.maximum(n, 1).astype(np.float32)
    large = 16 + (np.log(nf / np.float32(16)) / np.float32(math.log(128 / 16)) * np.float32(16)).astype(np.int32)
    return np.where(n < 16, n, np.minimum(large, 31)).astype(np.int64)


def _slc_overlap(n_cmp, n_slc):
    r, mc = 4, 2
    j = np.arange(n_slc)
    offs = (np.arange(r)[:, None] + np.arange(mc)[None, :]).reshape(-1)
    c = r * j[:, None] + offs[None, :]
    w = np.zeros((n_slc * r + mc, n_slc), np.float32)
    np.add.at(w, (c, np.broadcast_to(j[:, None], c.shape)), 1.0)
    return w[:n_cmp]


_CONSTS = None


def static_consts():
    global _CONSTS
    if _CONSTS is not None:
        return _CONSTS
    c = {}
    i = np.arange(RLEN)
    d = i - ROFF
    b = _bucket(d)
    okc = d >= 0
    okw = (d >= 0) & (d < 512)
    ohc = np.zeros((32, RLEN), np.float32)
    ohc[b[okc], i[okc]] = 1.0
    ohw = np.zeros((32, RLEN), np.float32)
    ohw[b[okw], i[okw]] = 1.0
    c["c_ohc"], c["c_ohw"] = ohc, ohw
    c["c_negc"] = np.broadcast_to(np.where(okc, 0.0, -BIG).astype(np.float32), (16, RLEN)).copy()
    c["c_negw"] = np.broadcast_to(np.where(okw, 0.0, -BIG).astype(np.float32), (16, RLEN)).copy()
    c["c_ident"] = np.eye(128, dtype=np.float32)
    c["c_jx"] = np.eye(128, dtype=np.float32)[::-1].copy()
    t = np.arange(S)
    cur = (t // 256)[:, None]
    n = np.arange(16)[None, :]
    def lay(a):
        return np.ascontiguousarray(a.reshape(32, 128, a.shape[-1]).transpose(1, 0, 2)).astype(np.float32)
    c["c_mneg"] = lay(np.where(n >= cur, -1e30, 0.0))
    c["c_mallow"] = lay((n < cur) * 1.0)
    c["c_mown"] = lay((n == cur) * 1.0)
    cur = (t // 64)[:, None]
    j = np.arange(64)[None, :]
    forced = (j == 0) | (j == cur) | (j == cur - 1)
    c["c_slc"] = lay(np.where(forced, 1e4, np.where(j <= cur, 0.0, -1e4)))
    k = np.arange(S)[None, :]
    c["c_kaugm"] = ((k // 256) == np.arange(16)[:, None]).astype(np.float32)
    c["c_kaugs"] = ((k // 64) == np.arange(64)[:, None]).astype(np.float32)
    ov = np.zeros((256, 65), np.float32)
    ov[:255, :64] = _slc_overlap(255, 64)
    ov[:255, 64] = 1.0
    c["c_ovl"] = ov
    _CONSTS = c
    return c


def const_phase(nc, tag, rel_bias, cst, Rc, Rw):
    sc = Sched(nc, tag)
    with ExitStack() as es:
        def sb(name, shape, dt):
            return es.enter_context(nc.sbuf_tensor(f"{tag}_{name}", shape, dt))
        tab = sb("tab", [32, 16], F32)
        oh = sb("oh", [32, RLEN], F32)
        ng = sb("ng", [16, RLEN], F32)
        rs = sb("rs", [16, RLEN], BF16)
        ps = [es.enter_context(nc.psum_tensor(f"{tag}_ps{i}", [128, 512], F32)) for i in range(2)]
        r_tab, r_oh, r_ng, r_rs, r_R = Res("tab"), Res("oh"), Res("ng"), Res("rs"), Res("R")
        r_ps = [Res("ps0"), Res("ps1")]
        sc.dma("sp", tab[:, :], rel_bias[:, :], writes=[r_tab])
        for ohd, ngd, Rd in ((cst["c_ohc"], cst["c_negc"], Rc), (cst["c_ohw"], cst["c_negw"], Rw)):
            sc.dma("sp", oh[:, :], ohd[:, :], writes=[r_oh])
            sc.dma("sp", ng[:, :], ngd[:, :], writes=[r_ng])
            for ch in range(RLEN // 512):
                j = ch % 2
                sl = slice(ch * 512, (ch + 1) * 512)
                sc.op("pe", lambda e, j=j, sl=sl: e.matmul(ps[j][0:16, :], tab[:, :], oh[:, sl], start=True, stop=True),
                      reads=[r_tab, r_oh], writes=[r_ps[j]])
                sc.op("dve", lambda e, j=j, sl=sl: e.scalar_tensor_tensor(out=rs[:, sl], in0=ps[j][0:16, :], scalar=8.0, in1=ng[:, sl],
                                                                          op0=ALU.mult, op1=ALU.add),
                      reads=[r_ps[j], r_ng], writes=[r_rs])
            sc.dma("sp", Rd[:, :], rs[:, :], reads=[r_rs], writes=[r_R])
        sc.emit()


FM_CHUNKS = [0, 1, 2, 3, 4, 5, 6, 7, 12, 13, 14, 15, 16, 17, 18, 20]
VT_GROUPS = [(1024, 512, 0), (2432, 128, 512), (2688, 128, 640)]


def inproj_phase(nc, tag, h2T, w_in, projT, Vtok, gT, ntok=S):
    sc = Sched(nc, tag)
    TN = 512
    with ExitStack() as es:
        def sb(name, shape, dt):
            return es.enter_context(nc.sbuf_tensor(f"{tag}_{name}", shape, dt))

        def pst(name):
            return es.enter_context(nc.psum_tensor(f"{tag}_{name}", [128, 512], F32))
        W = sb("w", [128, 8, DIN], BF16)
        ht = [sb(f"ht{i}", [128, 8, TN], BF16) for i in range(2)]
        stage = sb("stage", [128, 16, TN], BF16)
        gst = sb("gst", [24, TN], F32)
        vst = [sb(f"vst{i}", [128, 768], BF16) for i in range(2)]
        psA = [pst(f"psa{i}") for i in range(2)]
        psG = pst("psg")
        psV = [pst(f"psv{i}") for i in range(3)]
        r_w = [Res(f"w{k}") for k in range(8)]
        r_ht = [Res("ht0"), Res("ht1")]
        r_stage, r_gst = Res("stage"), Res("gst")
        r_vst = [Res("vst0"), Res("vst1")]
        r_psA = [Res("psa0"), Res("psa1")]
        r_psG = Res("psg")
        r_psV = [Res(f"psv{i}") for i in range(3)]
        r_in, r_proj, r_vt, r_g = Res("h2T"), Res("projT"), Res("Vtok"), Res("gT")
        for k in range(8):
            sc.dma("pool", W[:, k, :], w_in[k * 128:(k + 1) * 128, :], writes=[r_w[k]])
        hv = h2T.rearrange("(c p) t -> p c t", p=128)
        pv = projT.rearrange("(c p) t -> p c t", p=128)
        cp = 0
        for it in range(ntok // TN):
            t0 = it * TN
            H, rH = ht[it % 2], r_ht[it % 2]
            sc.dma("sp", H[:, :, :], hv[:, :, t0:t0 + TN], reads=[r_in], writes=[rH])
            for ci, c in enumerate(FM_CHUNKS):
                j = ci % 2
                for k in range(8):
                    sc.op("pe", lambda e, c=c, k=k, j=j, H=H: e.matmul(psA[j][:, :], W[:, k, c * 128:(c + 1) * 128], H[:, k, :],
                                                                      start=(k == 0), stop=(k == 7)),
                          reads=[r_w[k], rH], writes=[r_psA[j]])
                if ci % 2 == 0:
                    sc.op("act", lambda e, ci=ci, j=j: e.copy(out=stage[:, ci, :], in_=psA[j][:, :]), reads=[r_psA[j]], writes=[r_stage])
                else:
                    sc.op("dve", lambda e, ci=ci, j=j: e.tensor_copy(out=stage[:, ci, :], in_=psA[j][:, :]), reads=[r_psA[j]], writes=[r_stage])
            sc.dma("sp", pv[:, :, t0:t0 + TN], stage[:, :, :], reads=[r_stage], writes=[r_proj])
            for k in range(8):
                sc.op("pe", lambda e, k=k, H=H: e.matmul(psG[0:24, :], W[:, k, 2816:2840], H[:, k, :], start=(k == 0), stop=(k == 7)),
                      reads=[r_w[k], rH], writes=[r_psG])
            sc.op("act", lambda e: e.activation(out=gst[:, :], in_=psG[0:24, :], func=AF.Sigmoid), reads=[r_psG], writes=[r_gst])
            sc.dma("sp", gT[:, t0:t0 + TN], gst[:, :], reads=[r_gst], writes=[r_g])
            for s in range(4):
                vs, rvs = vst[s % 2], r_vst[s % 2]
                for gi, (c0, wd_, v0) in enumerate(VT_GROUPS):
                    for k in range(8):
                        sc.op("pe", lambda e, k=k, gi=gi, c0=c0, wd_=wd_, s=s, H=H: e.matmul(
                            psV[gi][:, 0:wd_], H[:, k, s * 128:(s + 1) * 128], W[:, k, c0:c0 + wd_], start=(k == 0), stop=(k == 7)),
                            reads=[r_w[k], rH], writes=[r_psV[gi]])
                    if gi == 0:
                        sc.op("act", lambda e, gi=gi, wd_=wd_, v0=v0, vs=vs: e.copy(out=vs[:, v0:v0 + wd_], in_=psV[gi][:, 0:wd_]),
                              reads=[r_psV[gi]], writes=[rvs])
                    else:
                        sc.op("dve", lambda e, gi=gi, wd_=wd_, v0=v0, vs=vs: e.tensor_copy(out=vs[:, v0:v0 + wd_], in_=psV[gi][:, 0:wd_]),
                              reads=[r_psV[gi]], writes=[rvs])
                tt = t0 + s * 128
                sc.dma("sp", Vtok[tt:tt + 128, :], vs[:, :], reads=[rvs], writes=[r_vt])
        sc.emit()


class AttnCtx:
    def __init__(self, nc, sc, es, tag):
        self.nc, self.sc = nc, sc

        def sb(name, shape, dt):
            return es.enter_context(nc.sbuf_tensor(f"{tag}_{name}", shape, dt))

        def pst(name):
            return es.enter_context(nc.psum_tensor(f"{tag}_{name}", [128, 512], F32))
        self.sb, self.pst = sb, pst
        self.ST = [pst(f"st{i}") for i in range(3)]
        self.rST = [Res(f"st{i}") for i in range(3)]
        self.ACC = [pst(f"acc{i}") for i in range(2)]
        self.rACC = [Res(f"acc{i}") for i in range(2)]
        self.BC = pst("bc")
        self.rBC = Res("bc")
        self.PT = [sb(f"pt{i}", [128, 512], BF16) for i in range(4)]
        self.rPT = [Res(f"pt{i}") for i in range(4)]
        self.rdb = [sb(f"rd{i}", [128, 512], F32) for i in range(2)]
        self.rRDb = [Res(f"rd{i}") for i in range(2)]
        self.rdi = 0
        self.rdh = [sb(f"rdh{i}", [128, 1024], BF16) for i in range(2)]
        self.rRDh = [Res(f"rdh{i}") for i in range(2)]
        self.sel64 = sb("sel64", [128, 128], BF16)
        self.osb = [sb(f"osb{i}", [64, 512], F32) for i in range(2)]
        self.rOSB = [Res(f"osb{i}") for i in range(2)]
        self.sti = self.pti = self.acci = self.osi = 0
        self.pending = []
        self.deferred = []
        self.ident = sb("ident", [128, 128], BF16)
        self.jx = sb("jx", [128, 128], BF16)
        self.ones32 = sb("ones32", [128, 64], F32)
        self.t31 = sb("t31", [128, 16], F32)
        self.rC = Res("const")

    def load_consts(self, cst, rel_bias):
        sc = self.sc
        sc.dma("pool", self.ident[:, :], cst["c_ident"][:, :], writes=[self.rC])
        sc.dma("pool", self.jx[:, :], cst["c_jx"][:, :], writes=[self.rC])
        sc.op("dve", lambda e: e.memset(self.ones32[:, :], 1.0), writes=[self.rC])
        sc.op("dve", lambda e: e.memset(self.sel64[:, :], 0.0), writes=[self.rC])
        sc.op("dve", lambda e: e.memset(self.sel64[64:65, :], 1.0), writes=[self.rC])
        for i in range(2):
            sc.op("dve", lambda e, i=i: e.memset(self.rdh[i][:, :], 0.0), writes=[self.rRDh[i]])
        sc.dma("sp", self.t31[:, :], dram_view(rel_bias.tensor, 31 * 16, [[0, 128], [1, 16]]), writes=[self.rC])

    LOOKAHEAD = 2

    def tile(self, lhsT, rK, rhs, rQ, bias_tile, rB, head, vaug, rV, acc, racc, first, last, extra=None, after=None, cols=(0, 512)):
        sc = self.sc
        st, rst = self.ST[self.sti % 3], self.rST[self.sti % 3]
        self.sti += 1
        pt, rpt = self.PT[self.pti % 4], self.rPT[self.pti % 4]
        self.pti += 1
        near = bias_tile is not None
        lo, hi = cols
        assert not (first and (lo, hi) != (0, 512))
        sc.op("pe", lambda e: e.matmul(st[:, lo:hi], lhsT, rhs[:, lo:hi], start=True, stop=not near), reads=[rK, rQ], writes=[rst])
        if near:
            sc.op("pe", lambda e: e.matmul(st[:, lo:hi], self.jx[:, :], bias_tile[:, lo:hi], start=False, stop=True), reads=[rB, self.rC], writes=[rst])
            sc.op("act", lambda e: e.activation(out=pt[:, lo:hi], in_=st[:, lo:hi], func=AF.Exp, scale=0.125), reads=[rst], writes=[rpt])
        else:
            sc.op("act", lambda e: e.activation(out=pt[:, lo:hi], in_=st[:, lo:hi], func=AF.Exp, scale=0.125, bias=self.t31[:, head:head + 1]),
                  reads=[rst, self.rC], writes=[rpt])

        def pv():
            sc.op("pe", lambda e: e.matmul(acc[0:65, lo:hi], vaug, pt[:, lo:hi], start=first, stop=last), reads=[rV, rpt], writes=[racc])
            if extra is not None:
                extra(pt, rpt)
            if after is not None:
                after()
        self.pending.append(pv)
        while len(self.pending) > self.LOOKAHEAD:
            self.pending.pop(0)()
        for d in self.deferred:
            d[0] -= 1
        while self.deferred and self.deferred[0][0] <= 0:
            self.deferred.pop(0)[1]()

    def flush(self):
        while self.pending:
            self.pending.pop(0)()
        while self.deferred:
            self.deferred.pop(0)[1]()

    def next_acc(self):
        a, r = self.ACC[self.acci % 2], self.rACC[self.acci % 2]
        self.acci += 1
        return a, r

    def finish(self, acc, racc, gate_row, rG, out_ap, rOut, mode, add_ap=None, rAdd=None, split=True, defer=4, copy_eng="dve"):
        sc = self.sc
        rd, rRD = self.rdb[self.rdi % 2], self.rRDb[self.rdi % 2]
        rh, rRH = self.rdh[self.rdi % 2], self.rRDh[self.rdi % 2]
        self.rdi += 1
        sc.op("dve", lambda e: e.tensor_scalar(out=rd[64:65, :], in0=acc[64:65, :], scalar1=1e-30, scalar2=None, op0=ALU.max),
              reads=[racc], writes=[rRD])
        sc.op("dve", lambda e: e.reciprocal(out=rd[64:65, :], in_=rd[64:65, :]), reads=[rRD], writes=[rRD])
        if gate_row is not None:
            sc.op("dve", lambda e: e.tensor_tensor(out=rd[64:65, :], in0=rd[64:65, :], in1=gate_row, op=ALU.mult),
                  reads=[rRD, rG], writes=[rRD])
        sc.op("dve", lambda e: e.tensor_copy(out=rh[64:65, 0:512], in_=rd[64:65, :]), reads=[rRD], writes=[rRH])
        sc.op("dve", lambda e: e.tensor_tensor(out=rh[64:65, 512:1024], in0=rd[64:65, :], in1=rh[64:65, 0:512], op=ALU.subtract),
              reads=[rRD, rRH], writes=[rRH])

        def part_b():
            sc.op("pe", lambda e: e.matmul(self.BC[:, :], self.sel64[:, :], rh[:, 0:512], start=True, stop=False),
                  reads=[rRH, self.rC], writes=[self.rBC])
            sc.op("pe", lambda e: e.matmul(self.BC[:, :], self.sel64[:, :], rh[:, 512:1024], start=False, stop=True),
                  reads=[rRH, self.rC], writes=[self.rBC])
            osb, rosb = self.osb[self.osi % 2], self.rOSB[self.osi % 2]
            self.osi += 1
            if copy_eng == "act":
                sc.op("act", lambda e: e.copy(out=osb[:, :], in_=acc[0:64, :]), reads=[racc], writes=[rosb])
            else:
                sc.op("dve", lambda e: e.tensor_copy(out=osb[:, :], in_=acc[0:64, :]), reads=[racc], writes=[rosb])
            if mode == "set":
                sc.op("dve", lambda e: e.tensor_tensor(out=out_ap, in0=osb[:, :], in1=self.BC[0:64, :], op=ALU.mult),
                      reads=[rosb, self.rBC], writes=[rOut])
            else:
                sc.op("dve", lambda e: e.tensor_tensor(out=osb[:, :], in0=osb[:, :], in1=self.BC[0:64, :], op=ALU.mult),
                      reads=[rosb, self.rBC], writes=[rosb])
                sc.op("dve", lambda e: e.tensor_tensor(out=out_ap, in0=osb[:, :], in1=add_ap, op=ALU.add),
                      reads=[rosb, rAdd], writes=[rOut])
        if split:
            self.deferred.append([defer, part_b])
        else:
            part_b()


def rvec(R, h, off, pstride):
    return dram_view(R.tensor, h * RLEN + off, [[pstride, 128], [1, 512]])


def moba_phase(nc, tag, projT, Vtok, Rc, rel_bias, cst, OT_all, heads=range(8), nqt=8):
    sc = Sched(nc, tag)
    with ExitStack() as es:
        A = AttnCtx(nc, sc, es, tag)
        sb, pst = A.sb, A.pst
        A.load_consts(cst, rel_bias)
        QA = [sb(f"qa{i}", [128, S], BF16) for i in range(2)]
        KA = [sb(f"ka{i}", [128, S], BF16) for i in range(2)]
        VA = [sb(f"va{i}", [128, 32, 65], BF16) for i in range(2)]
        BT = [sb(f"bt{i}", [128, 5, 512], BF16) for i in range(2)]
        OTs = [sb(f"ots{i}", [64, S], BF16) for i in range(2)]
        cneg = sb("cneg", [128, 32, 16], F32)
        callow = sb("callow", [128, 32, 16], F32)
        cown = sb("cown", [128, 32, 16], F32)
        km32 = sb("km32", [64, 16], F32)
        kmT = sb("kmT", [64, 16], BF16)
        gmA = sb("gmA", [128, 32, 16], F32)
        g2A = sb("g2A", [128, 32, 16], F32)
        eqA = sb("eqA", [128, 32, 16], F32)
        mx = sb("mx", [128, 32], F32)
        mpadA = sb("mpadA", [128, 32, 80], BF16)
        psG = pst("psg")
        psT = pst("pst")
        rQq = [Res("qq0"), Res("qq1")]
        rQm = [Res("qm0"), Res("qm1")]
        rK = [Res("k0"), Res("k1")]
        rKaug = [Res("kaug0"), Res("kaug1")]
        rV = [Res("v0"), Res("v1")]
        rVone = [Res("vone0"), Res("vone1")]
        rBT = [[Res(f"bt{b}_{i}") for i in range(5)] for b in range(2)]
        rOTs = [Res("ots0"), Res("ots1")]
        rGc, rkm32, rkm, rgm, rtop8, rsel, rmpad, rmpad0, rg2 = (Res(n) for n in ("gc", "km32", "km", "gm", "top8", "sel", "mpad", "mpad0", "g2"))
        rpsG, rpsT = Res("psg"), Res("pst")
        r_proj, r_vt, r_R, r_ot = Res("projT"), Res("Vtok"), Res("Rc"), Res("OT")
        sc.dma("sp", cneg[:, :, :], cst["c_mneg"][:, :, :], writes=[rGc])
        sc.dma("sp", callow[:, :, :], cst["c_mallow"][:, :, :], writes=[rGc])
        sc.dma("sp", cown[:, :, :], cst["c_mown"][:, :, :], writes=[rGc])
        sc.op("dve", lambda e: e.memset(mpadA[:, :, :], 0.0), writes=[rmpad0])
        for i in range(2):
            sc.op("pool", lambda e, i=i: e.memset(VA[i][:, :, 64:65], 1.0), writes=[rVone[i]])
            sc.dma("pool", KA[i][64:80, :], cst["c_kaugm"][:, :], writes=[rKaug[i]])
        vtv = Vtok.rearrange("(t p) c -> p t c", p=128)
        heads = list(heads)

        def load_head(h):
            hb = h % 2
            qrow = (h // 2) * 128 + (h % 2) * 64
            krow = (4 + h // 2) * 128 + (h % 2) * 64
            sc.dma("sp", QA[hb][0:64, :], projT[qrow:qrow + 64, :], reads=[r_proj], writes=[rQq[hb]])
            sc.dma("sp", KA[hb][0:64, :], projT[krow:krow + 64, :], reads=[r_proj], writes=[rK[hb]])
            sc.dma("sp", VA[hb][:, :, 0:64], vtv[:, :, h * 64:(h + 1) * 64], reads=[r_vt], writes=[rV[hb]], allow_slow_non_contiguous=True)
            for i in range(5):
                rel = i - 1
                sc.dma("sp", BT[hb][:, i, :], rvec(Rc, h, ROFF - 128 * rel - 127, 1), reads=[r_R], writes=[rBT[hb][i]])

        def gating(h):
            hb = h % 2
            sc.op("dve", lambda e, hb=hb: e.tensor_reduce(out=km32[:, :], in_=KA[hb][0:64, :].rearrange("p (n b) -> p n b", b=256),
                                                          axis=AX.X, op=ALU.add), reads=[rK[hb]], writes=[rkm32])
            sc.op("dve", lambda e: e.tensor_scalar(out=kmT[:, :], in0=km32[:, :], scalar1=1.0 / 256, scalar2=None, op0=ALU.mult),
                  reads=[rkm32], writes=[rkm])
            for i in range(4 * nqt):
                sc.op("pe", lambda e, hb=hb, i=i: e.matmul(psG[:, i * 16:(i + 1) * 16], QA[hb][0:64, i * 128:(i + 1) * 128], kmT[:, :],
                                                          start=True, stop=True), reads=[rQq[hb], rkm], writes=[rpsG])
            nq = 4 * nqt
            psGv = psG[:, 0:nq * 16].rearrange("p (s n) -> p s n", n=16)
            mb = mx[:, 0:nq].to_broadcast([128, nq, 16])
            sc.op("dve", lambda e: e.tensor_tensor(out=gmA[:, 0:nq, :], in0=psGv, in1=cneg[:, 0:nq, :], op=ALU.add), reads=[rpsG, rGc], writes=[rgm])
            src, rsrc = gmA, rgm
            for rnd in range(2):
                sc.op("dve", lambda e, src=src: e.tensor_reduce(out=mx[:, 0:nq], in_=src[:, 0:nq, :], axis=AX.X, op=ALU.max), reads=[rsrc], writes=[rtop8])
                sc.op("dve", lambda e, src=src: e.tensor_tensor(out=eqA[:, 0:nq, :], in0=src[:, 0:nq, :], in1=mb, op=ALU.is_ge),
                      reads=[rsrc, rtop8], writes=[rsel])
                sc.op("dve", lambda e, src=src: e.scalar_tensor_tensor(out=g2A[:, 0:nq, :], in0=eqA[:, 0:nq, :], scalar=-1e32, in1=src[:, 0:nq, :],
                                                                      op0=ALU.mult, op1=ALU.add), reads=[rsel, rsrc], writes=[rg2])
                src, rsrc = g2A, rg2
            sc.op("dve", lambda e: e.tensor_reduce(out=mx[:, 0:nq], in_=g2A[:, 0:nq, :], axis=AX.X, op=ALU.max), reads=[rg2], writes=[rtop8])
            sc.op("dve", lambda e: e.tensor_tensor(out=eqA[:, 0:nq, :], in0=gmA[:, 0:nq, :], in1=mb, op=ALU.is_ge), reads=[rgm, rtop8], writes=[rsel])
            sc.op("dve", lambda e: e.tensor_tensor(out=eqA[:, 0:nq, :], in0=eqA[:, 0:nq, :], in1=callow[:, 0:nq, :], op=ALU.mult),
                  reads=[rsel, rGc], writes=[rsel])
            sc.op("dve", lambda e: e.tensor_tensor(out=eqA[:, 0:nq, :], in0=eqA[:, 0:nq, :], in1=cown[:, 0:nq, :], op=ALU.add),
                  reads=[rsel, rGc], writes=[rsel])
            sc.op("dve", lambda e: e.tensor_scalar(out=mpadA[:, 0:nq, 64:80], in0=eqA[:, 0:nq, :], scalar1=BIG, scalar2=-BIG, op0=ALU.mult, op1=ALU.add),
                  reads=[rsel, rmpad0], writes=[rmpad])
            for i4 in range(nqt):
                for s_ in range(4):
                    sc.op("pe", lambda e, s_=s_, i4=i4: e.matmul(psT[0:80, s_ * 128:(s_ + 1) * 128], mpadA[:, i4 * 4 + s_, :], A.ident[:, :], start=True, stop=True),
                          reads=[rmpad, A.rC], writes=[rpsT])
                sc.op("act", lambda e, hb=hb, i4=i4: e.copy(out=QA[hb][64:80, i4 * 512:(i4 + 1) * 512], in_=psT[64:80, :]),
                      reads=[rpsT], writes=[rQm[hb]])

        load_head(heads[0])
        for hi, h in enumerate(heads):
            hb = h % 2
            if hi + 1 < len(heads):
                load_head(heads[hi + 1])
            if hi == 0:
                gating(h)
            for qt in range(nqt):
                acc, racc = A.next_acc()
                kts = list(range(0, 4 * qt + 4))
                for idx, kt in enumerate(kts):
                    near = kt >= 4 * qt - 1
                    lastt = idx == len(kts) - 1
                    fin = None
                    if lastt:
                        def fin(acc=acc, racc=racc, hb=hb, qt=qt):
                            A.finish(acc, racc, None, None, OTs[hb][:, qt * 512:(qt + 1) * 512], rOTs[hb], "set", defer=min(10, 4 * qt + 8))
                    A.tile(KA[hb][0:80, kt * 128:(kt + 1) * 128], Res2(rK[hb], rKaug[hb]), QA[hb][0:80, qt * 512:(qt + 1) * 512], Res2(rQq[hb], rQm[hb]),
                           BT[hb][:, kt - (4 * qt - 1), :] if near else None, rBT[hb][kt - (4 * qt - 1)] if near else None, h, VA[hb][:, kt, :], Res2(rV[hb], rVone[hb]),
                           acc, racc, idx == 0, lastt, after=fin, cols=(max(0, 128 * (kt - 4 * qt)), 512))
                if qt == min(5, nqt - 1) and hi + 1 < len(heads):
                    gating(heads[hi + 1])
            A.flush()
            sc.dma("sp", OT_all[h * 64:(h + 1) * 64, :], OTs[hb][:, :], reads=[rOTs[hb]], writes=[r_ot])
        sc.emit()


def compress_phase(nc, tag, projT, pos_k, w1_k, w2_k, pos_v, w1_v, w2_v, kcmpT, vcmp):
    sc = Sched(nc, tag)
    with ExitStack() as es:
        def sb(name, shape, dt):
            return es.enter_context(nc.sbuf_tensor(f"{tag}_{name}", shape, dt))

        def pst(name):
            return es.enter_context(nc.psum_tensor(f"{tag}_{name}", [128, 512], F32))
        w1 = sb("w1", [64, 32, 256], BF16)
        w2 = sb("w2", [128, 2, 64], BF16)
        posT = sb("posT", [64, 32], BF16)
        cb = sb("cb", [128, 2], F32)
        raw = sb("raw", [64, S], BF16)
        hid = sb("hid", [128, 2, 256], BF16)
        kc = sb("kc", [64, 256], BF16)
        vc = sb("vc", [128, 2, 64], BF16)
        psH = [pst(f"psh{i}") for i in range(2)]
        psC = pst("psc")
        ps2 = pst("ps2")
        r_w1, r_w2, r_pos, r_cb, r_raw, r_hid, r_kc, r_vc = (Res(n) for n in ("w1", "w2", "pos", "cb", "raw", "hid", "kc", "vc"))
        r_psH = [Res("psh0"), Res("psh1")]
        r_psC, r_ps2 = Res("psc"), Res("ps2")
        r_proj, r_out = Res("projT"), Res("out")
        sc.op("dve", lambda e: e.memset(kc[:, :], 0.0), writes=[r_kc])
        sc.op("dve", lambda e: e.memset(vc[:, :, :], 0.0), writes=[r_vc])
        sc.op("dve", lambda e: e.memset(hid[:, :, :], 0.0), writes=[r_hid])
        for kv, (pos, w1d, w2d, chunk) in enumerate(((pos_k, w1_k, w2_k, 12), (pos_v, w1_v, w2_v, 13))):
            sc.dma("pool", w1[:, :, :], w1d.rearrange("(l d) j -> d l j", d=64), writes=[r_w1])
            sc.dma("pool", w2[:, :, :], w2d.rearrange("(c p) d -> p c d", p=128), writes=[r_w2])
            sc.dma("pool", posT[:, :], pos.rearrange("l d -> d l"), writes=[r_pos], allow_slow_non_contiguous=True)
            for jc in range(2):
                for l in range(32):
                    sc.op("pe", lambda e, l=l, jc=jc: e.matmul(psC[:, 0:1], w1[:, l, jc * 128:(jc + 1) * 128], posT[:, l:l + 1],
                                                             start=(l == 0), stop=(l == 31)), reads=[r_w1, r_pos], writes=[r_psC])
                sc.op("act", lambda e, jc=jc: e.copy(out=cb[:, jc:jc + 1], in_=psC[:, 0:1]), reads=[r_psC], writes=[r_cb])
            for g in range(2):
                row = chunk * 128 + g * 64
                sc.dma("sp", raw[:, :], projT[row:row + 64, :], reads=[r_proj], writes=[r_raw])
                rv = raw[:, :].rearrange("p (n s) -> p n s", s=16)
                for jc in range(2):
                    for l in range(32):
                        rhs = rv[:, 0:255, l] if l < 16 else rv[:, 1:256, l - 16]
                        sc.op("pe", lambda e, l=l, jc=jc, rhs=rhs: e.matmul(psH[jc][:, 0:255], w1[:, l, jc * 128:(jc + 1) * 128], rhs,
                                                                          start=(l == 0), stop=(l == 31)), reads=[r_w1, r_raw], writes=[r_psH[jc]])
                    sc.op("act", lambda e, jc=jc: e.activation(out=hid[:, jc, 0:255], in_=psH[jc][:, 0:255], func=AF.Silu, bias=cb[:, jc:jc + 1]),
                          reads=[r_psH[jc], r_cb], writes=[r_hid])
                if kv == 0:
                    for jc in range(2):
                        sc.op("pe", lambda e, jc=jc: e.matmul(ps2[0:64, 0:255], w2[:, jc, :], hid[:, jc, 0:255], start=(jc == 0), stop=(jc == 1)),
                              reads=[r_w2, r_hid], writes=[r_ps2])
                    sc.op("dve", lambda e: e.tensor_copy(out=kc[:, 0:255], in_=ps2[0:64, 0:255]), reads=[r_ps2], writes=[r_kc])
                    sc.dma("sp", kcmpT[g * 64:(g + 1) * 64, :], kc[:, :], reads=[r_kc], writes=[r_out])
                else:
                    for c in range(2):
                        for jc in range(2):
                            sc.op("pe", lambda e, jc=jc, c=c: e.matmul(ps2[:, c * 64:(c + 1) * 64], hid[:, jc, c * 128:(c + 1) * 128], w2[:, jc, :],
                                                                     start=(jc == 0), stop=(jc == 1)), reads=[r_w2, r_hid], writes=[r_ps2])
                    sc.op("dve", lambda e: e.tensor_copy(out=vc[:, :, :], in_=ps2[:, 0:128].rearrange("p (c d) -> p c d", d=64)),
                          reads=[r_ps2], writes=[r_vc])
                    sc.dma("sp", vcmp[g * 256:(g + 1) * 256, :].rearrange("(c p) d -> p c d", p=128), vc[:, :, :], reads=[r_vc], writes=[r_out])
        sc.emit()


def nsa_phase(nc, tag, projT, Vtok, gT, kcmpT, vcmp, Rc, Rw, rel_bias, cst, OT_all, groups=range(2), nqt=8, parts=('cmp', 'sel', 'sw'), selv=2):
    sc = Sched(nc, tag)
    with ExitStack() as es:
        A = AttnCtx(nc, sc, es, tag)
        sb, pst = A.sb, A.pst
        A.load_consts(cst, rel_bias)
        QN = [sb(f"qn{i}", [128, S], BF16) for i in range(4)]
        KS = sb("ks", [128, S], BF16)
        KW = sb("kw", [128, S], BF16)
        VS = sb("vs", [128, 32, 65], BF16)
        VW = sb("vw", [128, 32, 65], BF16)
        KC = sb("kc", [128, 256], BF16)
        VC = sb("vc", [128, 2, 65], BF16)
        OTc = [sb(f"otc{i}", [64, S], BF16) for i in range(4)]
        OTs = [sb(f"ots{i}", [64, S], BF16) for i in range(2)]
        OTh = sb("oth", [64, 512], F32)
        BT = [sb(f"bt{i}", [128, 13, 512], BF16) for i in range(2)]
        CB = [sb(f"cbt{i}", [128, 512], BF16) for i in range(2)]
        G = [sb(f"g{i}", [128, 3, 512], F32) for i in range(2)]
        imp = sb("imp", [128, 32, 64], F32)
        cslc = sb("cslc", [128, 32, 64], F32)
        ovl = sb("ovl", [128, 2, 65], BF16)
        rdi = sb("rdi", [128, 4], F32)
        scr = sb("scr", [128, 4, 64], F32)
        wk = sb("wk", [128, 4, 64], F32)
        t8a = sb("t8a", [128, 4, 8], F32)
        t8b = sb("t8b", [128, 4, 8], F32)
        mpad = sb("mpad", [128, 4, 128], BF16)
        psI = pst("psi")
        psT = pst("pst")
        rQq = [Res(f"qq{i}") for i in range(4)]
        rQm = [Res(f"qm{i}") for i in range(4)]
        rKS, rKSaug, rKW, rVS, rVW, rVone, rKC, rVC = (Res(n) for n in ("ks", "ksaug", "kw", "vs", "vw", "vone", "kc", "vc"))
        rOTc = [Res(f"otc{i}") for i in range(4)]
        rOTs = [Res("ots0"), Res("ots1")]
        rOTh = Res("oth")
        rBT = [[Res(f"bt{b}_{i}") for i in range(13)] for b in range(2)]
        rCB = [Res("cb0"), Res("cb1")]
        rG = [Res("g0"), Res("g1")]
        rimp, rC2, rrdi, rscr, rwk, rt8a, rt8b, rmpad, rmpad0 = (Res(n) for n in ("imp", "c2", "rdi", "scr", "wk", "t8a", "t8b", "mpad", "mpad0"))
        rpsI, rpsT = Res("psi"), Res("pst")
        r_proj, r_vt, r_g, r_kc, r_vc, r_R, r_ot = (Res(n) for n in ("projT", "Vtok", "gT", "kcmp", "vcmp", "R", "OT"))
        sc.dma("sp", cslc[:, :, :], cst["c_slc"][:, :, :], writes=[rC2])
        sc.dma("pool", ovl[:, :, :], cst["c_ovl"].rearrange("(c p) j -> p c j", p=128), writes=[rC2])
        sc.dma("pool", KS[64:128, :], cst["c_kaugs"][:, :], writes=[rKSaug])
        sc.op("dve", lambda e: e.memset(mpad[:, :, :], 0.0), writes=[rmpad0])
        sc.op("dve", lambda e: e.memset(imp[:, :, :], 0.0), writes=[rimp])
        rKz = Res("kzero")
        sc.op("pool", lambda e: e.memset(KW[64:128, :], 0.0), writes=[rKz])
        sc.op("pool", lambda e: e.memset(KC[64:128, :], 0.0), writes=[rKz])
        for r in range(4):
            sc.op("pool", lambda e, r=r: e.memset(QN[r][64:128, :], 0.0), writes=[rQm[r]])
        sc.op("pool", lambda e: e.memset(VS[:, :, 64:65], 1.0), writes=[rVone])
        sc.op("pool", lambda e: e.memset(VW[:, :, 64:65], 1.0), writes=[rVone])
        sc.op("pool", lambda e: e.memset(VC[:, :, 64:65], 1.0), writes=[rVone])
        vtv = Vtok.rearrange("(t p) c -> p t c", p=128)
        gi = [0]

        def load_gates(hb, qt):
            j = gi[0] % 2
            gi[0] += 1
            sc.dma("sp", G[j][64:65, :, :], dram_view(gT.tensor, (hb * 3) * S + qt * 512, [[0, 1], [S, 3], [1, 512]]),
                   reads=[r_g], writes=[rG[j]])
            return G[j], rG[j]

        def load_bt(hb):
            hg, b2 = 8 + hb, hb % 2
            for i in range(5):
                sc.dma("sp", BT[b2][:, i, :], rvec(Rc, hg, ROFF - 128 * (i - 1) - 127, 1), reads=[r_R], writes=[rBT[b2][i]])
            for i in range(8):
                sc.dma("sp", BT[b2][:, 5 + i, :], rvec(Rw, hg, ROFF - 128 * (i - 4) - 127, 1), reads=[r_R], writes=[rBT[b2][5 + i]])

        for g in groups:
            sc.dma("sp", KS[0:64, :], projT[14 * 128 + g * 64:14 * 128 + g * 64 + 64, :], reads=[r_proj], writes=[rKS])
            sc.dma("sp", KW[0:64, :], projT[15 * 128 + g * 64:15 * 128 + g * 64 + 64, :], reads=[r_proj], writes=[rKW])
            sc.dma("sp", VS[:, :, 0:64], vtv[:, :, 512 + g * 64:512 + (g + 1) * 64], reads=[r_vt], writes=[rVS], allow_slow_non_contiguous=True)
            sc.dma("sp", VW[:, :, 0:64], vtv[:, :, 640 + g * 64:640 + (g + 1) * 64], reads=[r_vt], writes=[rVW], allow_slow_non_contiguous=True)
            sc.dma("sp", KC[0:64, :], kcmpT[g * 64:(g + 1) * 64, :], reads=[r_kc], writes=[rKC])
            sc.dma("sp", VC[:, :, 0:64], vcmp[g * 256:(g + 1) * 256, :].rearrange("(c p) d -> p c d", p=128), reads=[r_vc], writes=[rVC],
                   allow_slow_non_contiguous=True)
            for r in range(4):
                hb = 4 * g + r
                qrow = (8 + hb // 2) * 128 + (hb % 2) * 64
                sc.dma("sp", QN[r][0:64, :], projT[qrow:qrow + 64, :], reads=[r_proj], writes=[rQq[r]])
            if 'sw' in parts:
                load_bt(4 * g)
            cbi = 0
            for r in (range(4) if 'cmp' in parts else []):
                hb = 4 * g + r
                hg = 8 + hb
                for qt in range(nqt):
                    Gt, rGt = load_gates(hb, qt)
                    acc, racc = A.next_acc()
                    chunks = [0] if qt <= 3 else [0, 1]
                    for ci, c in enumerate(chunks):
                        needb = (c == 1) or (qt <= 4)
                        bt, rbt = None, None
                        if needb:
                            bt, rbt = CB[cbi % 2], rCB[cbi % 2]
                            cbi += 1
                            sc.dma("sp", bt[:, :], rvec(Rc, hg, ROFF + 512 * qt - 2048 * c - 2063, 16), reads=[r_R], writes=[rbt])
                        first, last = ci == 0, ci == len(chunks) - 1

                        def extra(pt, rpt, c=c, first=first, last=last):
                            for s in range(4):
                                sc.op("pe", lambda e, s=s: e.matmul(psI[:, s * 65:(s + 1) * 65], pt[:, s * 128:(s + 1) * 128], ovl[:, c, :],
                                                                    start=(first and s == 0), stop=last, skip_group_check=True),
                                      reads=[rpt, rC2], writes=[rpsI])
                        A.tile(KC[0:128, c * 128:(c + 1) * 128], Res2(rKC, rKz), QN[r][0:128, qt * 512:(qt + 1) * 512], Res2(rQq[r], rQm[r]),
                               bt[:, :] if needb else None, rbt, hg, VC[:, c, :], Res2(rVC, rVone), acc, racc, first, last, extra=extra)
                    A.flush()
                    A.finish(acc, racc, Gt[64:65, 0, :], rGt, OTc[r][:, qt * 512:(qt + 1) * 512], rOTc[r], "set", split=True, defer=10 ** 6, copy_eng="act")
                    pv = psI[:, 0:260].rearrange("p (s j) -> p s j", j=65)
                    sc.op("dve", lambda e, pv=pv: e.tensor_scalar(out=rdi[:, :], in0=pv[:, :, 64], scalar1=1e-30, scalar2=None, op0=ALU.max),
                          reads=[rpsI], writes=[rrdi])
                    sc.op("dve", lambda e: e.reciprocal(out=rdi[:, :], in_=rdi[:, :]), reads=[rrdi], writes=[rrdi])
                    for s in range(4):
                        i = qt * 4 + s
                        if r == 0:
                            sc.op("dve", lambda e, s=s, i=i: e.tensor_scalar(out=imp[:, i, :], in0=psI[:, s * 65:s * 65 + 64], scalar1=rdi[:, s:s + 1],
                                                                          scalar2=None, op0=ALU.mult), reads=[rpsI, rrdi], writes=[rimp])
                        else:
                            sc.op("dve", lambda e, s=s, i=i: e.scalar_tensor_tensor(out=imp[:, i, :], in0=psI[:, s * 65:s * 65 + 64], scalar=rdi[:, s:s + 1],
                                                                                 in1=imp[:, i, :], op0=ALU.mult, op1=ALU.add),
                                  reads=[rpsI, rrdi, rimp], writes=[rimp])
            A.flush()
            for i4 in (range(nqt) if 'sel' in parts else []):
                sc.op("dve", lambda e, i4=i4: e.tensor_tensor(out=scr[:, :, :], in0=imp[:, i4 * 4:(i4 + 1) * 4, :], in1=cslc[:, i4 * 4:(i4 + 1) * 4, :],
                                                              op=ALU.add), reads=[rimp, rC2], writes=[rscr])
                for s in range(4):
                    sc.op("dve", lambda e, s=s: e.max(out=t8a[:, s, :], in_=scr[:, s, :]), reads=[rscr], writes=[rt8a])
                    sc.op("dve", lambda e, s=s: e.tensor_scalar(out=wk[:, s, :], in0=scr[:, s, :], scalar1=t8a[:, s, 7:8], scalar2=-6e4,
                                                                op0=ALU.is_ge, op1=ALU.mult), reads=[rscr, rt8a], writes=[rwk])
                    sc.op("dve", lambda e, s=s: e.tensor_tensor(out=wk[:, s, :], in0=wk[:, s, :], in1=scr[:, s, :], op=ALU.add),
                          reads=[rscr, rwk], writes=[rwk])
                    sc.op("dve", lambda e, s=s: e.max(out=t8b[:, s, :], in_=wk[:, s, :]), reads=[rwk], writes=[rt8b])
                    sc.op("dve", lambda e, s=s: e.tensor_scalar(out=wk[:, s, :], in0=scr[:, s, :], scalar1=t8b[:, s, 7:8], scalar2=BIG,
                                                                op0=ALU.is_ge, op1=ALU.mult), reads=[rscr, rt8b, rwk], writes=[rwk])
                sc.op("dve", lambda e: e.tensor_scalar(out=mpad[:, :, 64:128], in0=wk[:, :, :], scalar1=-BIG, scalar2=None, op0=ALU.add),
                      reads=[rwk, rmpad0], writes=[rmpad])
                for s in (range(4) if selv >= 1 else []):
                    sc.op("pe", lambda e, s=s: e.matmul(psT[:, s * 128:(s + 1) * 128], mpad[:, s, :], A.ident[:, :], start=True, stop=True),
                          reads=[rmpad, A.rC], writes=[rpsT])
                for r in (range(4) if selv >= 2 else []):
                    sc.op("act", lambda e, r=r, i4=i4: e.copy(out=QN[r][64:128, i4 * 512:(i4 + 1) * 512], in_=psT[64:128, :]),
                          reads=[rpsT], writes=[rQm[r]])
            for r in (range(4) if 'sw' in parts else []):
                hb = 4 * g + r
                hg = 8 + hb
                b2 = hb % 2
                if r + 1 < 4:
                    load_bt(4 * g + r + 1)
                for qt in range(nqt):
                    Gt, rGt = load_gates(hb, qt)
                    acc, racc = A.next_acc()
                    kts = list(range(0, 4 * qt + 4))
                    for idx, kt in enumerate(kts):
                        near = kt >= 4 * qt - 1
                        lastt = idx == len(kts) - 1
                        fin = None
                        if lastt:
                            def fin(acc=acc, racc=racc, Gt=Gt, rGt=rGt, r=r, qt=qt):
                                A.finish(acc, racc, Gt[64:65, 1, :], rGt, OTh[:, :], rOTh, "add", add_ap=OTc[r][:, qt * 512:(qt + 1) * 512], rAdd=rOTc[r],
                                         defer=(4 if qt == 0 else 8))
                        A.tile(KS[0:128, kt * 128:(kt + 1) * 128], Res2(rKS, rKSaug), QN[r][0:128, qt * 512:(qt + 1) * 512], Res2(rQq[r], rQm[r]),
                               BT[b2][:, kt - (4 * qt - 1), :] if near else None, rBT[b2][kt - (4 * qt - 1)] if near else None, hg, VS[:, kt, :], Res2(rVS, rVone),
                               acc, racc, idx == 0, lastt, after=fin, cols=(max(0, 128 * (kt - 4 * qt)), 512))
                    acc, racc = A.next_acc()
                    kts = list(range(max(0, 4 * qt - 4), 4 * qt + 4))
                    if qt > 0:
                        kts = [4 * qt - 1] + [k_ for k_ in kts if k_ != 4 * qt - 1]
                    for idx, kt in enumerate(kts):
                        lastt = idx == len(kts) - 1
                        rel = kt - 4 * qt
                        wcols = (128 * rel, 512) if rel >= 0 else (0, min(512, 128 * (rel + 5)))
                        fin = None
                        if lastt:
                            def fin(acc=acc, racc=racc, Gt=Gt, rGt=rGt, b2=b2, qt=qt):
                                A.finish(acc, racc, Gt[64:65, 2, :], rGt, OTs[b2][:, qt * 512:(qt + 1) * 512], rOTs[b2], "add", add_ap=OTh[:, :], rAdd=rOTh,
                                         defer=min(10, 4 * qt + 8))
                        A.tile(KW[0:128, kt * 128:(kt + 1) * 128], Res2(rKW, rKz), QN[r][0:128, qt * 512:(qt + 1) * 512], Res2(rQq[r], rQm[r]),
                               BT[b2][:, 5 + kt - (4 * qt - 4), :], rBT[b2][5 + kt - (4 * qt - 4)], hg, VW[:, kt, :], Res2(rVW, rVone),
                               acc, racc, idx == 0, lastt, after=fin, cols=wcols)
                A.flush()
                sc.dma("sp", OT_all[hg * 64:(hg + 1) * 64, :], OTs[b2][:, :], reads=[rOTs[b2]], writes=[r_ot])
        sc.emit()


def outproj_phase(nc, tag, OT_all, x1T, w_out, x2T, ntok=S):
    sc = Sched(nc, tag)
    TN = 512
    with ExitStack() as es:
        def sb(name, shape, dt):
            return es.enter_context(nc.sbuf_tensor(f"{tag}_{name}", shape, dt))
        wo = sb("wo", [128, 8, D], BF16)
        ot = [sb(f"ot{i}", [128, 8, TN], BF16) for i in range(2)]
        xt = [sb(f"xt{i}", [128, 8, TN], F32) for i in range(2)]
        ps = [es.enter_context(nc.psum_tensor(f"{tag}_ps{i}", [128, 512], F32)) for i in range(2)]
        r_w = Res("w")
        r_ot = [Res("ot0"), Res("ot1")]
        r_xt = [Res("xt0"), Res("xt1")]
        r_ps = [Res("ps0"), Res("ps1")]
        r_in, r_x1, r_x2 = Res("OT"), Res("x1"), Res("x2")
        for k in range(8):
            sc.dma("pool", wo[:, k, :], w_out[k * 128:(k + 1) * 128, :], writes=[r_w])
        ov = OT_all.rearrange("(c p) t -> p c t", p=128)
        x1v = x1T.rearrange("(c p) t -> p c t", p=128)
        x2v = x2T.rearrange("(c p) t -> p c t", p=128)
        for it in range(ntok // TN):
            t0 = it * TN
            b = it % 2
            sc.dma("sp", ot[b][:, :, :], ov[:, :, t0:t0 + TN], reads=[r_in], writes=[r_ot[b]])
            sc.dma("sp", xt[b][:, :, :], x1v[:, :, t0:t0 + TN], reads=[r_x1], writes=[r_xt[b]])
            for d in range(8):
                j = d % 2
                for o in range(8):
                    sc.op("pe", lambda e, o=o, d=d, j=j, b=b: e.matmul(ps[j][:, :], wo[:, o, d * 128:(d + 1) * 128], ot[b][:, o, :],
                                                                      start=(o == 0), stop=(o == 7)), reads=[r_w, r_ot[b]], writes=[r_ps[j]])
                sc.op("dve", lambda e, d=d, j=j, b=b: e.tensor_tensor(out=xt[b][:, d, :], in0=ps[j][:, :], in1=xt[b][:, d, :], op=ALU.add),
                      reads=[r_ps[j], r_xt[b]], writes=[r_xt[b]])
            sc.dma("sp", x2v[:, :, t0:t0 + TN], xt[b][:, :, :], reads=[r_xt[b]], writes=[r_x2])
        sc.emit()


W_NAMES = ["norm_ffn1", "w_ffn1_gate", "w_ffn1_up", "w_ffn1_down", "norm_mix", "w_in", "cmp_pos_k", "cmp_w1_k", "cmp_w2_k",
           "cmp_pos_v", "cmp_w1_v", "cmp_w2_v", "w_out", "norm_ffn2", "w_ffn2_gate", "w_ffn2_up", "w_ffn2_down"]
W_SHAPES = {"norm_ffn1": [D], "w_ffn1_gate": [D, DFF], "w_ffn1_up": [D, DFF], "w_ffn1_down": [DFF, D], "norm_mix": [D], "w_in": [D, DIN],
            "cmp_pos_k": [32, 64], "cmp_w1_k": [2048, 256], "cmp_w2_k": [256, 64], "cmp_pos_v": [32, 64], "cmp_w1_v": [2048, 256],
            "cmp_w2_v": [256, 64], "w_out": [D, D], "norm_ffn2": [D], "w_ffn2_gate": [D, DFF], "w_ffn2_up": [D, DFF], "w_ffn2_down": [DFF, D]}


def build_program(stages=None, debug_out=(), nsa_kw={}):
    nc = bass.Bass("TRN2", target_bir_lowering=False)
    cstn = static_consts()

    def din(name, shape, dt=F32):
        return nc.dram_tensor(name, list(shape), dt, kind="ExternalInput").ap()

    xT = din("xT", [D, S])
    w = {n: din(n, W_SHAPES[n]) for n in W_NAMES}
    rel_bias = din("rel_bias", [32, 16])
    norm_final = din("norm_final", [D])
    cst = {n: din(n, a.shape) for n, a in cstn.items()}
    outT = nc.dram_tensor("outT", [D, S], F32, kind="ExternalOutput").ap()

    def scratch(name, shape, dt):
        kind = "ExternalOutput" if name in debug_out else "Internal"
        return nc.dram_tensor(name, list(shape), dt, kind=kind).ap()
    x1T = scratch("x1T", [D, S], F32)
    h2T = scratch("h2T", [D, S], BF16)
    projT = scratch("projT", [16 * 128, S], BF16)
    Vtok = scratch("Vtok", [S, 768], BF16)
    gT = scratch("gT", [24, S], F32)
    Rc = scratch("Rc", [16, RLEN], BF16)
    Rw = scratch("Rw", [16, RLEN], BF16)
    kcmpT = scratch("kcmpT", [128, 256], BF16)
    vcmp = scratch("vcmp", [512, 64], BF16)
    OT_all = scratch("OT_all", [D, S], BF16)
    x2T = scratch("x2T", [D, S], F32)
    st = stages or ["const", "ffn1", "inproj", "moba", "compress", "nsa", "outproj", "ffn2"]
    if "const" in st:
        const_phase(nc, "c0", rel_bias, cst, Rc, Rw)
    if "ffn1" in st:
        ffn_phase(nc, "f1", xT, x1T, w["norm_ffn1"], w["w_ffn1_gate"], w["w_ffn1_up"], w["w_ffn1_down"], w["norm_mix"], h2T, BF16)
    if "inproj" in st:
        inproj_phase(nc, "ip", h2T, w["w_in"], projT, Vtok, gT)
    if "moba" in st:
        moba_phase(nc, "mo", projT, Vtok, Rc, rel_bias, cst, OT_all)
    if "compress" in st:
        compress_phase(nc, "cp", projT, w["cmp_pos_k"], w["cmp_w1_k"], w["cmp_w2_k"], w["cmp_pos_v"], w["cmp_w1_v"], w["cmp_w2_v"], kcmpT, vcmp)
    if "nsa" in st:
        nsa_phase(nc, "ns", projT, Vtok, gT, kcmpT, vcmp, Rc, Rw, rel_bias, cst, OT_all, **nsa_kw)
    if "outproj" in st:
        outproj_phase(nc, "op", OT_all, x1T, w["w_out"], x2T)
    if "ffn2" in st:
        ffn_phase(nc, "f2", x2T, None, w["norm_ffn2"], w["w_ffn2_gate"], w["w_ffn2_up"], w["w_ffn2_down"], norm_final, outT, F32)
    return nc


def make_in_map(inputs, b):
    m = {"xT": np.ascontiguousarray(inputs["x"][b].T)}
    for n in W_NAMES:
        m[n] = np.ascontiguousarray(np.asarray(inputs[n], dtype=np.float32)[0])
    m["rel_bias"] = np.ascontiguousarray(np.asarray(inputs["rel_bias"], dtype=np.float32))
    m["norm_final"] = np.ascontiguousarray(np.asarray(inputs["norm_final"], dtype=np.float32))
    m.update(static_consts())
    return m


def kernel(**inputs):
    inputs = {k: np.asarray(v) for k, v in inputs.items()}
    nb = inputs["x"].shape[0]
    nc = build_program()
    in_maps = [make_in_map(inputs, b) for b in range(nb)]
    res = run_bass_kernel_spmd(nc, in_maps, core_ids=list(range(nb)))
    out = np.stack([np.ascontiguousarray(np.asarray(r["outT"]).T) for r in res.results], axis=0)
    return out.astype(np.float32)
```
